# Optimizing a Trainium2 kernel written in Bass

```python
import jax, jax.numpy as jnp
from jax import lax
import numpy as np

D_MODEL = 2048
BATCH = 1
SEQ = 8192
DEPTH = 4

HEAD_DIM = 128
ROPE_THETA = 10000.0
Q_CHUNK = 128
LN_EPS = 1e-5
RMS_EPS = 1e-6

MOBA_HEADS = 4
MOBA_BLOCK = 256
MOBA_TOPK = 3

MLA_HEADS = 4
MLA_Q_RANK = 512
MLA_KV_RANK = 512
MLA_NOPE = 128
MLA_ROPE = 64
MLA_V = 128

NSA_HEADS = 4
NSA_CMP_STRIDE = 16
NSA_CMP_LEN = 2 * NSA_CMP_STRIDE
NSA_SEL_BLOCK = 64
NSA_SEL_TOPK = 16
NSA_WINDOW = 512
NSA_BRANCH_KV = 6

DSA_HEADS = 4
DSA_IDX_HEADS = 16
DSA_IDX_DIM = 64
DSA_TOPK = 256

MEM_LEN = 256
MEM_HEADS = 4

D_FF = 5632
D_MIX = (MOBA_HEADS + MLA_HEADS + NSA_HEADS + DSA_HEADS) * HEAD_DIM
DN_ALPHA = (2 * DEPTH) ** 0.25
DN_BETA = (8 * DEPTH) ** -0.25

IN_SIZES = (
    MOBA_HEADS * HEAD_DIM, MOBA_HEADS * HEAD_DIM, MOBA_HEADS * HEAD_DIM,
    MLA_Q_RANK, MLA_KV_RANK, MLA_ROPE,
    NSA_HEADS * HEAD_DIM, NSA_BRANCH_KV * HEAD_DIM, 3 * NSA_HEADS,
    DSA_HEADS * HEAD_DIM, DSA_HEADS * HEAD_DIM, DSA_HEADS * HEAD_DIM,
    DSA_IDX_HEADS * DSA_IDX_DIM, DSA_IDX_DIM, DSA_IDX_HEADS,
)
D_IN = sum(IN_SIZES)

kernel_name = 'hybrid_parallel_sparse_attention_trunk'


def layer_norm(x, g, b):
    xf = x.astype(jnp.float32)
    mu = jnp.mean(xf, -1, keepdims=True)
    var = jnp.mean(jnp.square(xf - mu), -1, keepdims=True)
    return ((xf - mu) * lax.rsqrt(var + LN_EPS) * g + b).astype(x.dtype)


def rms_norm(x, g):
    xf = x.astype(jnp.float32)
    return (xf * lax.rsqrt(jnp.mean(xf * xf, -1, keepdims=True) + RMS_EPS) * g).astype(x.dtype)


def rope(x, pos):
    d = x.shape[-1]
    inv = ROPE_THETA ** (-jnp.arange(0, d, 2, dtype=jnp.float32) / d)
    ang = pos.astype(jnp.float32)[..., None] * inv
    cos, sin = jnp.cos(ang)[:, :, None, :], jnp.sin(ang)[:, :, None, :]
    x1, x2 = jnp.split(x.astype(jnp.float32), 2, axis=-1)
    return jnp.concatenate([x1 * cos - x2 * sin, x1 * sin + x2 * cos], -1).astype(x.dtype)


def masked_softmax(logits, mask):
    lf = jnp.where(mask, logits.astype(jnp.float32), -jnp.inf)
    m = jnp.max(lf, -1, keepdims=True)
    m = jnp.where(jnp.isfinite(m), m, 0.0)
    e = jnp.where(mask, jnp.exp(lf - m), 0.0)
    s = jnp.sum(e, -1, keepdims=True)
    return e / jnp.where(s > 0, s, 1.0)


def to_chunks(t):
    b, s = t.shape[:2]
    return jnp.moveaxis(t.reshape((b, s // Q_CHUNK, Q_CHUNK) + t.shape[2:]), 1, 0)


def from_chunks(t):
    t = jnp.moveaxis(t, 0, 1)
    return t.reshape((t.shape[0], -1) + t.shape[3:])


def swiglu(x, w_gu, w_down):
    g, u = jnp.split(x @ w_gu, 2, axis=-1)
    return (jax.nn.silu(g) * u) @ w_down


def moba_attention(q, k, v, pos):
    B, S, H, D = q.shape
    q, k = rope(q, pos), rope(k, pos)
    nb = -(-S // MOBA_BLOCK)
    pad = ((0, 0), (0, nb * MOBA_BLOCK - S), (0, 0), (0, 0))
    kb = jnp.pad(k, pad).reshape(B, nb, MOBA_BLOCK, H, D).transpose(0, 3, 1, 2, 4)
    vb = jnp.pad(v, pad).reshape(B, nb, MOBA_BLOCK, H, D).transpose(0, 3, 1, 2, 4)
    k_mean = jnp.mean(kb.astype(jnp.float32), axis=3)
    n_sel = min(MOBA_TOPK, nb - 1)
    scale = D ** -0.5
    b_ix = jnp.arange(B)[:, None, None, None]
    h_ix = jnp.arange(H)[None, :, None, None]

    def block(args):
        qc, c = args
        qc = qc.transpose(0, 2, 1, 3)
        qpos = c * Q_CHUNK + jnp.arange(Q_CHUNK)
        cur = qpos[0] // MOBA_BLOCK
        k_own = lax.dynamic_index_in_dim(kb, cur, axis=2, keepdims=False)
        v_own = lax.dynamic_index_in_dim(vb, cur, axis=2, keepdims=False)
        own_pos = cur * MOBA_BLOCK + jnp.arange(MOBA_BLOCK)
        logits = [jnp.einsum('bhqd,bhkd->bhqk', qc, k_own)]
        masks = [jnp.broadcast_to(own_pos[None, :] <= qpos[:, None], (B, H, Q_CHUNK, MOBA_BLOCK))]
        if n_sel > 0:
            gate = jnp.einsum('bhqd,bhnd->bhqn', qc.astype(jnp.float32), k_mean)
            gate = jnp.where(jnp.arange(nb) < cur, gate, -jnp.inf)
            _, idx = lax.top_k(gate, n_sel)
            k_sel = kb[b_ix, h_ix, idx]
            v_sel = vb[b_ix, h_ix, idx]
            logits.append(jnp.einsum('bhqd,bhqnkd->bhqnk', qc, k_sel).reshape(B, H, Q_CHUNK, n_sel * MOBA_BLOCK))
            sel_ok = jnp.broadcast_to((idx < cur)[..., None], idx.shape + (MOBA_BLOCK,))
            masks.append(sel_ok.reshape(B, H, Q_CHUNK, n_sel * MOBA_BLOCK))
        p = masked_softmax(jnp.concatenate(logits, -1) * scale, jnp.concatenate(masks, -1)).astype(v.dtype)
        out = jnp.einsum('bhqk,bhkd->bqhd', p[..., :MOBA_BLOCK], v_own)
        if n_sel > 0:
            p_sel = p[..., MOBA_BLOCK:].reshape(B, H, Q_CHUNK, n_sel, MOBA_BLOCK)
            out = out + jnp.einsum('bhqnk,bhqnkd->bqhd', p_sel, v_sel)
        return out

    out = lax.map(block, (to_chunks(q), jnp.arange(S // Q_CHUNK)))
    return from_chunks(out).reshape(B, S, H * D)


def mla_attention(c_q, c_kv, k_rope, g_cq, g_ckv, w_uq, w_ukv, pos):
    B, S, _ = c_q.shape
    q = (rms_norm(c_q, g_cq) @ w_uq).reshape(B, S, MLA_HEADS, MLA_NOPE + MLA_ROPE)
    q_nope, q_rope = q[..., :MLA_NOPE], rope(q[..., MLA_NOPE:], pos)
    kv = (rms_norm(c_kv, g_ckv) @ w_ukv).reshape(B, S, MLA_HEADS, MLA_NOPE + MLA_V)
    k_nope, v = kv[..., :MLA_NOPE], kv[..., MLA_NOPE:]
    k_r = rope(k_rope[:, :, None, :], pos)[:, :, 0]
    scale = (MLA_NOPE + MLA_ROPE) ** -0.5
    kpos = jnp.arange(S)

    def block(args):
        qn, qr, c = args
        qpos = c * Q_CHUNK + jnp.arange(Q_CHUNK)
        logits = (jnp.einsum('bqhd,bkhd->bhqk', qn, k_nope) + jnp.einsum('bqhd,bkd->bhqk', qr, k_r)) * scale
        p = masked_softmax(logits, kpos[None, :] <= qpos[:, None]).astype(v.dtype)
        return jnp.einsum('bhqk,bkhd->bqhd', p, v)

    out = lax.map(block, (to_chunks(q_nope), to_chunks(q_rope), jnp.arange(S // Q_CHUNK)))
    return from_chunks(out).reshape(B, S, MLA_HEADS * MLA_V)


def nsa_attention(q, kv, gate_logits, cmp_pe, cmp_w1, cmp_w2, pos):
    B, S, H, D = q.shape
    k_cmp, v_cmp, k_slc, v_slc, k_win, v_win = (kv[:, :, i] for i in range(NSA_BRANCH_KV))
    scale = D ** -0.5
    t_pos = jnp.arange(S)

    n_cmp = S // NSA_CMP_STRIDE - 1

    def compress(t, i):
        tw = t.reshape(B, S // NSA_CMP_STRIDE, NSA_CMP_STRIDE, D)
        blocks = jnp.concatenate([tw[:, :-1], tw[:, 1:]], axis=2) + cmp_pe[i]
        return jax.nn.gelu(blocks.reshape(B, n_cmp, NSA_CMP_LEN * D) @ cmp_w1[i]) @ cmp_w2[i]

    kc, vc = compress(k_cmp, 0), compress(v_cmp, 1)
    cmp_end = jnp.arange(n_cmp) * NSA_CMP_STRIDE + NSA_CMP_LEN - 1
    p_cmp = masked_softmax(jnp.einsum('bshd,bnd->bhsn', q, kc) * scale, cmp_end[None, :] <= t_pos[:, None])
    o_cmp = jnp.einsum('bhsn,bnd->bshd', p_cmp.astype(vc.dtype), vc)

    ratio = NSA_SEL_BLOCK // NSA_CMP_STRIDE
    lead = NSA_CMP_LEN // NSA_CMP_STRIDE - 1
    n_blk = S // NSA_SEL_BLOCK
    pp = jnp.pad(jnp.sum(p_cmp, axis=1), ((0, 0), (0, 0), (lead, ratio * n_blk - n_cmp)))
    imp = sum(pp[..., r:r + ratio * n_blk:ratio] for r in range(ratio + lead))
    blk = jnp.arange(n_blk)
    cur = (t_pos // NSA_SEL_BLOCK)[:, None]
    forced = (blk == 0) | (blk == cur) | (blk == cur - 1)
    imp = jnp.where(blk > cur, -jnp.inf, jnp.where(forced, jnp.inf, imp))
    k_top = min(NSA_SEL_TOPK, n_blk)
    _, sel_idx = lax.top_k(imp, k_top)
    sel_valid = sel_idx <= cur

    q_r = rope(q, pos)
    k_slc = rope(k_slc[:, :, None], pos)[:, :, 0]
    k_win = rope(k_win[:, :, None], pos)[:, :, 0]
    ksb = k_slc.reshape(B, n_blk, NSA_SEL_BLOCK, D)
    vsb = v_slc.reshape(B, n_blk, NSA_SEL_BLOCK, D)
    kwp = jnp.pad(k_win, ((0, 0), (NSA_WINDOW, 0), (0, 0)))
    vwp = jnp.pad(v_win, ((0, 0), (NSA_WINDOW, 0), (0, 0)))
    b_ix = jnp.arange(B)[:, None, None]

    def block(args):
        qc, idx, valid, c = args
        qpos = c * Q_CHUNK + jnp.arange(Q_CHUNK)
        k_sel = ksb[b_ix, idx].reshape(B, Q_CHUNK, k_top * NSA_SEL_BLOCK, D)
        v_sel = vsb[b_ix, idx].reshape(B, Q_CHUNK, k_top * NSA_SEL_BLOCK, D)
        kpos = (idx[..., None] * NSA_SEL_BLOCK + jnp.arange(NSA_SEL_BLOCK)).reshape(B, Q_CHUNK, -1)
        m_sel = (kpos <= qpos[None, :, None]) & jnp.repeat(valid, NSA_SEL_BLOCK, axis=-1)
        p = masked_softmax(jnp.einsum('bqhd,bqkd->bhqk', qc, k_sel) * scale, m_sel[:, None])
        o_slc = jnp.einsum('bhqk,bqkd->bqhd', p.astype(v_sel.dtype), v_sel)
        start = c * Q_CHUNK
        k_w = lax.dynamic_slice_in_dim(kwp, start, NSA_WINDOW + Q_CHUNK, axis=1)
        v_w = lax.dynamic_slice_in_dim(vwp, start, NSA_WINDOW + Q_CHUNK, axis=1)
        wpos = start - NSA_WINDOW + jnp.arange(NSA_WINDOW + Q_CHUNK)
        diff = qpos[:, None] - wpos[None, :]
        m_win = (wpos[None, :] >= 0) & (diff >= 0) & (diff < NSA_WINDOW)
        p = masked_softmax(jnp.einsum('bqhd,bkd->bhqk', qc, k_w) * scale, m_win)
        o_win = jnp.einsum('bhqk,bkd->bqhd', p.astype(v_w.dtype), v_w)
        return o_slc, o_win

    o_slc, o_win = lax.map(block, (to_chunks(q_r), to_chunks(sel_idx), to_chunks(sel_valid), jnp.arange(S // Q_CHUNK)))
    o_slc, o_win = from_chunks(o_slc), from_chunks(o_win)
    g = jax.nn.sigmoid(gate_logits.astype(jnp.float32)).reshape(B, S, H, 3).astype(q.dtype)
    out = g[..., 0:1] * o_cmp + g[..., 1:2] * o_slc + g[..., 2:3] * o_win
    return out.reshape(B, S, H * D)


def dsa_attention(q, k, v, iq, ik, iw, pos):
    B, S, H, D = q.shape
    q, k = rope(q, pos), rope(k, pos)
    iq = rope(iq.reshape(B, S, DSA_IDX_HEADS, DSA_IDX_DIM), pos)
    ik = rope(ik[:, :, None], pos)[:, :, 0]
    keep = min(DSA_TOPK, S // 4)
    scale = D ** -0.5
    b_ix = jnp.arange(B)[:, None, None]
    kpos = jnp.arange(S)

    def block(args):
        qc, iqc, wc, c = args
        qpos = c * Q_CHUNK + jnp.arange(Q_CHUNK)
        dots = jnp.einsum('bqhd,bkd->bqhk', iqc, ik).astype(jnp.float32) * (DSA_IDX_DIM ** -0.5)
        score = jnp.einsum('bqhk,bqh->bqk', jax.nn.relu(dots), wc.astype(jnp.float32) * (DSA_IDX_HEADS ** -0.5))
        score = jnp.where(kpos[None, :] <= qpos[:, None], score, -jnp.inf)
        _, idx = lax.top_k(score, keep)
        valid = idx <= qpos[None, :, None]
        k_sel = k[b_ix, idx]
        v_sel = v[b_ix, idx]
        p = masked_softmax(jnp.einsum('bqhd,bqkhd->bhqk', qc, k_sel) * scale, valid[:, None])
        return jnp.einsum('bhqk,bqkhd->bqhd', p.astype(v_sel.dtype), v_sel)

    out = lax.map(block, (to_chunks(q), to_chunks(iq), to_chunks(iw), jnp.arange(S // Q_CHUNK)))
    return from_chunks(out).reshape(B, S, H * D)


def hybrid_mixer(x, pos, w_in, w_out, g_cq, g_ckv, w_uq, w_ukv, cmp_pe, cmp_w1, cmp_w2):
    B, S, _ = x.shape
    h = x @ w_in
    splits = np.cumsum(IN_SIZES)[:-1].tolist()
    (a_q, a_k, a_v, b_cq, b_ckv, b_kr, c_q, c_kv, c_g,
     d_q, d_k, d_v, d_iq, d_ik, d_iw) = jnp.split(h, splits, axis=-1)
    heads = lambda t, n: t.reshape(B, S, n, -1)
    o_a = moba_attention(heads(a_q, MOBA_HEADS), heads(a_k, MOBA_HEADS), heads(a_v, MOBA_HEADS), pos)
    o_b = mla_attention(b_cq, b_ckv, b_kr, g_cq, g_ckv, w_uq, w_ukv, pos)
    o_c = nsa_attention(heads(c_q, NSA_HEADS), heads(c_kv, NSA_BRANCH_KV), c_g, cmp_pe, cmp_w1, cmp_w2, pos)
    o_d = dsa_attention(heads(d_q, DSA_HEADS), heads(d_k, DSA_HEADS), heads(d_v, DSA_HEADS), d_iq, d_ik, d_iw, pos)
    return jnp.concatenate([o_a, o_b, o_c, o_d], axis=-1) @ w_out


def memory_cross_attention(x, mem, wq, wkv, wo):
    B, S, _ = x.shape
    M = mem.shape[1]
    q = (x @ wq).reshape(B, S, MEM_HEADS, HEAD_DIM)
    k, v = jnp.split((mem @ wkv).reshape(B, M, 2 * MEM_HEADS, HEAD_DIM), 2, axis=2)
    logits = jnp.einsum('bshd,bmhd->bhsm', q, k) * (HEAD_DIM ** -0.5)
    p = jax.nn.softmax(logits.astype(jnp.float32), axis=-1).astype(v.dtype)
    return jnp.einsum('bhsm,bmhd->bshd', p, v).reshape(B, S, MEM_HEADS * HEAD_DIM) @ wo


def setup_inputs(seed: int = 0) -> dict:
    key = jax.random.key(seed)
    ks = jax.random.split(key, 20)

    def w(k, shape, fan_in, scale=1.0):
        return jax.random.normal(k, shape, jnp.float32) * (scale * fan_in ** -0.5)

    hd = HEAD_DIM
    return {
        'x': jax.random.normal(ks[0], (BATCH, SEQ, D_MODEL), jnp.float32),
        'mem': jax.random.normal(ks[1], (BATCH, MEM_LEN, D_MODEL), jnp.float32),
        'positions': jnp.broadcast_to(jnp.arange(SEQ, dtype=jnp.int32), (BATCH, SEQ)),
        'ln_g': 1.0 + 0.02 * jax.random.normal(ks[2], (DEPTH, 4, D_MODEL), jnp.float32),
        'ln_b': 0.02 * jax.random.normal(ks[3], (DEPTH, 4, D_MODEL), jnp.float32),
        'ffn_w_gu': w(ks[4], (DEPTH, 2, D_MODEL, 2 * D_FF), D_MODEL),
        'ffn_w_down': w(ks[5], (DEPTH, 2, D_FF, D_MODEL), D_FF, DN_BETA),
        'w_in': w(ks[6], (DEPTH, D_MODEL, D_IN), D_MODEL),
        'w_out': w(ks[7], (DEPTH, D_MIX, D_MODEL), D_MIX, DN_BETA),
        'mla_g_cq': 1.0 + 0.02 * jax.random.normal(ks[8], (DEPTH, MLA_Q_RANK), jnp.float32),
        'mla_g_ckv': 1.0 + 0.02 * jax.random.normal(ks[9], (DEPTH, MLA_KV_RANK), jnp.float32),
        'mla_w_uq': w(ks[10], (DEPTH, MLA_Q_RANK, MLA_HEADS * (MLA_NOPE + MLA_ROPE)), MLA_Q_RANK),
        'mla_w_ukv': w(ks[11], (DEPTH, MLA_KV_RANK, MLA_HEADS * (MLA_NOPE + MLA_V)), MLA_KV_RANK),
        'nsa_cmp_pe': 0.02 * jax.random.normal(ks[12], (DEPTH, 2, NSA_CMP_LEN, hd), jnp.float32),
        'nsa_cmp_w1': w(ks[13], (DEPTH, 2, NSA_CMP_LEN * hd, hd), NSA_CMP_LEN * hd),
        'nsa_cmp_w2': w(ks[14], (DEPTH, 2, hd, hd), hd),
        'mem_wq': w(ks[15], (DEPTH, D_MODEL, MEM_HEADS * hd), D_MODEL),
        'mem_wkv': w(ks[16], (DEPTH, D_MODEL, 2 * MEM_HEADS * hd), D_MODEL),
        'mem_wo': w(ks[17], (DEPTH, MEM_HEADS * hd, D_MODEL), MEM_HEADS * hd, DN_BETA),
    }


def reference(x, mem, positions, ln_g, ln_b, ffn_w_gu, ffn_w_down, w_in, w_out,
              mla_g_cq, mla_g_ckv, mla_w_uq, mla_w_ukv, nsa_cmp_pe, nsa_cmp_w1, nsa_cmp_w2,
              mem_wq, mem_wkv, mem_wo):
    for l in range(DEPTH):
        x = layer_norm(DN_ALPHA * x + 0.5 * swiglu(x, ffn_w_gu[l, 0], ffn_w_down[l, 0]), ln_g[l, 0], ln_b[l, 0])
        mix = hybrid_mixer(x, positions, w_in[l], w_out[l], mla_g_cq[l], mla_g_ckv[l], mla_w_uq[l], mla_w_ukv[l],
                           nsa_cmp_pe[l], nsa_cmp_w1[l], nsa_cmp_w2[l])
        x = layer_norm(DN_ALPHA * x + mix, ln_g[l, 1], ln_b[l, 1])
        x = layer_norm(DN_ALPHA * x + memory_cross_attention(x, mem, mem_wq[l], mem_wkv[l], mem_wo[l]), ln_g[l, 2], ln_b[l, 2])
        x = layer_norm(DN_ALPHA * x + 0.5 * swiglu(x, ffn_w_gu[l, 1], ffn_w_down[l, 1]), ln_g[l, 3], ln_b[l, 3])
    return x
```

```python
import numpy as np
from contextlib import ExitStack
import concourse.bass as bass
import concourse.mybir as mybir
import concourse.bass_utils as _bu

F32 = mybir.dt.float32
BF16 = mybir.dt.bfloat16
I32 = mybir.dt.int32
ALU = mybir.AluOpType
AF = mybir.ActivationFunctionType
AX = mybir.AxisListType


class Tl:
    __slots__ = ("h", "name", "w", "r", "sem", "dcount", "psum")

    def __init__(self, h, name):
        self.h = h
        self.name = name
        self.w = None
        self.r = {}
        self.sem = None
        self.dcount = 0
        self.psum = False

    def __getitem__(self, idx):
        return V(self, self.h[idx])

    @property
    def v(self):
        return V(self, self.h[:])


class V:
    __slots__ = ("t", "ap")

    def __init__(self, t, ap):
        self.t = t
        self.ap = ap

    def __getitem__(self, idx):
        return V(self.t, self.ap[idx])

    def re(self, pat, **kw):
        return V(self.t, self.ap.rearrange(pat, **kw))

    def bc(self, shape):
        return V(self.t, self.ap.to_broadcast(shape))


def _ap(x):
    return x.ap if isinstance(x, V) else x


class K:
    ENG = ("pe", "act", "dve", "pool", "sp")

    def __init__(self, nc, stack):
        self.nc = nc
        self.stack = stack
        self.q = {e: [] for e in self.ENG}
        self.cnt = {e: 0 for e in self.ENG}
        self.sem = {e: stack.enter_context(nc.semaphore("s_" + e)) for e in self.ENG}
        self.known = {e: {} for e in self.ENG}
        self.out_waits = []
        self.ntile = 0
        self.outer = stack
        self.sempool = []
        self.scoped = []

    def sb(self, shape, dt, name=None):
        self.ntile += 1
        name = "sb_" + (name or "t%d" % self.ntile)
        h = self.stack.enter_context(self.nc.sbuf_tensor(name, list(shape), dt))
        t = Tl(h, name)
        if self.scoped:
            self.scoped[-1].append(t)
        return t

    def ps(self, shape, dt=F32, name=None):
        self.ntile += 1
        name = "ps_" + (name or "p%d" % self.ntile)
        h = self.stack.enter_context(self.nc.psum_tensor(name, list(shape), dt))
        t = Tl(h, name)
        t.psum = True
        if self.scoped:
            self.scoped[-1].append(t)
        return t

    def _deps(self, eng, reads, writes):
        deps = []
        for t in reads:
            if t.w is not None:
                deps.append(t.w)
        for t in writes:
            if t.w is not None:
                deps.append(t.w)
            deps.extend(t.r.values())
        waits = []
        kn = self.known[eng]
        for d in deps:
            if d[0] == "eng":
                _, e2, idx = d
                if e2 == eng and eng in ("pe", "sp"):
                    continue
                key = e2
                if kn.get(key, 0) >= idx:
                    continue
                kn[key] = idx
                waits.append((self.sem[e2], idx))
            else:
                _, t, c = d
                key = t.name
                if kn.get(key, 0) >= c:
                    continue
                kn[key] = c
                waits.append((t.sem, 16 * c))
        return waits

    def op(self, eng, fn, reads, writes):
        reads = [x.t if isinstance(x, V) else x for x in reads if x is not None and not isinstance(x, (int, float))]
        writes = [x.t if isinstance(x, V) else x for x in writes]
        reads = [t for t in reads if isinstance(t, Tl)]
        writes = writes + [t for t in reads if t.psum]
        reads = [t for t in reads if not t.psum]
        waits = self._deps(eng, reads, writes)
        self.cnt[eng] += 1
        idx = self.cnt[eng]
        self.q[eng].append((waits, fn, True))
        for t in reads:
            t.r[eng] = ("eng", eng, idx)
        for t in writes:
            t.w = ("eng", eng, idx)
            t.r = {}

    def dma(self, qeng, out, in_, is_output=False):
        reads = [in_.t] if isinstance(in_, V) else []
        writes = [out.t] if isinstance(out, V) else []
        waits = self._deps(qeng, reads, writes)
        t = (writes or reads)[0]
        if t.sem is None:
            if self.sempool:
                t.sem, t.dcount = self.sempool.pop()
            else:
                t.sem = self.outer.enter_context(self.nc.semaphore("d_" + t.name))
        t.dcount += 1
        c = t.dcount
        sem = t.sem
        o, i = _ap(out), _ap(in_)
        self.q[qeng].append((waits, lambda e: e.dma_start(out=o, in_=i), ("dma", sem)))
        for x in writes:
            x.w = ("dma", t, c)
            x.r = {}
        for x in reads:
            x.r["dma"] = ("dma", t, c)
        if is_output:
            self.out_waits.append((sem, 16 * c))

    def mm(self, out, lhsT, rhs, start=True, stop=True):
        o, l, r = _ap(out), _ap(lhsT), _ap(rhs)
        self.op("pe", lambda e: e.matmul(o, l, r, start=start, stop=stop), [lhsT, rhs], [out])

    def tr(self, out, in_, ident):
        o, i, d = _ap(out), _ap(in_), _ap(ident)
        self.op("pe", lambda e: e.transpose(o, i, d), [in_, ident], [out])

    def act(self, out, in_, func, bias=None, scale=None, accum_out=None, eng="act"):
        o, i = _ap(out), _ap(in_)
        kw = {}
        if bias is not None:
            kw["bias"] = _ap(bias)
        if scale is not None:
            kw["scale"] = _ap(scale)
        if accum_out is not None:
            kw["accum_out"] = _ap(accum_out)
        w = [out] + ([accum_out] if accum_out is not None else [])
        self.op(eng, lambda e: e.activation(o, i, func, **kw), [in_, bias, scale], w)

    def tt(self, out, in0, in1, op, eng="dve"):
        o, a, b = _ap(out), _ap(in0), _ap(in1)
        self.op(eng, lambda e: e.tensor_tensor(o, a, b, op), [in0, in1], [out])

    def ts(self, out, in0, s1, op0, s2=None, op1=None, accum_out=None, eng="dve"):
        o, a = _ap(out), _ap(in0)
        x1, x2 = _ap(s1), _ap(s2)
        kw = {}
        if op1 is not None:
            kw["op1"] = op1
        if accum_out is not None:
            kw["accum_out"] = _ap(accum_out)
        w = [out] + ([accum_out] if accum_out is not None else [])
        self.op(eng, lambda e: e.tensor_scalar(o, a, x1, x2, op0, **kw), [in0, s1, s2], w)

    def stt(self, out, in0, scalar, in1, op0, op1, eng="dve"):
        o, a, s, b = _ap(out), _ap(in0), _ap(scalar), _ap(in1)
        self.op(eng, lambda e: e.scalar_tensor_tensor(o, a, s, b, op0, op1), [in0, scalar, in1], [out])

    def copy(self, out, in_, eng="dve"):
        o, i = _ap(out), _ap(in_)
        if eng == "act":
            self.op(eng, lambda e: e.copy(o, i), [in_], [out])
        else:
            self.op(eng, lambda e: e.tensor_copy(o, i), [in_], [out])

    def memset(self, out, val, eng="dve"):
        o = _ap(out)
        self.op(eng, lambda e: e.memset(o, val), [], [out])

    def reduce(self, out, in_, op, axis=AX.X, eng="dve"):
        o, i = _ap(out), _ap(in_)
        self.op(eng, lambda e: e.tensor_reduce(o, i, axis, op), [in_], [out])

    def max8(self, out, in_):
        o, i = _ap(out), _ap(in_)
        self.op("dve", lambda e: e.max(o, i), [in_], [out])

    def mrep(self, out, rep, vals, imm):
        o, r, v = _ap(out), _ap(rep), _ap(vals)
        self.op("dve", lambda e: e.match_replace(o, r, v, imm), [rep, vals], [out])

    def recip(self, out, in_):
        o, i = _ap(out), _ap(in_)
        self.op("dve", lambda e: e.reciprocal(o, i), [in_], [out])

    def scope(self):
        k = self

        class _S:
            def __enter__(self_):
                self_.prev = k.stack
                self_.st = ExitStack()
                self_.st.__enter__()
                k.stack = self_.st
                k.scoped.append([])
                return self_

            def __exit__(self_, *a):
                tiles = k.scoped.pop()
                extra = [(t.sem, 16 * t.dcount) for t in tiles if t.sem is not None]
                for t in tiles:
                    d_ = t.r.get("dma")
                    if d_ is not None:
                        extra.append((d_[1].sem, 16 * d_[2]))
                k.emit(extra)
                for t in tiles:
                    if t.sem is not None:
                        k.sempool.append((t.sem, t.dcount))
                        t.sem = None
                k.stack = self_.prev
                self_.st.__exit__(None, None, None)
                return False
        return _S()

    def emit(self, extra=()):
        nc = self.nc
        fin = list(self.out_waits) + list(extra)
        self.out_waits = []
        q = self.q
        self.q = {e: [] for e in self.ENG}
        sems = self.sem
        self.cnt["sp"] += 1
        cnt = dict(self.cnt)
        for e_ in self.ENG:
            for e2 in self.ENG:
                self.known[e_][e2] = max(self.known[e_].get(e2, 0), cnt[e2])

        def replay(name, e):
            for waits, fn, inc in q[name]:
                for s, v in waits:
                    e.wait_ge(s, v)
                ins = fn(e)
                if inc is True:
                    ins.then_inc(sems[name], 1)
                elif inc is not None:
                    ins.then_inc(inc[1], 16)
            if name == "sp":
                for s, v in fin:
                    e.wait_ge(s, v)
                e.sem_inc(sems["sp"], 1)
            for e2 in ("pe", "act", "dve", "pool", "sp"):
                if e2 != name and cnt[e2] > 0:
                    e.wait_ge(sems[e2], cnt[e2])

        with nc.Block() as block:
            @block.tensor
            def _(e):
                replay("pe", e)

            @block.scalar
            def _(e):
                replay("act", e)

            @block.vector
            def _(e):
                replay("dve", e)

            @block.gpsimd
            def _(e):
                replay("pool", e)

            @block.sync
            def _(e):
                replay("sp", e)

import math

ALPHA = 8 ** 0.25
TOK = 1024
NJ = 8
TWO_PI = 2 * math.pi

FM_QA, FM_KA, FM_QBN, FM_QBR, FM_KBN, FM_KBR = 0, 4, 8, 12, 14, 18
FM_QC, FM_QCR, FM_KCMP, FM_VCMP, FM_KSLC, FM_KWIN = 19, 23, 27, 28, 29, 30
FM_QD, FM_KD, FM_IQ, FM_IK = 31, 35, 39, 47
NFM = 48
TM_VA, TM_VB, TM_VSLC, TM_VWIN, TM_VD, NTM = 0, 512, 1024, 1152, 1280, 1792


def _swap(cols, half):
    c = np.asarray(cols).reshape(-1, 2, half)
    return c[:, ::-1, :].reshape(-1)


def fm_entries():
    hd = lambda base, h: np.arange(base + h * 128, base + (h + 1) * 128)
    E = []
    for h in range(4):
        E.append(dict(cols=hd(0, h), rope=128, dst=FM_QA + h))
    for h in range(4):
        E.append(dict(cols=hd(512, h), rope=128, dst=FM_KA + h))
    for fc in range(4):
        E.append(dict(cols=hd(1536, fc), rope=None, dst=("cq", fc)))
    for fc in range(4):
        E.append(dict(cols=hd(2048, fc), rope=None, dst=("ckv", fc)))
    kr = np.arange(2560, 2624)
    E.append(dict(cols=np.concatenate([kr, kr]), rope=64, dst=FM_KBR))
    for h in range(4):
        E.append(dict(cols=hd(2624, h), rope=128, dst=FM_QCR + h, raw=FM_QC + h))
    E.append(dict(cols=hd(3136, 0), rope=None, dst=FM_KCMP))
    E.append(dict(cols=hd(3136, 1), rope=None, dst=FM_VCMP))
    E.append(dict(cols=hd(3136, 2), rope=128, dst=FM_KSLC))
    E.append(dict(cols=hd(3136, 4), rope=128, dst=FM_KWIN))
    for h in range(4):
        E.append(dict(cols=hd(3916, h), rope=128, dst=FM_QD + h))
    for h in range(4):
        E.append(dict(cols=hd(4428, h), rope=128, dst=FM_KD + h))
    for j in range(8):
        E.append(dict(cols=hd(5452, j), rope=64, dst=FM_IQ + j))
    ik = np.arange(6476, 6540)
    E.append(dict(cols=np.concatenate([ik, ik]), rope=64, dst=FM_IK))
    n = 0
    for e in E:
        e["w1"] = n
        n += 1
        if e["rope"]:
            e["w2"] = n
            n += 1
    return E, n


def host_prep_A(l, inp):
    E, nw = fm_entries()
    w_in = inp["w_in"][l]
    cols = []
    for e in E:
        cols.append(e["cols"])
        if e["rope"]:
            cols.append(_swap(e["cols"], e["rope"] // 2))
    cols = np.concatenate(cols)
    wfm = w_in[:, cols].reshape(16, 128, nw, 128).transpose(2, 1, 0, 3)
    d = {}
    d["wfm"] = np.ascontiguousarray(wfm)
    tmc1 = np.concatenate([np.arange(1024, 1536), np.arange(4940, 5452)])
    tmc2 = np.concatenate([np.arange(3136 + 3 * 128, 3136 + 4 * 128), np.arange(3136 + 5 * 128, 3136 + 6 * 128),
                           np.arange(3904, 3916), np.arange(6540, 6556)])
    d["wtm1"] = np.ascontiguousarray(w_in[:, tmc1].reshape(16, 128, 1024).transpose(1, 0, 2))
    d["wtm2"] = np.ascontiguousarray(w_in[:, tmc2].reshape(16, 128, 284).transpose(1, 0, 2))
    for i in range(1):
        gu = inp["ffn_w_gu"][l, i]
        d["wgu%d" % i] = np.ascontiguousarray(gu.reshape(16, 128, 88, 128).transpose(2, 1, 0, 3))
        d["wd%d" % i] = np.ascontiguousarray(inp["ffn_w_down"][l, i].reshape(44, 128, 2048))
    uq = inp["mla_w_uq"][l]
    qc = []
    for h in range(4):
        qc.append(np.arange(h * 192, h * 192 + 128))
    for pr in range(2):
        rc = np.concatenate([np.arange((2 * pr) * 192 + 128, (2 * pr) * 192 + 192), np.arange((2 * pr + 1) * 192 + 128, (2 * pr + 1) * 192 + 192)])
        qc.append(rc)
        qc.append(_swap(rc, 32))
    qc = np.concatenate(qc)
    d["wuq"] = np.ascontiguousarray(uq[:, qc].reshape(4, 128, 8, 128).transpose(2, 1, 0, 3))
    ukv = inp["mla_w_ukv"][l]
    kc = np.concatenate([np.arange(h * 256, h * 256 + 128) for h in range(4)])
    vc = np.concatenate([np.arange(h * 256 + 128, h * 256 + 256) for h in range(4)])
    d["wuk"] = np.ascontiguousarray(ukv[:, kc].reshape(4, 128, 4, 128).transpose(2, 1, 0, 3))
    d["wuv"] = np.ascontiguousarray(ukv[:, vc].reshape(4, 128, 512).transpose(1, 0, 2))
    d["gq"] = np.ascontiguousarray(inp["mla_g_cq"][l].reshape(4, 128).T)
    d["gkv"] = np.ascontiguousarray(inp["mla_g_ckv"][l].reshape(4, 128).T)
    d["lng"] = np.ascontiguousarray(np.broadcast_to(inp["ln_g"][l][None], (128, 4, 2048)))
    d["lnb"] = np.ascontiguousarray(np.broadcast_to(inp["ln_b"][l][None], (128, 4, 2048)))
    return d


def rope_consts():
    i = np.arange(128)
    c = np.zeros((128, 8), np.float32)
    c[:, 0] = 10000.0 ** (-(2.0 * (i % 64)) / 128.0)
    c[:, 1] = 10000.0 ** (-(2.0 * (i % 32)) / 64.0)
    c[:, 2] = np.where(i % 128 < 64, 1.0, -1.0)
    c[:, 3] = np.where(i % 64 < 32, 1.0, -1.0)
    return c


def ident_np():
    import ml_dtypes
    return np.eye(128, dtype=np.float32).astype(ml_dtypes.bfloat16)


def emit_layernorm(k, Xj, g_v, b_v, sm, junk):
    k.reduce(sm[:, 0:1], Xj.v, ALU.add)
    k.act(junk.v, Xj.v, AF.Square, accum_out=sm[:, 1:2])
    k.ts(sm[:, 2:3], sm[:, 0:1], 1.0 / 2048, ALU.mult)
    k.tt(sm[:, 3:4], sm[:, 2:3], sm[:, 2:3], ALU.mult)
    k.stt(sm[:, 4:5], sm[:, 1:2], 1.0 / 2048, sm[:, 3:4], ALU.mult, ALU.subtract)
    k.ts(sm[:, 4:5], sm[:, 4:5], 1e-5, ALU.add)
    k.act(sm[:, 5:6], sm[:, 4:5], AF.Sqrt)
    k.recip(sm[:, 6:7], sm[:, 5:6])
    k.ts(Xj.v, Xj.v, sm[:, 2:3], ALU.subtract, sm[:, 6:7], ALU.mult)
    k.tt(Xj.v, Xj.v, g_v, ALU.mult)
    k.tt(Xj.v, Xj.v, b_v, ALU.add, eng="pool")


def emit_transposeX(k, X, XT, xb, ptr, ident):
    for j in range(NJ):
        k.copy(xb.v, X[j].v, eng="act")
        for g in range(4):
            p = ptr[g % 2]
            for q in range(4):
                kc = g * 4 + q
                k.tr(p[:, q * 128:(q + 1) * 128], xb[:, kc * 128:(kc + 1) * 128], ident.v)
            k.copy(XT[:, g * 4:(g + 1) * 4, j * 128:(j + 1) * 128], p[:, 0:512].re("p (q n) -> p q n", q=4),
                   eng="dve" if g % 2 == 0 else "act")


def emit_ffn(k, X, XT, wgu, wd, R):
    HT, WG, WU, WD, PG, PU, PD, SG = R["HT"], R["WG"], R["WU"], R["WD"], R["PG"], R["PU"], R["PD"], R["SG"]
    nd = 0
    for fb in range(4):
        for fc in range(11):
            f = fb * 11 + fc
            wg, wu = WG[f % 3], WU[f % 3]
            k.dma("pool", wg.v, wgu[f])
            k.dma("pool", wu.v, wgu[44 + f])
            for half in range(2):
                ts_ = slice(half * 512, (half + 1) * 512)
                for kc in range(16):
                    k.mm(PG[half].v, wg[:, kc, :], XT[:, kc, ts_], start=(kc == 0), stop=(kc == 15))
                for kc in range(16):
                    k.mm(PU[half].v, wu[:, kc, :], XT[:, kc, ts_], start=(kc == 0), stop=(kc == 15))
                k.act(SG[half].v, PG[half].v, AF.Silu)
                k.tt(HT[:, fc, ts_], SG[half].v, PU[half].v, ALU.mult)
        for n in range(4):
            w = WD[nd % 2]
            nd += 1
            k.dma("pool", w.v, wd[fb * 11:(fb + 1) * 11, :, n * 512:(n + 1) * 512].rearrange("f p n -> p f n"))
            for j in range(NJ):
                pd = PD[j % 2]
                for fc in range(11):
                    k.mm(pd.v, HT[:, fc, j * 128:(j + 1) * 128], w[:, fc, :], start=(fc == 0), stop=(fc == 10))
                xs = X[j][:, n * 512:(n + 1) * 512]
                k.stt(xs, pd.v, 0.5, xs, ALU.mult, ALU.add)


def alloc_common(k):
    R = {}
    R["XT"] = k.sb([128, 16, TOK], BF16, "XT")
    R["xb"] = k.sb([128, 2048], BF16, "xb")
    R["junk"] = k.sb([128, 2048], BF16, "junk")
    R["ident"] = k.sb([128, 128], BF16, "ident")
    R["sm"] = k.sb([128, 8], F32, "sm")
    R["lng"] = k.sb([128, 2048], F32, "lng")
    R["lnb"] = k.sb([128, 2048], F32, "lnb")
    R["ptr"] = [k.ps([128, 1024], BF16, "ptr%d" % i) for i in range(2)]
    R["P"] = [k.ps([128, 512], F32, "P%d" % i) for i in range(6)]
    return R


def alloc_ffn(k, R):
    R["HT"] = k.sb([128, 11, TOK], BF16, "HT")
    R["WG"] = [k.sb([128, 16, 128], BF16, "WG%d" % i) for i in range(3)]
    R["WU"] = [k.sb([128, 16, 128], BF16, "WU%d" % i) for i in range(3)]
    R["WD"] = [k.sb([128, 11, 512], BF16, "WD%d" % i) for i in range(2)]
    R["SG"] = [k.sb([128, 512], F32, "SG%d" % i) for i in range(2)]
    P = R["P"]
    R["PG"], R["PU"], R["PD"] = P[0:2], P[2:4], P[4:6]


def build_A(UPTO=9):
    nc = bass.Bass("TRN2", target_bir_lowering=False)
    E, nw = fm_entries()
    dt = lambda name, shape, dtype, kind="ExternalInput": nc.dram_tensor(name, list(shape), dtype, kind=kind).ap()
    x_in = dt("x_in", [TOK, 2048], F32)
    pos_in = dt("pos", [128, TOK], I32)
    rc_in = dt("ropec", [128, 8], F32)
    id_in = dt("ident", [128, 128], BF16)
    wgu = dt("wgu0", [88, 128, 16, 128], F32)
    wd = dt("wd0", [44, 128, 2048], F32)
    wfm = dt("wfm", [nw, 128, 16, 128], F32)
    wtm1 = dt("wtm1", [128, 16, 1024], F32)
    wtm2 = dt("wtm2", [128, 16, 284], F32)
    wuq = dt("wuq", [8, 128, 4, 128], F32)
    wuk = dt("wuk", [4, 128, 4, 128], F32)
    wuv = dt("wuv", [128, 4, 512], F32)
    gq_in = dt("gq", [128, 4], F32)
    gkv_in = dt("gkv", [128, 4], F32)
    lng_in = dt("lng", [128, 4, 2048], F32)
    lnb_in = dt("lnb", [128, 4, 2048], F32)
    x1_out = dt("x1", [TOK, 2048], F32, "ExternalOutput")
    fm_out = dt("fm", [NFM, 128, TOK], BF16, "ExternalOutput")
    tm_out = dt("tm", [TOK, NTM], BF16, "ExternalOutput")
    tmf_out = dt("tmf", [TOK, 32], F32, "ExternalOutput")

    with ExitStack() as st:
        k = K(nc, st)
        R = alloc_common(k)
        XT, P = R["XT"], R["P"]
        k.dma("sp", R["ident"].v, id_in)
        k.dma("sp", R["lng"].v, lng_in[:, 0, :])
        k.dma("sp", R["lnb"].v, lnb_in[:, 0, :])
        with k.scope():
            X = R["X"] = [k.sb([128, 2048], F32, "X%d" % j) for j in range(NJ)]
            for j in range(NJ):
                k.dma("sp", X[j].v, x_in[j * 128:(j + 1) * 128, :])
            emit_transposeX(k, X, XT, R["xb"], R["ptr"], R["ident"])
            for j in range(NJ):
                k.ts(X[j].v, X[j].v, ALPHA, ALU.mult, eng="pool")
            if UPTO >= 2:
              with k.scope():
                alloc_ffn(k, R)
                emit_ffn(k, X, XT, wgu, wd, R)
            for j in range(NJ):
                if UPTO >= 2:
                    emit_layernorm(k, X[j], R["lng"].v, R["lnb"].v, R["sm"], R["junk"])
                k.dma("sp", x1_out[j * 128:(j + 1) * 128, :], X[j].v, is_output=True)
            emit_transposeX(k, X, XT, R["xb"], R["ptr"], R["ident"])

        if UPTO < 3:
            k.emit()
            return nc
        rc = k.sb([128, 8], F32, "rc")
        k.dma("sp", rc.v, rc_in)
        CT, ST = {}, {}
        for kind in (128, 64):
            CT[kind] = k.sb([128, TOK], F32, "C%d" % kind)
            ST[kind] = k.sb([128, TOK], F32, "S%d" % kind)
        ropescope = k.scope()
        ropescope.__enter__()
        posi = k.sb([128, TOK], I32, "posi")
        k.dma("sp", posi.v, pos_in)
        posf = k.sb([128, TOK], F32, "posf")
        k.copy(posf.v, posi.v)
        ang = k.sb([128, TOK], F32, "ang")
        kf = k.sb([128, TOK], F32, "kf")
        r = k.sb([128, TOK], F32, "r")
        for kind, ci in ((128, 0), (64, 1)):
            for which in ("sin", "cos"):
                k.ts(ang.v, posf.v, rc[:, ci:ci + 1], ALU.mult)
                if which == "cos":
                    k.ts(ang.v, ang.v, math.pi / 2, ALU.add)
                k.ts(kf.v, ang.v, 1.0 / TWO_PI, ALU.mult)
                k.copy(posi.v, kf.v)
                k.copy(kf.v, posi.v)
                k.stt(r.v, kf.v, -TWO_PI, ang.v, ALU.mult, ALU.add)
                k.ts(kf.v, r.v, math.pi, ALU.is_gt, -TWO_PI, ALU.mult)
                k.tt(r.v, r.v, kf.v, ALU.add)
                k.ts(kf.v, r.v, -math.pi, ALU.is_lt, TWO_PI, ALU.mult)
                k.tt(r.v, r.v, kf.v, ALU.add)
                if which == "sin":
                    k.act(ST[kind].v, r.v, AF.Sin)
                    k.ts(ST[kind].v, ST[kind].v, rc[:, 2 + ci:3 + ci], ALU.mult, -1.0, ALU.mult)
                else:
                    k.act(CT[kind].v, r.v, AF.Sin)

        ropescope.__exit__(None, None, None)
        if UPTO < 3.5:
            dbg = k.sb([128, TOK], BF16, "dbg")
            for i_, t_ in enumerate((CT[128], ST[128], CT[64], ST[64])):
                k.copy(dbg.v, t_.v)
                k.dma("sp", fm_out[i_], dbg.v, is_output=True)
            k.emit()
            return nc
        fmscope = k.scope()
        fmscope.__enter__()
        W1 = [k.sb([128, 16, 128], BF16, "W1_%d" % i) for i in range(2)]
        W2 = [k.sb([128, 16, 128], BF16, "W2_%d" % i) for i in range(2)]
        DST = [k.sb([128, TOK], BF16, "DST%d" % i) for i in range(4)]
        T1 = [k.sb([128, 512], F32, "T1_%d" % i) for i in range(2)]
        T2 = [k.sb([128, 512], F32, "T2_%d" % i) for i in range(2)]
        cqg = k.sb([128, 4, TOK], BF16, "cqg")
        ckvg = k.sb([128, 4, TOK], BF16, "ckvg")
        sqq = k.sb([128, 4, TOK], BF16, "sqq")
        sqkv = k.sb([128, 4, TOK], BF16, "sqkv")
        gq = k.sb([128, 4], F32, "gq")
        gkv = k.sb([128, 4], F32, "gkv")
        k.dma("sp", gq.v, gq_in)
        k.dma("sp", gkv.v, gkv_in)
        nd = 0
        PY, PS = P[0:2], P[2:4]
        import os
        if os.environ.get("ENT"):
            E = [E[int(i_)] for i_ in os.environ["ENT"].split(",")]
        for ei, e in enumerate(E):
            w1 = W1[ei % 2]
            k.dma("pool", w1.v, wfm[e["w1"]])
            if e["rope"]:
                w2 = W2[ei % 2]
                k.dma("pool", w2.v, wfm[e["w2"]])
            dst = None
            raw = None
            if isinstance(e["dst"], int):
                dst = DST[nd % 4]
                nd += 1
                if "raw" in e:
                    raw = DST[nd % 4]
                    nd += 1
            for half in range(2):
                ts_ = slice(half * 512, (half + 1) * 512)
                for kc in range(16):
                    k.mm(PY[half].v, w1[:, kc, :], XT[:, kc, ts_], start=(kc == 0), stop=(kc == 15))
                if e["rope"]:
                    for kc in range(16):
                        k.mm(PS[half].v, w2[:, kc, :], XT[:, kc, ts_], start=(kc == 0), stop=(kc == 15))
                    kind = e["rope"]
                    k.tt(T1[half].v, PY[half].v, CT[kind][:, ts_], ALU.mult)
                    k.tt(T2[half].v, PS[half].v, ST[kind][:, ts_], ALU.mult)
                    k.tt(dst[:, ts_], T1[half].v, T2[half].v, ALU.add)
                    if raw is not None:
                        k.copy(raw[:, ts_], PY[half].v, eng="act")
                elif dst is not None:
                    k.copy(dst[:, ts_], PY[half].v, eng="act")
                else:
                    which, fc = e["dst"]
                    cg, sq, g = (cqg, sqq, gq) if which == "cq" else (ckvg, sqkv, gkv)
                    k.ts(cg[:, fc, ts_], PY[half].v, g[:, fc:fc + 1], ALU.mult)
                    k.act(sq[:, fc, ts_], PY[half].v, AF.Square)
            if dst is not None:
                k.dma("sp", fm_out[e["dst"]], dst.v, is_output=True)
            if raw is not None:
                k.dma("sp", fm_out[e["raw"]], raw.v, is_output=True)

        if UPTO < 4:
            fmscope.__exit__(None, None, None)
            return nc
        ones = k.sb([128, 128], BF16, "ones")
        k.memset(ones.v, 1.0)
        rstdB = {}
        rstdC = {}
        for which, sq in (("q", sqq), ("kv", sqkv)):
            rb = k.sb([128, TOK], F32, "rstdB" + which)
            rcl = k.sb([128, NJ], F32, "rstdC" + which)
            for half in range(2):
                ts_ = slice(half * 512, (half + 1) * 512)
                for fc in range(4):
                    k.mm(PY[half].v, ones.v, sq[:, fc, ts_], start=(fc == 0), stop=(fc == 3))
                k.ts(T1[half].v, PY[half].v, 1.0 / 512, ALU.mult, 1e-6, ALU.add)
                k.act(T1[half].v, T1[half].v, AF.Sqrt)
                k.recip(rb[:, ts_], T1[half].v)
            for j in range(NJ):
                for fc in range(4):
                    k.mm(PS[0][:, j:j + 1], sq[:, fc, j * 128:(j + 1) * 128], ones[:, 0:1], start=(fc == 0), stop=(fc == 3))
            k.ts(T2[0][:, 0:NJ], PS[0][:, 0:NJ], 1.0 / 512, ALU.mult, 1e-6, ALU.add)
            k.act(T2[0][:, 0:NJ], T2[0][:, 0:NJ], AF.Sqrt)
            k.recip(rcl.v, T2[0][:, 0:NJ])
            rstdB[which], rstdC[which] = rb, rcl
        WQ = [k.sb([128, 4, 128], BF16, "WQ%d" % i) for i in range(4)]
        jobs = [("n", c, None, FM_QBN + c) for c in range(4)] + [("r", 4, 5, FM_QBR), ("r", 6, 7, FM_QBR + 1)]
        wi = 0
        for kind_, c1, c2, di in jobs:
            w1 = WQ[wi % 4]; wi += 1
            k.dma("pool", w1.v, wuq[c1])
            if c2 is not None:
                w2 = WQ[wi % 4]; wi += 1
                k.dma("pool", w2.v, wuq[c2])
            dst = DST[nd % 4]; nd += 1
            for half in range(2):
                ts_ = slice(half * 512, (half + 1) * 512)
                for fc in range(4):
                    k.mm(PY[half].v, w1[:, fc, :], cqg[:, fc, ts_], start=(fc == 0), stop=(fc == 3))
                if c2 is None:
                    k.tt(dst[:, ts_], PY[half].v, rstdB["q"][:, ts_], ALU.mult)
                else:
                    for fc in range(4):
                        k.mm(PS[half].v, w2[:, fc, :], cqg[:, fc, ts_], start=(fc == 0), stop=(fc == 3))
                    k.tt(T1[half].v, PY[half].v, CT[64][:, ts_], ALU.mult)
                    k.tt(T2[half].v, PS[half].v, ST[64][:, ts_], ALU.mult)
                    k.tt(T1[half].v, T1[half].v, T2[half].v, ALU.add)
                    k.tt(dst[:, ts_], T1[half].v, rstdB["q"][:, ts_], ALU.mult)
            k.dma("sp", fm_out[di], dst.v, is_output=True)
        for c in range(4):
            w1 = WQ[wi % 4]; wi += 1
            k.dma("pool", w1.v, wuk[c])
            dst = DST[nd % 4]; nd += 1
            for half in range(2):
                ts_ = slice(half * 512, (half + 1) * 512)
                for fc in range(4):
                    k.mm(PY[half].v, w1[:, fc, :], ckvg[:, fc, ts_], start=(fc == 0), stop=(fc == 3))
                k.tt(dst[:, ts_], PY[half].v, rstdB["kv"][:, ts_], ALU.mult)
            k.dma("sp", fm_out[FM_KBN + c], dst.v, is_output=True)
        WV = k.sb([128, 4, 512], BF16, "WV")
        k.dma("pool", WV.v, wuv)
        TMO = [k.sb([128, 512], BF16, "TMO%d" % i) for i in range(2)]
        nt = 0
        for j in range(NJ):
            p = P[4 + j % 2]
            for fc in range(4):
                k.mm(p.v, ckvg[:, fc, j * 128:(j + 1) * 128], WV[:, fc, :], start=(fc == 0), stop=(fc == 3))
            o = TMO[nt % 2]; nt += 1
            k.ts(o.v, p.v, rstdC["kv"][:, j:j + 1], ALU.mult)
            k.dma("sp", tm_out[j * 128:(j + 1) * 128, TM_VB:TM_VB + 512], o.v, is_output=True)

        fmscope.__exit__(None, None, None)
        TMO = [k.sb([128, 512], BF16, "TMOb%d" % i) for i in range(2)]
        WT = k.sb([128, 16, 1024], BF16, "WT")
        k.dma("pool", WT.v, wtm1)
        for j in range(NJ):
            for gi, c0 in ((0, TM_VA), (1, TM_VD)):
                p = P[4 + gi]
                for kc in range(16):
                    k.mm(p.v, XT[:, kc, j * 128:(j + 1) * 128], WT[:, kc, gi * 512:(gi + 1) * 512], start=(kc == 0), stop=(kc == 15))
                o = TMO[nt % 2]; nt += 1
                k.copy(o.v, p.v, eng="act")
                k.dma("sp", tm_out[j * 128:(j + 1) * 128, c0:c0 + 512], o.v, is_output=True)
        WT2 = k.sb([128, 16, 284], BF16, "WT2")
        k.dma("pool", WT2.v, wtm2)
        TF = [k.sb([128, 32], F32, "TF%d" % i) for i in range(2)]
        for j in range(NJ):
            p = P[4 + j % 2]
            for kc in range(16):
                k.mm(p[:, 0:284], XT[:, kc, j * 128:(j + 1) * 128], WT2[:, kc, :], start=(kc == 0), stop=(kc == 15))
            o = TMO[nt % 2]; nt += 1
            k.copy(o[:, 0:256], p[:, 0:256], eng="act")
            k.dma("sp", tm_out[j * 128:(j + 1) * 128, TM_VSLC:TM_VSLC + 256], o[:, 0:256], is_output=True)
            tf = TF[j % 2]
            k.memset(tf.v, 0.0)
            k.act(tf[:, 0:12], p[:, 256:268], AF.Sigmoid)
            k.copy(tf[:, 12:28], p[:, 268:284])
            k.dma("sp", tmf_out[j * 128:(j + 1) * 128, :], tf.v, is_output=True)
        k.emit()
    return nc

import ml_dtypes

NEG = -32768.0
IMM = -2.0e30
BIGN = -1.0e30
KG_KA, KG_KBN, KG_KBR, KG_KCMP, KG_VCMP, KG_KSLC, KG_KWIN, KG_KD, KG_IK, NKG = 0, 4, 8, 9, 10, 11, 12, 13, 17, 18
QF_QA, QF_QBN, QF_QBR, QF_QC, QF_QCR, QF_QD, QF_IQ, NQF = 0, 4, 8, 10, 14, 18, 22, 30
BF = ml_dtypes.bfloat16


def core_masks(c):
    d = {}
    kk = np.arange(128)[:, None]
    qq = np.arange(128)[None, :]
    tri = np.where(kk <= qq, 0.0, NEG).astype(np.float32)
    anti = np.where(kk > qq, 0.0, NEG).astype(np.float32)
    full = np.zeros((128, 128), np.float32)
    none = np.full((128, 128), NEG, np.float32)
    cT = np.stack([full if m < c else (tri if m == c else none) for m in range(8)], 1)
    d["causalT4"] = np.ascontiguousarray(np.tile(cT[:, :, None, :], (1, 1, 4, 1)).reshape(128, 8, 512)).astype(BF)
    w = []
    for mp in range(12):
        rel = mp - 4 - c
        w.append(none if (rel < -4 or rel > 0) else (anti if rel == -4 else (tri if rel == 0 else full)))
    wT = np.stack(w, 1)
    d["winT4"] = np.ascontiguousarray(np.tile(wT[:, :, None, :], (1, 1, 4, 1)).reshape(128, 12, 512)).astype(BF)
    triq = np.where(kk.T >= qq.T * 0 + np.arange(128)[None, :], 0.0, BIGN)
    qi = np.arange(128)[:, None]
    ki = np.arange(128)[None, :]
    triq = np.where(ki <= qi, 0.0, BIGN).astype(np.float32)
    cq = np.stack([np.zeros((128, 128), np.float32) if m < c else (triq if m == c else np.full((128, 128), BIGN, np.float32)) for m in range(8)], 1)
    d["causalQ"] = np.ascontiguousarray(cq).astype(np.float32)
    cm = np.zeros((128, 8, 4, 128), np.float32)
    for j in range(8):
        gc = 8 * j + c
        for nch in range(4):
            ng = nch * 128 + np.arange(128)[:, None]
            t = 128 * gc + np.arange(128)[None, :]
            ok = (16 * ng + 31 <= t) & (ng <= 510)
            cm[:, j, nch, :] = np.where(ok, 0.0, NEG)
    d["cmpmask4"] = np.ascontiguousarray(np.tile(cm[:, :, :, None, :], (1, 1, 1, 4, 1)).reshape(128, 8, 4, 512)).astype(BF)
    gm = np.zeros((128, 8, 4, 32), np.float32)
    own = np.zeros((128, 8, 4, 32), np.float32)
    F = np.zeros((128, 8, 128), np.float32)
    for j in range(8):
        gc = 8 * j + c
        cur = gc // 2
        n = np.arange(32)
        gm[:, j, :, :] = np.where(n < cur, 0.0, BIGN)[None, None, :]
        own[:, j, :, :] = np.where(n >= cur, 1.0, 0.0)[None, None, :]
        curq = 2 * gc + (np.arange(128) >= 64).astype(np.int64)
        b = np.arange(128)[None, :]
        cq_ = curq[:, None]
        forced = ((b == 0) | (b == cq_) | (b == cq_ - 1)) & (b <= cq_)
        F[:, j, :] = np.where(b > cq_, -1e4, np.where(forced, 1e4, 0.0))
    d["gm4"], d["own4"], d["nsaF"] = gm, own, F
    return d


def shared_consts():
    d = {}
    E = np.zeros((32, 32, 128), np.float32)
    for n in range(32):
        E[n, n, :] = 1.0
    d["Emoba"] = E.astype(BF)
    A = np.zeros((128, 4, 128), np.float32)
    for nch in range(4):
        for n in range(128):
            ng = nch * 128 + n
            if ng > 510:
                continue
            for b in range(128):
                if 4 * b - 1 <= ng <= 4 * b + 3:
                    A[n, nch, b] = 1.0
    d["Aimp"] = A.astype(BF)
    d["ident"] = ident_np()
    d["ident4"] = np.ascontiguousarray(np.tile(np.eye(128, dtype=np.float32), (1, 4))).astype(BF)
    return d


def host_prep_B(l, inp):
    d = {}
    d["wout"] = np.ascontiguousarray(inp["w_out"][l].reshape(16, 128, 2048).transpose(1, 0, 2))
    wq = inp["mem_wq"][l]
    d["wqm"] = np.ascontiguousarray(wq.reshape(16, 128, 4, 128).transpose(2, 1, 0, 3))
    wkv = inp["mem_wkv"][l]
    d["wkm"] = np.ascontiguousarray(wkv[:, 0:512].reshape(16, 128, 4, 128).transpose(2, 1, 0, 3))
    d["wvm"] = np.ascontiguousarray(wkv[:, 512:1024].reshape(16, 128, 512).transpose(1, 0, 2))
    d["wom"] = np.ascontiguousarray(inp["mem_wo"][l].reshape(4, 128, 2048).transpose(1, 0, 2))
    gu = inp["ffn_w_gu"][l, 1]
    d["wgu1"] = np.ascontiguousarray(gu.reshape(16, 128, 88, 128).transpose(2, 1, 0, 3))
    d["wd1"] = np.ascontiguousarray(inp["ffn_w_down"][l, 1].reshape(44, 128, 2048))
    d["lng"] = np.ascontiguousarray(np.broadcast_to(inp["ln_g"][l][None], (128, 4, 2048)))
    d["lnb"] = np.ascontiguousarray(np.broadcast_to(inp["ln_b"][l][None], (128, 4, 2048)))
    d["cpe"] = np.ascontiguousarray(inp["nsa_cmp_pe"][l].transpose(0, 2, 1))
    d["cw1"] = np.ascontiguousarray(inp["nsa_cmp_w1"][l].reshape(2, 32, 128, 128).transpose(0, 2, 1, 3))
    d["cw2"] = np.ascontiguousarray(inp["nsa_cmp_w2"][l])
    d["mem"] = np.ascontiguousarray(inp["mem"][0])
    return d


def assemble_global(resA):
    fm = np.stack([r["fm"] for r in resA], 0)
    tm = np.stack([r["tm"] for r in resA], 0)
    kidx = list(range(FM_KA, FM_KA + 4)) + list(range(FM_KBN, FM_KBN + 4)) + [FM_KBR, FM_KCMP, FM_VCMP, FM_KSLC, FM_KWIN] + list(range(FM_KD, FM_KD + 4)) + [FM_IK]
    kg = fm[:, kidx].reshape(8, NKG, 128, 8, 128).transpose(1, 2, 3, 0, 4).reshape(NKG, 128, 8192)
    kg = np.ascontiguousarray(kg)
    tg = tm.reshape(8, 8, 128, NTM).transpose(1, 0, 2, 3).reshape(8192, NTM)
    one = np.ones((8192, 1), BF)

    def aug4(c0):
        v = tg[:, c0:c0 + 512].reshape(8192, 4, 128)
        return np.ascontiguousarray(np.concatenate([v, np.ones((8192, 4, 1), BF)], 2).reshape(8192, 516))

    def aug1(c0):
        return np.ascontiguousarray(np.concatenate([tg[:, c0:c0 + 128], one], 1))
    vs = dict(vA=aug4(TM_VA), vB=aug4(TM_VB), vD=aug4(TM_VD), vslc=aug1(TM_VSLC), vwin=aug1(TM_VWIN))
    qidx = list(range(FM_QA, FM_QA + 4)) + list(range(FM_QBN, FM_QBN + 4)) + [FM_QBR, FM_QBR + 1] + list(range(FM_QC, FM_QC + 4)) + \
        list(range(FM_QCR, FM_QCR + 4)) + list(range(FM_QD, FM_QD + 4)) + list(range(FM_IQ, FM_IQ + 8))
    qfs = [np.ascontiguousarray(fm[c][qidx]) for c in range(8)]
    return qfs, kg, vs


def emit_attn(k, R, nch, G, parts_fn, bias_fn, v_fn, W, scale):
    S, O, PT = R["S"], R["O"], R["PT"]
    for kc_ in range(nch):
        s = S[kc_ % 2]
        biases = bias_fn(kc_)
        first = True
        for (l, r) in biases:
            k.mm(s[:, 0:G * 128], l, r, start=first, stop=False)
            first = False
        for h in range(G):
            parts = parts_fn(kc_, h)
            for pi, (l, r) in enumerate(parts):
                k.mm(s[:, h * 128:(h + 1) * 128], l, r, start=(pi == 0 and not biases), stop=(pi == len(parts) - 1))
        pt = PT[kc_ % 3]
        k.act(pt[:, 0:G * 128], s[:, 0:G * 128], AF.Exp, scale=scale)
        for h in range(G):
            k.mm(O[h][:, 0:W], pt[:, h * 128:(h + 1) * 128], v_fn(kc_, h), start=(kc_ == 0), stop=(kc_ == nch - 1))


def emit_norm(k, R, G, out_fn, gate_fn=None, accumulate=False, first=False):
    O, rs = R["O"], R["rs"]
    for h in range(G):
        k.ts(rs[:, h:h + 1], O[h][:, 128:129], 1e-30, ALU.max)
        k.recip(rs[:, h:h + 1], rs[:, h:h + 1])


def load_v(k, tile, src, W):
    sv = src.rearrange("(ch p) w -> p ch w", p=128)
    for q in range(4):
        k.dma("sp", tile[:, q * 16:(q + 1) * 16, :], sv[:, q * 16:(q + 1) * 16, :])


def build_B(UPTO=9):
    nc = bass.Bass("TRN2", target_bir_lowering=False)
    dt = lambda name, shape, dtype, kind="ExternalInput": nc.dram_tensor(name, list(shape), dtype, kind=kind).ap()
    x1_in = dt("x1", [TOK, 2048], F32)
    qf = dt("qf", [NQF, 128, TOK], BF16)
    kg = dt("kg", [NKG, 128, 8192], BF16)
    vA_in, vB_in, vD_in = dt("vA", [8192, 516], BF16), dt("vB", [8192, 516], BF16), dt("vD", [8192, 516], BF16)
    vslc_in, vwin_in = dt("vslc", [8192, 129], BF16), dt("vwin", [8192, 129], BF16)
    tmf_in = dt("tmf", [TOK, 32], F32)
    mem_in = dt("mem", [256, 2048], F32)
    wout = dt("wout", [128, 16, 2048], F32)
    wqm, wkm = dt("wqm", [4, 128, 16, 128], F32), dt("wkm", [4, 128, 16, 128], F32)
    wvm, wom = dt("wvm", [128, 16, 512], F32), dt("wom", [128, 4, 2048], F32)
    wgu, wd = dt("wgu1", [88, 128, 16, 128], F32), dt("wd1", [44, 128, 2048], F32)
    lng_in, lnb_in = dt("lng", [128, 4, 2048], F32), dt("lnb", [128, 4, 2048], F32)
    cpe, cw1, cw2 = dt("cpe", [2, 128, 32], F32), dt("cw1", [2, 128, 32, 128], F32), dt("cw2", [2, 128, 128], F32)
    causalT4_in, winT4_in = dt("causalT4", [128, 8, 512], BF16), dt("winT4", [128, 12, 512], BF16)
    causalQ_in = dt("causalQ", [128, 8, 128], F32)
    cmpmask4_in = dt("cmpmask4", [128, 8, 4, 512], BF16)
    gm4_in, own4_in = dt("gm4", [128, 8, 4, 32], F32), dt("own4", [128, 8, 4, 32], F32)
    nsaF_in = dt("nsaF", [128, 8, 128], F32)
    Emoba_in, Aimp_in = dt("Emoba", [32, 32, 128], BF16), dt("Aimp", [128, 4, 128], BF16)
    id_in, id4_in = dt("ident", [128, 128], BF16), dt("ident4", [128, 512], BF16)
    x4_out = dt("x4", [TOK, 2048], F32, "ExternalOutput")
    ocat_out = dt("ocat", [TOK, 2048], BF16, "ExternalOutput")
    mqs = nc.dram_tensor("mqs", [8, 128, 8192], BF16, kind="Internal").ap()

    with ExitStack() as st:
        k = K(nc, st)
        R = {}
        R["ident"] = ident = k.sb([128, 128], BF16, "ident")
        P = R["P"] = [k.ps([128, 512], F32, "P%d" % i) for i in range(7)]
        ptr = k.ps([128, 1024], BF16, "ptr")
        attn_scope = k.scope()
        attn_scope.__enter__()
        ident4 = k.sb([128, 512], BF16, "ident4")
        causalT4 = k.sb([128, 8, 512], BF16, "causalT4")
        G_t = k.sb([128, 8, 32], F32, "gates")
        Ocat = k.sb([128, 8, 2048], BF16, "Ocat")
        R["rs"] = rs = k.sb([128, 16], F32, "rs")
        R["PT"] = [k.sb([128, 512], BF16, "PT%d" % i) for i in range(3)]
        R["ptr"] = [ptr, ptr]
        R["S"], R["O"] = P[0:2], P[2:6]
        PX = P[6]
        O = R["O"]
        Ocat_attn, rs_attn, G_attn = Ocat, rs, G_t
        k.dma("sp", ident.v, id_in)
        k.dma("sp", ident4.v, id4_in)
        k.dma("sp", causalT4.v, causalT4_in)
        k.dma("sp", G_t.v, tmf_in.rearrange("(j p) c -> p j c", p=128))
        js = lambda j: slice(j * 128, (j + 1) * 128)
        cs = lambda c: slice(c * 128, (c + 1) * 128)

        def causal_bias(j, kc_):
            return [(ident.v, causalT4[:, kc_ - 8 * j, :])] if kc_ >= 8 * j else []

        def finish(G, j, col0, gate_col=None, mode="set", dst=None, Ocat=None, rs=None, G_t=None):
            Ocat = Ocat if Ocat is not None else Ocat_attn
            rs = rs if rs is not None else rs_attn
            G_t = G_t if G_t is not None else G_attn
            for h in range(G):
                k.ts(rs[:, h:h + 1], O[h][:, 128:129], 1e-30, ALU.max)
                k.recip(rs[:, h:h + 1], rs[:, h:h + 1])
                if gate_col is not None:
                    k.tt(rs[:, h:h + 1], rs[:, h:h + 1], G_t[:, j, 3 * h + gate_col:3 * h + gate_col + 1], ALU.mult)
                if dst is None:
                    k.ts(Ocat[:, j, col0 + h * 128:col0 + (h + 1) * 128], O[h][:, 0:128], rs[:, h:h + 1], ALU.mult)
                elif mode == "set":
                    k.ts(dst[:, h * 128:(h + 1) * 128], O[h][:, 0:128], rs[:, h:h + 1], ALU.mult)
                else:
                    k.stt(dst[:, h * 128:(h + 1) * 128], O[h][:, 0:128], rs[:, h:h + 1], dst[:, h * 128:(h + 1) * 128], ALU.mult, ALU.add)

        if UPTO >= 1:
          with k.scope():
            KT = [k.sb([128, 8192], BF16, "aKT%d" % h) for h in range(4)]
            VA = k.sb([128, 64, 516], BF16, "aV")
            Em = k.sb([32, 32, 128], BF16, "Em")
            Q = [k.sb([128, TOK], BF16, "aQ%d" % h) for h in range(4)]
            gm4 = k.sb([128, 8, 4, 32], F32, "gm4")
            own4 = k.sb([128, 8, 4, 32], F32, "own4")
            kms = k.sb([128, 4, 32], F32, "kms")
            kmb = k.sb([128, 4, 32], BF16, "kmb")
            gate = k.sb([128, 4, 32], F32, "gate")
            sel = k.sb([128, 4, 32], F32, "sel")
            selb = k.sb([128, 4, 32], BF16, "selb")
            m8 = k.sb([128, 32], F32, "m8")
            BT = k.sb([32, 512], BF16, "BT")
            for h in range(4):
                k.dma("sp", KT[h].v, kg[KG_KA + h])
                k.dma("sp", Q[h].v, qf[QF_QA + h])
            load_v(k, VA, vA_in, 516)
            k.dma("sp", Em.v, Emoba_in)
            k.dma("sp", gm4.v, gm4_in)
            k.dma("sp", own4.v, own4_in)
            for h in range(4):
                k.reduce(kms[:, h, :], KT[h].v.re("p (n b) -> p n b", b=256), ALU.add)
            k.copy(kmb.v, kms.v)
            for j in range(8):
                for h in range(4):
                    k.mm(PX[:, h * 32:(h + 1) * 32], Q[h][:, js(j)], kmb[:, h, :])
                k.tt(gate.v, PX[:, 0:128].re("p (h n) -> p h n", h=4), gm4[:, j, :, :], ALU.add)
                for h in range(4):
                    k.max8(m8[:, h * 8:(h + 1) * 8], gate[:, h, :])
                for h in range(4):
                    k.ts(sel[:, h, :], gate[:, h, :], m8[:, h * 8 + 2:h * 8 + 3], ALU.is_ge)
                k.tt(sel.v, sel.v, own4[:, j, :, :], ALU.max)
                k.ts(selb.v, sel.v, 1.0, ALU.subtract, 32768.0, ALU.mult)
                for h in range(4):
                    k.tr(ptr[0:32, h * 128:(h + 1) * 128], selb[:, h, :], ident.v)
                k.copy(BT.v, ptr[0:32, 0:512])
                emit_attn(k, R, 8 * j + 8, 4,
                          lambda kc_, h: [(KT[h][:, cs(kc_)], Q[h][:, js(j)])],
                          lambda kc_: [(Em[:, kc_ // 2, :], BT.v)] + causal_bias(j, kc_),
                          lambda kc_, h: VA[:, kc_, h * 129:(h + 1) * 129], 129, 128 ** -0.5)
                finish(4, j, 0)
        if UPTO >= 2:
          with k.scope():
            KT = [k.sb([128, 8192], BF16, "bKT%d" % h) for h in range(4)]
            KR = k.sb([128, 8192], BF16, "bKR")
            VB = k.sb([128, 64, 516], BF16, "bV")
            Qn = [k.sb([128, TOK], BF16, "bQn%d" % h) for h in range(4)]
            Qr = [k.sb([128, TOK], BF16, "bQr%d" % h) for h in range(2)]
            for h in range(4):
                k.dma("sp", KT[h].v, kg[KG_KBN + h])
                k.dma("sp", Qn[h].v, qf[QF_QBN + h])
            for h in range(2):
                k.dma("sp", Qr[h].v, qf[QF_QBR + h])
            k.dma("sp", KR.v, kg[KG_KBR])
            load_v(k, VB, vB_in, 516)
            for j in range(8):
                def parts(kc_, h, j=j):
                    hs = slice((h % 2) * 64, (h % 2) * 64 + 64)
                    return [(KT[h][:, cs(kc_)], Qn[h][:, js(j)]), (KR[hs, cs(kc_)], Qr[h // 2][hs, js(j)])]
                emit_attn(k, R, 8 * j + 8, 4, parts, lambda kc_: causal_bias(j, kc_),
                          lambda kc_, h: VB[:, kc_, h * 129:(h + 1) * 129], 129, 192 ** -0.5)
                finish(4, j, 512)
        if UPTO >= 3:
          with k.scope():
            KC = k.sb([128, 512], BF16, "KC")
            VC = k.sb([128, 4, 257], BF16, "VC")
            with k.scope():
                kcT = k.sb([128, 8192], BF16, "kcT")
                vcT = k.sb([128, 8192], BF16, "vcT")
                k.dma("sp", kcT.v, kg[KG_KCMP])
                k.dma("sp", vcT.v, kg[KG_VCMP])
                W1 = [k.sb([128, 32, 128], BF16, "cW1_%d" % i) for i in range(2)]
                W2 = [k.sb([128, 128], BF16, "cW2_%d" % i) for i in range(2)]
                PE_ = [k.sb([128, 32], BF16, "cpe%d" % i) for i in range(2)]
                Aimp = k.sb([128, 4, 128], BF16, "Aimp")
                k.dma("sp", Aimp.v, Aimp_in)
                pb = k.sb([128, 1], F32, "pb")
                xh = k.sb([128, 512], F32, "xh")
                uh = k.sb([128, 512], F32, "uh")
                hg = k.sb([128, 512], BF16, "hg")
                for i in range(2):
                    k.dma("pool", W1[i].v, cw1[i])
                    k.dma("pool", W2[i].v, cw2[i])
                    k.dma("pool", PE_[i].v, cpe[i])
                for i, src in ((0, kcT), (1, vcT)):
                    sv = src.v.re("p (n s) -> p n s", s=16)
                    H = P[0]
                    for jj in range(32):
                        k.mm(H[:, 0:511], W1[i][:, jj, :], sv[:, (jj // 16):(jj // 16) + 511, jj % 16], start=(jj == 0), stop=(jj == 31))
                    for jj in range(32):
                        k.mm(PX[:, 0:1], W1[i][:, jj, :], PE_[i][:, jj:jj + 1], start=(jj == 0), stop=(jj == 31))
                    k.copy(pb.v, PX[:, 0:1])
                    k.memset(xh.v, 0.0)
                    k.ts(xh[:, 0:511], H[:, 0:511], pb[:, 0:1], ALU.add)
                    k.tt(uh.v, xh.v, xh.v, ALU.mult)
                    k.ts(uh.v, uh.v, 0.044715, ALU.mult, 1.0, ALU.add)
                    k.tt(uh.v, uh.v, xh.v, ALU.mult)
                    k.act(uh.v, uh.v, AF.Sigmoid, scale=1.5957691216057308)
                    k.tt(hg.v, xh.v, uh.v, ALU.mult)
                    if i == 0:
                        k.mm(P[1].v, W2[0].v, hg.v)
                        k.copy(KC.v, P[1].v, eng="act")
                    else:
                        for nch_ in range(4):
                            k.mm(P[1][:, nch_ * 128:(nch_ + 1) * 128], hg[:, cs(nch_)], W2[1].v)
                        k.copy(VC[:, :, 0:128], P[1].v.re("p (c d) -> p c d", c=4), eng="act")
                        k.memset(VC[:, :, 128:129], 1.0)
                        k.copy(VC[:, :, 129:257], Aimp.v)
            KS = k.sb([128, 8192], BF16, "KS")
            KW = k.sb([128, 8192], BF16, "KW")
            VS = k.sb([128, 64, 129], BF16, "VS")
            VW = k.sb([128, 64, 129], BF16, "VW")
            QC = [k.sb([128, TOK], BF16, "QC%d" % h) for h in range(4)]
            QR = [k.sb([128, TOK], BF16, "QR%d" % h) for h in range(4)]
            cmpmask4 = k.sb([128, 8, 4, 512], BF16, "cmpmask4")
            winT4 = k.sb([128, 12, 512], BF16, "winT4")
            nsaF = k.sb([128, 8, 128], F32, "nsaF")
            Mq = k.sb([128, 8192], BF16, "Mq")
            imp = k.sb([128, 128], F32, "imp")
            wk = k.sb([128, 128], F32, "wk")
            selq = k.sb([128, 128], F32, "selq")
            selqb = k.sb([128, 128], BF16, "selqb")
            m8 = k.sb([128, 16], F32, "m8n")
            acc = k.sb([128, 512], F32, "acc")
            k.dma("sp", KS.v, kg[KG_KSLC])
            k.dma("sp", KW.v, kg[KG_KWIN])
            load_v(k, VS, vslc_in, 129)
            load_v(k, VW, vwin_in, 129)
            for h in range(4):
                k.dma("sp", QC[h].v, qf[QF_QC + h])
                k.dma("sp", QR[h].v, qf[QF_QCR + h])
            k.dma("sp", cmpmask4.v, cmpmask4_in)
            k.dma("sp", winT4.v, winT4_in)
            k.dma("sp", nsaF.v, nsaF_in)
            for j in range(8):
                emit_attn(k, R, 4, 4, lambda kc_, h: [(KC[:, cs(kc_)], QC[h][:, js(j)])],
                          lambda kc_: [(ident.v, cmpmask4[:, j, kc_, :])],
                          lambda kc_, h: VC[:, kc_, :], 257, 128 ** -0.5)
                finish(4, j, 0, gate_col=None, mode="set", dst=None) if False else None
                for h in range(4):
                    k.ts(rs[:, h:h + 1], O[h][:, 128:129], 1e-30, ALU.max)
                    k.recip(rs[:, h:h + 1], rs[:, h:h + 1])
                    if h == 0:
                        k.ts(imp.v, O[h][:, 129:257], rs[:, h:h + 1], ALU.mult)
                    else:
                        k.stt(imp.v, O[h][:, 129:257], rs[:, h:h + 1], imp.v, ALU.mult, ALU.add)
                    k.tt(rs[:, 8 + h:9 + h], rs[:, h:h + 1], G_t[:, j, 3 * h:3 * h + 1], ALU.mult)
                    k.ts(acc[:, cs(h)], O[h][:, 0:128], rs[:, 8 + h:9 + h], ALU.mult)
                k.tt(imp.v, imp.v, nsaF[:, j, :], ALU.add)
                k.max8(m8[:, 0:8], imp.v)
                k.mrep(wk.v, m8[:, 0:8], imp.v, -3.0e4)
                k.max8(m8[:, 8:16], wk.v)
                k.ts(selq.v, imp.v, m8[:, 15:16], ALU.is_ge)
                k.ts(selqb.v, selq.v, 1.0, ALU.subtract, 32768.0, ALU.mult)
                nb = 2 * (8 * j + 8)
                k.copy(Mq[:, 0:nb * 64].re("p (b s) -> p b s", s=64), selqb[:, 0:nb].re("p (b o) -> p b o", o=1).bc([128, nb, 64]))
                emit_attn(k, R, 8 * j + 8, 4, lambda kc_, h: [(KS[:, cs(kc_)], QR[h][:, js(j)])],
                          lambda kc_: [(Mq[:, cs(kc_)], ident4.v)] + causal_bias(j, kc_),
                          lambda kc_, h: VS[:, kc_, :], 129, 128 ** -0.5)
                finish(4, j, 0, gate_col=1, mode="add", dst=acc)
                mps = [mp for mp in range(12) if 8 * j - 4 + mp >= 0]
                emit_attn(k, R, len(mps), 4, lambda ii, h: [(KW[:, cs(8 * j - 4 + mps[ii])], QR[h][:, js(j)])],
                          lambda ii: [(ident.v, winT4[:, mps[ii], :])],
                          lambda ii, h: VW[:, 8 * j - 4 + mps[ii], :], 129, 128 ** -0.5)
                finish(4, j, 0, gate_col=2, mode="add", dst=acc)
                k.copy(Ocat[:, j, 1024:1536], acc.v, eng="act")
        if UPTO >= 4:
          MQS = [Tl(mqs[j], "mqs%d" % j) for j in range(8)]
          with k.scope():
            IK = k.sb([128, 8192], BF16, "IK")
            IQ = [k.sb([128, TOK], BF16, "IQ%d" % i) for i in range(8)]
            causalQ = k.sb([128, 8, 128], F32, "causalQ")
            SC = [k.sb([128, 8192], F32, "SC%d" % i) for i in range(2)]
            MqD = [k.sb([128, 8192], BF16, "MqD%d" % i) for i in range(2)]
            Rr = [k.sb([128, 512], F32, "Rr%d" % i) for i in range(3)]
            m8 = k.sb([128, 8], F32, "m8d")
            k.dma("sp", IK.v, kg[KG_IK])
            for i in range(8):
                k.dma("sp", IQ[i].v, qf[QF_IQ + i])
            k.dma("sp", causalQ.v, causalQ_in)
            nr = 0
            for j in range(8):
                sc = SC[j % 2]
                Kc = (8 * j + 8) * 128
                for blk in range(Kc // 512):
                    bs = slice(blk * 512, (blk + 1) * 512)
                    for h in range(16):
                        hs = slice((h % 2) * 64, (h % 2) * 64 + 64)
                        s = R["S"][h % 2]
                        k.mm(s.v, IQ[h // 2][hs, js(j)], IK[hs, bs])
                        rr = Rr[nr % 3]
                        nr += 1
                        k.act(rr.v, s.v, AF.Relu)
                        wv = G_t[:, j, 12 + h:13 + h]
                        if h == 0:
                            k.ts(sc[:, bs], rr.v, wv, ALU.mult)
                        else:
                            k.stt(sc[:, bs], rr.v, wv, sc[:, bs], ALU.mult, ALU.add)
                k.tt(sc[:, (8 * j) * 128:Kc].re("p (m k) -> p m k", m=8), sc[:, (8 * j) * 128:Kc].re("p (m k) -> p m k", m=8), causalQ.v, ALU.add)
                for it in range(32):
                    k.max8(m8.v, sc[:, 0:Kc])
                    k.mrep(sc[:, 0:Kc], m8.v, sc[:, 0:Kc], IMM)
                mq = MqD[j % 2]
                k.ts(mq[:, 0:Kc], sc[:, 0:Kc], IMM, ALU.not_equal, NEG, ALU.mult)
                k.dma("sp", V(MQS[j], mqs[j][:, 0:Kc]), mq[:, 0:Kc])
          with k.scope():
            KT = [k.sb([128, 8192], BF16, "dKT%d" % h) for h in range(4)]
            VD = k.sb([128, 64, 516], BF16, "dV")
            Q = [k.sb([128, TOK], BF16, "dQ%d" % h) for h in range(4)]
            MqL = [k.sb([128, 8192], BF16, "MqL%d" % i) for i in range(1)]
            for h in range(4):
                k.dma("sp", KT[h].v, kg[KG_KD + h])
                k.dma("sp", Q[h].v, qf[QF_QD + h])
            load_v(k, VD, vD_in, 516)
            for j in range(8):
                Kc = (8 * j + 8) * 128
                mq = MqL[0]
                k.dma("sp", mq[:, 0:Kc], V(MQS[j], mqs[j][:, 0:Kc]))
                emit_attn(k, R, 8 * j + 8, 4, lambda kc_, h: [(KT[h][:, cs(kc_)], Q[h][:, js(j)])],
                          lambda kc_: [(mq[:, cs(kc_)], ident4.v)] + causal_bias(j, kc_),
                          lambda kc_, h: VD[:, kc_, h * 129:(h + 1) * 129], 129, 128 ** -0.5)
                finish(4, j, 1536)
        OCD = [Tl(ocat_out[js(j), :], "ocd%d" % j) for j in range(8)]
        for j in range(8):
            k.dma("sp", V(OCD[j], ocat_out[js(j), :]), Ocat[:, j, :], is_output=True)
        attn_scope.__exit__(None, None, None)
        if UPTO < 5:
            return nc
        R["XT"] = XT = k.sb([128, 16, TOK], BF16, "XT")
        R["xb"] = k.sb([128, 2048], BF16, "xb")
        R["junk"] = R["xb"]
        R["sm"] = k.sb([128, 8], F32, "sm")
        lng = k.sb([128, 2048], F32, "lng")
        lnb = k.sb([128, 2048], F32, "lnb")
        X = R["X"] = [k.sb([128, 2048], F32, "X%d" % j) for j in range(NJ)]
        PD = P[4:6]
        for j in range(NJ):
            k.dma("sp", X[j].v, x1_in[js(j), :])
            k.ts(X[j].v, X[j].v, ALPHA, ALU.mult, eng="pool")
        with k.scope():
            OCL = [k.sb([128, 2048], BF16, "OCL%d" % i) for i in range(2)]
            for j in range(NJ):
                ocl = OCL[j % 2]
                k.dma("sp", ocl.v, V(OCD[j], ocat_out[js(j), :]))
                for g in range(4):
                    for q in range(4):
                        kc = g * 4 + q
                        k.tr(ptr[:, q * 128:(q + 1) * 128], ocl[:, cs(kc)], ident.v)
                    k.copy(XT[:, g * 4:(g + 1) * 4, js(j)], ptr[:, 0:512].re("p (q n) -> p q n", q=4), eng="dve" if g % 2 == 0 else "act")
            WO = [k.sb([128, 16, 512], BF16, "WO%d" % i) for i in range(2)]
            for n in range(4):
                w = WO[n % 2]
                k.dma("pool", w.v, wout[:, :, n * 512:(n + 1) * 512])
                for j in range(NJ):
                    pd = PD[j % 2]
                    for kc in range(16):
                        k.mm(pd.v, XT[:, kc, js(j)], w[:, kc, :], start=(kc == 0), stop=(kc == 15))
                    xs = X[j][:, n * 512:(n + 1) * 512]
                    k.tt(xs, pd.v, xs, ALU.add)
            k.dma("sp", lng.v, lng_in[:, 1, :])
            k.dma("sp", lnb.v, lnb_in[:, 1, :])
            for j in range(NJ):
                emit_layernorm(k, X[j], lng.v, lnb.v, R["sm"], R["junk"])
        with k.scope():
            R["PT"] = [k.sb([128, 512], BF16, "PTm%d" % i) for i in range(3)]
            rs_m = k.sb([128, 16], F32, "rs_m")
            OM = k.sb([128, 1, 512], BF16, "OM")
            QM = [k.sb([128, TOK], BF16, "QM%d" % h) for h in range(4)]
            KM = [k.sb([128, 256], BF16, "KM%d" % h) for h in range(4)]
            VM = k.sb([128, 2, 516], BF16, "VM")
            projscope = k.scope()
            projscope.__enter__()
            MT = k.sb([128, 16, 256], BF16, "MT")
            mf = k.sb([128, 2048], F32, "mf")
            for mc in range(2):
                k.dma("sp", mf.v, mem_in[mc * 128:(mc + 1) * 128, :])
                k.copy(R["xb"].v, mf.v, eng="act")
                for g in range(4):
                    for q in range(4):
                        kc = g * 4 + q
                        k.tr(ptr[:, q * 128:(q + 1) * 128], R["xb"][:, cs(kc)], ident.v)
                    k.copy(MT[:, g * 4:(g + 1) * 4, mc * 128:(mc + 1) * 128], ptr[:, 0:512].re("p (q n) -> p q n", q=4))
            emit_transposeX(k, X, XT, R["xb"], R["ptr"], ident)
            for j in range(NJ):
                k.ts(X[j].v, X[j].v, ALPHA, ALU.mult, eng="pool")
            Wm = [k.sb([128, 16, 128], BF16, "Wm%d" % i) for i in range(2)]
            Wv = k.sb([128, 16, 512], BF16, "Wvm")
            k.dma("pool", Wv.v, wvm)
            nw = 0
            for h in range(4):
                w = Wm[nw % 2]; nw += 1
                k.dma("pool", w.v, wqm[h])
                for half in range(2):
                    ts_ = slice(half * 512, (half + 1) * 512)
                    for kc in range(16):
                        k.mm(P[half].v, w[:, kc, :], XT[:, kc, ts_], start=(kc == 0), stop=(kc == 15))
                    k.copy(QM[h][:, ts_], P[half].v, eng="act")
                w = Wm[nw % 2]; nw += 1
                k.dma("pool", w.v, wkm[h])
                for kc in range(16):
                    k.mm(PX[:, 0:256], w[:, kc, :], MT[:, kc, :], start=(kc == 0), stop=(kc == 15))
                k.copy(KM[h].v, PX[:, 0:256])
            k.memset(VM.v, 1.0)
            for mc in range(2):
                for kc in range(16):
                    k.mm(PX.v, MT[:, kc, mc * 128:(mc + 1) * 128], Wv[:, kc, :], start=(kc == 0), stop=(kc == 15))
                k.copy(VM[:, mc, :].re("p (h d) -> p h d", h=4)[:, :, 0:128], PX.v.re("p (h d) -> p h d", h=4))
            projscope.__exit__(None, None, None)
            Wo = k.sb([128, 4, 2048], BF16, "Wom")
            k.dma("pool", Wo.v, wom)
            OMT = k.sb([128, 4, TOK], BF16, "OMT")
            for j in range(NJ):
                emit_attn(k, R, 2, 4, lambda kc_, h: [(KM[h][:, cs(kc_)], QM[h][:, js(j)])], lambda kc_: [],
                          lambda kc_, h: VM[:, kc_, h * 129:(h + 1) * 129], 129, 128 ** -0.5)
                finish(4, 0, 0, Ocat=OM, rs=rs_m)
                for q in range(4):
                    k.tr(ptr[:, q * 128:(q + 1) * 128], OM[:, 0, cs(q)], ident.v)
                k.copy(OMT[:, :, js(j)], ptr[:, 0:512].re("p (q n) -> p q n", q=4))
            for n in range(4):
                for j in range(NJ):
                    pd = PD[j % 2]
                    for kc in range(4):
                        k.mm(pd.v, OMT[:, kc, js(j)], Wo[:, kc, n * 512:(n + 1) * 512], start=(kc == 0), stop=(kc == 3))
                    xs = X[j][:, n * 512:(n + 1) * 512]
                    k.tt(xs, pd.v, xs, ALU.add)
            k.dma("sp", lng.v, lng_in[:, 2, :])
            k.dma("sp", lnb.v, lnb_in[:, 2, :])
            for j in range(NJ):
                emit_layernorm(k, X[j], lng.v, lnb.v, R["sm"], R["junk"])
        emit_transposeX(k, X, XT, R["xb"], R["ptr"], ident)
        for j in range(NJ):
            k.ts(X[j].v, X[j].v, ALPHA, ALU.mult, eng="pool")
        with k.scope():
            alloc_ffn(k, R)
            emit_ffn(k, X, XT, wgu, wd, R)
        k.dma("sp", lng.v, lng_in[:, 3, :])
        k.dma("sp", lnb.v, lnb_in[:, 3, :])
        for j in range(NJ):
            emit_layernorm(k, X[j], lng.v, lnb.v, R["sm"], R["junk"])
            k.dma("sp", x4_out[js(j), :], X[j].v, is_output=True)
        k.emit()
    return nc


_PROGS = {}


def _prog(name):
    if name not in _PROGS:
        _PROGS[name] = build_A() if name == "A" else build_B()
    return _PROGS[name]


def kernel(**inputs):
    inp = {k_: np.asarray(v_) for k_, v_ in inputs.items()}
    x = inp["x"][0]
    pos = inp["positions"][0].astype(np.int32)
    rows = [np.concatenate([np.arange((8 * j + c) * 128, (8 * j + c + 1) * 128) for j in range(8)]) for c in range(8)]
    xs = [np.ascontiguousarray(x[rows[c]]) for c in range(8)]
    posb = [np.ascontiguousarray(np.broadcast_to(pos[rows[c]][None], (128, 1024))).astype(np.int32) for c in range(8)]
    rc, idn = rope_consts(), ident_np()
    sc = shared_consts()
    cms = [core_masks(c) for c in range(8)]
    for l in range(4):
        dA = host_prep_A(l, inp)
        wA = {k_: dA[k_] for k_ in ("wgu0", "wd0", "wfm", "wtm1", "wtm2", "wuq", "wuk", "wuv", "gq", "gkv", "lng", "lnb")}
        mapsA = [dict(x_in=xs[c], pos=posb[c], ropec=rc, ident=idn, **wA) for c in range(8)]
        resA = _bu.run_bass_kernel_spmd(_prog("A"), mapsA, core_ids=list(range(8))).results
        del mapsA, wA
        qfs, kg, vs = assemble_global(resA)
        dB = host_prep_B(l, inp)
        del dA
        mapsB = [dict(x1=resA[c]["x1"], qf=qfs[c], kg=kg, tmf=resA[c]["tmf"], **vs, **dB, **sc, **cms[c]) for c in range(8)]
        resB = _bu.run_bass_kernel_spmd(_prog("B"), mapsB, core_ids=list(range(8))).results
        del mapsB, dB
        xs = [np.ascontiguousarray(resB[c]["x4"]) for c in range(8)]
    out = np.zeros((8192, 2048), np.float32)
    for c in range(8):
        out[rows[c]] = xs[c]
    return out[None]
```

```python
import numpy as np
from contextlib import ExitStack
import concourse.bass as bass
import concourse.mybir as mybir
import concourse.bass_utils as _bu

F32 = mybir.dt.float32
BF16 = mybir.dt.bfloat16
I32 = mybir.dt.int32
ALU = mybir.AluOpType
AF = mybir.ActivationFunctionType
AX = mybir.AxisListType


class Tl:
    __slots__ = ("h", "name", "w", "r", "sem", "dcount", "psum")

    def __init__(self, h, name):
        self.h = h
        self.name = name
        self.w = None
        self.r = {}
        self.sem = None
        self.dcount = 0
        self.psum = False

    def __getitem__(self, idx):
        return V(self, self.h[idx])

    @property
    def v(self):
        return V(self, self.h[:])


class V:
    __slots__ = ("t", "ap")

    def __init__(self, t, ap):
        self.t = t
        self.ap = ap

    def __getitem__(self, idx):
        return V(self.t, self.ap[idx])

    def re(self, pat, **kw):
        return V(self.t, self.ap.rearrange(pat, **kw))

    def bc(self, shape):
        return V(self.t, self.ap.to_broadcast(shape))


def _ap(x):
    return x.ap if isinstance(x, V) else x


class K:
    ENG = ("pe", "act", "dve", "pool", "sp")

    def __init__(self, nc, stack):
        self.nc = nc
        self.stack = stack
        self.q = {e: [] for e in self.ENG}
        self.cnt = {e: 0 for e in self.ENG}
        self.sem = {e: stack.enter_context(nc.semaphore("s_" + e)) for e in self.ENG}
        self.known = {e: {} for e in self.ENG}
        self.out_waits = []
        self.ntile = 0
        self.outer = stack
        self.sempool = []
        self.scoped = []

    def sb(self, shape, dt, name=None):
        self.ntile += 1
        name = "sb_%s_%d" % (name or "t", self.ntile)
        h = self.stack.enter_context(self.nc.sbuf_tensor(name, list(shape), dt))
        t = Tl(h, name)
        if self.scoped:
            self.scoped[-1].append(t)
        return t

    def ps(self, shape, dt=F32, name=None):
        self.ntile += 1
        name = "ps_%s_%d" % (name or "p", self.ntile)
        h = self.stack.enter_context(self.nc.psum_tensor(name, list(shape), dt))
        t = Tl(h, name)
        t.psum = True
        if self.scoped:
            self.scoped[-1].append(t)
        return t

    def _deps(self, eng, reads, writes):
        deps = []
        for t in reads:
            if t.w is not None:
                deps.append(t.w)
        for t in writes:
            if t.w is not None:
                deps.append(t.w)
            deps.extend(t.r.values())
        waits = []
        kn = self.known[eng]
        for d in deps:
            if d[0] == "eng":
                _, e2, idx = d
                if e2 == eng and eng in ("pe", "sp"):
                    continue
                key = e2
                if kn.get(key, 0) >= idx:
                    continue
                kn[key] = idx
                waits.append((self.sem[e2], idx))
            else:
                _, t, c = d
                key = t.name
                if kn.get(key, 0) >= c:
                    continue
                kn[key] = c
                waits.append((t.sem, 16 * c))
        return waits

    def op(self, eng, fn, reads, writes):
        reads = [x.t if isinstance(x, V) else x for x in reads if x is not None and not isinstance(x, (int, float))]
        writes = [x.t if isinstance(x, V) else x for x in writes]
        reads = [t for t in reads if isinstance(t, Tl)]
        writes = writes + [t for t in reads if t.psum]
        reads = [t for t in reads if not t.psum]
        waits = self._deps(eng, reads, writes)
        self.cnt[eng] += 1
        idx = self.cnt[eng]
        self.q[eng].append((waits, fn, True))
        for t in reads:
            t.r[eng] = ("eng", eng, idx)
        for t in writes:
            t.w = ("eng", eng, idx)
            t.r = {}

    def uid(self):
        self.ntile += 1
        return self.ntile

    def dma(self, qeng, out, in_, is_output=False, chain=False):
        reads = [in_.t] if isinstance(in_, V) else []
        writes = [out.t] if isinstance(out, V) else []
        saved = None
        if chain and writes and writes[0].w is not None and writes[0].w[0] == "dma" and writes[0].w[1] is writes[0]:
            saved = writes[0].w
            writes[0].w = None
        waits = self._deps(qeng, reads, writes)
        if saved is not None:
            writes[0].w = saved
        t = (writes or reads)[0]
        if t.sem is None:
            if self.sempool:
                t.sem, t.dcount = self.sempool.pop()
            else:
                t.sem = self.outer.enter_context(self.nc.semaphore("d_" + t.name))
        t.dcount += 1
        c = t.dcount
        sem = t.sem
        o, i = _ap(out), _ap(in_)
        self.q[qeng].append((waits, lambda e: e.dma_start(out=o, in_=i), ("dma", sem)))
        for x in writes:
            x.w = ("dma", t, c)
            x.r = {}
        for x in reads:
            x.r["dma"] = ("dma", t, c)
        if is_output:
            self.out_waits.append((sem, 16 * c))

    def mm(self, out, lhsT, rhs, start=True, stop=True):
        o, l, r = _ap(out), _ap(lhsT), _ap(rhs)
        self.op("pe", lambda e: e.matmul(o, l, r, start=start, stop=stop), [lhsT, rhs], [out])

    def tr(self, out, in_, ident):
        o, i, d = _ap(out), _ap(in_), _ap(ident)
        self.op("pe", lambda e: e.transpose(o, i, d), [in_, ident], [out])

    def act(self, out, in_, func, bias=None, scale=None, accum_out=None, eng="act"):
        o, i = _ap(out), _ap(in_)
        kw = {}
        if bias is not None:
            kw["bias"] = _ap(bias)
        if scale is not None:
            kw["scale"] = _ap(scale)
        if accum_out is not None:
            kw["accum_out"] = _ap(accum_out)
        w = [out] + ([accum_out] if accum_out is not None else [])
        self.op(eng, lambda e: e.activation(o, i, func, **kw), [in_, bias, scale], w)

    def tt(self, out, in0, in1, op, eng="dve"):
        o, a, b = _ap(out), _ap(in0), _ap(in1)
        self.op(eng, lambda e: e.tensor_tensor(o, a, b, op), [in0, in1], [out])

    def ts(self, out, in0, s1, op0, s2=None, op1=None, accum_out=None, eng="dve"):
        o, a = _ap(out), _ap(in0)
        x1, x2 = _ap(s1), _ap(s2)
        kw = {}
        if op1 is not None:
            kw["op1"] = op1
        if accum_out is not None:
            kw["accum_out"] = _ap(accum_out)
        w = [out] + ([accum_out] if accum_out is not None else [])
        self.op(eng, lambda e: e.tensor_scalar(o, a, x1, x2, op0, **kw), [in0, s1, s2], w)

    def stt(self, out, in0, scalar, in1, op0, op1, eng="dve"):
        o, a, s, b = _ap(out), _ap(in0), _ap(scalar), _ap(in1)
        self.op(eng, lambda e: e.scalar_tensor_tensor(o, a, s, b, op0, op1), [in0, scalar, in1], [out])

    def copy(self, out, in_, eng="dve"):
        o, i = _ap(out), _ap(in_)
        if eng == "act":
            self.op(eng, lambda e: e.copy(o, i), [in_], [out])
        else:
            self.op(eng, lambda e: e.tensor_copy(o, i), [in_], [out])

    def memset(self, out, val, eng="dve"):
        o = _ap(out)
        self.op(eng, lambda e: e.memset(o, val), [], [out])

    def reduce(self, out, in_, op, axis=AX.X, eng="dve"):
        o, i = _ap(out), _ap(in_)
        self.op(eng, lambda e: e.tensor_reduce(o, i, axis, op), [in_], [out])

    def max8(self, out, in_):
        o, i = _ap(out), _ap(in_)
        self.op("dve", lambda e: e.max(o, i), [in_], [out])

    def mrep(self, out, rep, vals, imm):
        o, r, v = _ap(out), _ap(rep), _ap(vals)
        self.op("dve", lambda e: e.match_replace(o, r, v, imm), [rep, vals], [out])

    def recip(self, out, in_):
        o, i = _ap(out), _ap(in_)
        self.op("dve", lambda e: e.reciprocal(o, i), [in_], [out])

    def scope(self):
        k = self

        class _S:
            def __enter__(self_):
                self_.prev = k.stack
                self_.st = ExitStack()
                self_.st.__enter__()
                k.stack = self_.st
                k.scoped.append([])
                return self_

            def __exit__(self_, *a):
                tiles = k.scoped.pop()
                extra = [(t.sem, 16 * t.dcount) for t in tiles if t.sem is not None]
                for t in tiles:
                    d_ = t.r.get("dma")
                    if d_ is not None:
                        extra.append((d_[1].sem, 16 * d_[2]))
                k.emit(extra)
                for t in tiles:
                    if t.sem is not None:
                        k.sempool.append((t.sem, t.dcount))
                        t.sem = None
                k.stack = self_.prev
                self_.st.__exit__(None, None, None)
                return False
        return _S()

    def emit(self, extra=()):
        nc = self.nc
        fin = list(self.out_waits) + list(extra)
        self.out_waits = []
        q = self.q
        self.q = {e: [] for e in self.ENG}
        sems = self.sem
        self.cnt["sp"] += 1
        cnt = dict(self.cnt)
        for e_ in self.ENG:
            for e2 in self.ENG:
                self.known[e_][e2] = max(self.known[e_].get(e2, 0), cnt[e2])

        def replay(name, e):
            for waits, fn, inc in q[name]:
                for s, v in waits:
                    e.wait_ge(s, v)
                ins = fn(e)
                if inc is True:
                    ins.then_inc(sems[name], 1)
                elif inc is not None:
                    ins.then_inc(inc[1], 16)
            if name == "sp":
                for s, v in fin:
                    e.wait_ge(s, v)
                e.sem_inc(sems["sp"], 1)
            for e2 in ("pe", "act", "dve", "pool", "sp"):
                if e2 != name and cnt[e2] > 0:
                    e.wait_ge(sems[e2], cnt[e2])

        with nc.Block() as block:
            @block.tensor
            def _(e):
                replay("pe", e)

            @block.scalar
            def _(e):
                replay("act", e)

            @block.vector
            def _(e):
                replay("dve", e)

            @block.gpsimd
            def _(e):
                replay("pool", e)

            @block.sync
            def _(e):
                replay("sp", e)

import math

ALPHA = 8 ** 0.25
TOK = 1024
NJ = 8
TWO_PI = 2 * math.pi

FM_QA, FM_KA, FM_QBN, FM_QBR, FM_KBN, FM_KBR = 0, 4, 8, 12, 14, 18
FM_QC, FM_QCR, FM_KCMP, FM_VCMP, FM_KSLC, FM_KWIN = 19, 23, 27, 28, 29, 30
FM_QD, FM_KD, FM_IQ, FM_IK = 31, 35, 39, 47
NFM = 48
TM_VA, TM_VB, TM_VSLC, TM_VWIN, TM_VD, NTM = 0, 512, 1024, 1152, 1280, 1792


def _swap(cols, half):
    c = np.asarray(cols).reshape(-1, 2, half)
    return c[:, ::-1, :].reshape(-1)


def fm_entries():
    hd = lambda base, h: np.arange(base + h * 128, base + (h + 1) * 128)
    E = []
    for h in range(4):
        E.append(dict(cols=hd(0, h), rope=128, dst=FM_QA + h))
    for h in range(4):
        E.append(dict(cols=hd(512, h), rope=128, dst=FM_KA + h))
    for fc in range(4):
        E.append(dict(cols=hd(1536, fc), rope=None, dst=("cq", fc)))
    for fc in range(4):
        E.append(dict(cols=hd(2048, fc), rope=None, dst=("ckv", fc)))
    kr = np.arange(2560, 2624)
    E.append(dict(cols=np.concatenate([kr, kr]), rope=64, dst=FM_KBR))
    for h in range(4):
        E.append(dict(cols=hd(2624, h), rope=128, dst=FM_QCR + h, raw=FM_QC + h))
    E.append(dict(cols=hd(3136, 0), rope=None, dst=FM_KCMP))
    E.append(dict(cols=hd(3136, 1), rope=None, dst=FM_VCMP))
    E.append(dict(cols=hd(3136, 2), rope=128, dst=FM_KSLC))
    E.append(dict(cols=hd(3136, 4), rope=128, dst=FM_KWIN))
    for h in range(4):
        E.append(dict(cols=hd(3916, h), rope=128, dst=FM_QD + h))
    for h in range(4):
        E.append(dict(cols=hd(4428, h), rope=128, dst=FM_KD + h))
    for j in range(8):
        E.append(dict(cols=hd(5452, j), rope=64, dst=FM_IQ + j))
    ik = np.arange(6476, 6540)
    E.append(dict(cols=np.concatenate([ik, ik]), rope=64, dst=FM_IK))
    n = 0
    for e in E:
        e["w1"] = n
        n += 1
        if e["rope"]:
            e["w2"] = n
            n += 1
    return E, n


def host_prep_A(l, inp):
    E, nw = fm_entries()
    w_in = inp["w_in"][l]
    cols = []
    for e in E:
        cols.append(e["cols"])
        if e["rope"]:
            cols.append(_swap(e["cols"], e["rope"] // 2))
    cols = np.concatenate(cols)
    wfm = w_in[:, cols].reshape(16, 128, nw, 128).transpose(2, 1, 0, 3)
    d = {}
    d["wfm"] = np.ascontiguousarray(wfm)
    tmc1 = np.concatenate([np.arange(1024, 1536), np.arange(4940, 5452)])
    tmc2 = np.concatenate([np.arange(3136 + 3 * 128, 3136 + 4 * 128), np.arange(3136 + 5 * 128, 3136 + 6 * 128),
                           np.arange(3904, 3916), np.arange(6540, 6556)])
    d["wtm1"] = np.ascontiguousarray(w_in[:, tmc1].reshape(16, 128, 1024).transpose(1, 0, 2))
    d["wtm2"] = np.ascontiguousarray(w_in[:, tmc2].reshape(16, 128, 284).transpose(1, 0, 2))
    for i in range(1):
        gu = inp["ffn_w_gu"][l, i]
        d["wgu%d" % i] = np.ascontiguousarray(gu.reshape(16, 128, 88, 128).transpose(2, 1, 0, 3))
        d["wd%d" % i] = np.ascontiguousarray(inp["ffn_w_down"][l, i].reshape(44, 128, 2048))
    uq = inp["mla_w_uq"][l]
    qc = []
    for h in range(4):
        qc.append(np.arange(h * 192, h * 192 + 128))
    for pr in range(2):
        rc = np.concatenate([np.arange((2 * pr) * 192 + 128, (2 * pr) * 192 + 192), np.arange((2 * pr + 1) * 192 + 128, (2 * pr + 1) * 192 + 192)])
        qc.append(rc)
        qc.append(_swap(rc, 32))
    qc = np.concatenate(qc)
    d["wuq"] = np.ascontiguousarray(uq[:, qc].reshape(4, 128, 8, 128).transpose(2, 1, 0, 3))
    ukv = inp["mla_w_ukv"][l]
    kc = np.concatenate([np.arange(h * 256, h * 256 + 128) for h in range(4)])
    vc = np.concatenate([np.arange(h * 256 + 128, h * 256 + 256) for h in range(4)])
    d["wuk"] = np.ascontiguousarray(ukv[:, kc].reshape(4, 128, 4, 128).transpose(2, 1, 0, 3))
    d["wuv"] = np.ascontiguousarray(ukv[:, vc].reshape(4, 128, 512).transpose(1, 0, 2))
    d["gq"] = np.ascontiguousarray(inp["mla_g_cq"][l].reshape(4, 128).T)
    d["gkv"] = np.ascontiguousarray(inp["mla_g_ckv"][l].reshape(4, 128).T)
    d["lng"] = np.ascontiguousarray(np.broadcast_to(inp["ln_g"][l][None], (128, 4, 2048)))
    d["lnb"] = np.ascontiguousarray(np.broadcast_to(inp["ln_b"][l][None], (128, 4, 2048)))
    return d


def rope_consts():
    i = np.arange(128)
    c = np.zeros((128, 8), np.float32)
    c[:, 0] = 10000.0 ** (-(2.0 * (i % 64)) / 128.0)
    c[:, 1] = 10000.0 ** (-(2.0 * (i % 32)) / 64.0)
    c[:, 2] = np.where(i % 128 < 64, 1.0, -1.0)
    c[:, 3] = np.where(i % 64 < 32, 1.0, -1.0)
    return c


def ident_np():
    import ml_dtypes
    return np.eye(128, dtype=np.float32).astype(ml_dtypes.bfloat16)


def emit_layernorm(k, Xj, g_v, b_v, sm, junk):
    k.reduce(sm[:, 0:1], Xj.v, ALU.add)
    k.act(junk.v, Xj.v, AF.Square, accum_out=sm[:, 1:2])
    k.ts(sm[:, 2:3], sm[:, 0:1], 1.0 / 2048, ALU.mult)
    k.tt(sm[:, 3:4], sm[:, 2:3], sm[:, 2:3], ALU.mult)
    k.stt(sm[:, 4:5], sm[:, 1:2], 1.0 / 2048, sm[:, 3:4], ALU.mult, ALU.subtract)
    k.ts(sm[:, 4:5], sm[:, 4:5], 1e-5, ALU.add)
    k.act(sm[:, 5:6], sm[:, 4:5], AF.Sqrt)
    k.recip(sm[:, 6:7], sm[:, 5:6])
    k.ts(Xj.v, Xj.v, sm[:, 2:3], ALU.subtract, sm[:, 6:7], ALU.mult)
    k.tt(Xj.v, Xj.v, g_v, ALU.mult)
    k.tt(Xj.v, Xj.v, b_v, ALU.add, eng="pool")


def emit_transposeX(k, X, XT, xb, ptr, ident):
    for j in range(NJ):
        k.copy(xb.v, X[j].v, eng="act")
        for g in range(4):
            p = ptr[g % 2]
            for q in range(4):
                kc = g * 4 + q
                k.tr(p[:, q * 128:(q + 1) * 128], xb[:, kc * 128:(kc + 1) * 128], ident.v)
            k.copy(XT[:, g * 4:(g + 1) * 4, j * 128:(j + 1) * 128], p[:, 0:512].re("p (q n) -> p q n", q=4),
                   eng="dve" if g % 2 == 0 else "act")


def emit_ffn(k, X, XT, wgu, wd, R):
    HT, WG, WU, WD, PG, PU, PD, SG = R["HT"], R["WG"], R["WU"], R["WD"], R["PG"], R["PU"], R["PD"], R["SG"]
    nd = 0
    for fb in range(4):
        for fc in range(11):
            f = fb * 11 + fc
            wg, wu = WG[f % 3], WU[f % 3]
            k.dma("pool", wg.v, wgu[f])
            k.dma("pool", wu.v, wgu[44 + f])
            for half in range(2):
                ts_ = slice(half * 512, (half + 1) * 512)
                for kc in range(16):
                    k.mm(PG[half].v, wg[:, kc, :], XT[:, kc, ts_], start=(kc == 0), stop=(kc == 15))
                for kc in range(16):
                    k.mm(PU[half].v, wu[:, kc, :], XT[:, kc, ts_], start=(kc == 0), stop=(kc == 15))
                k.act(SG[half].v, PG[half].v, AF.Silu)
                k.tt(HT[:, fc, ts_], SG[half].v, PU[half].v, ALU.mult)
        for n in range(4):
            w = WD[nd % 2]
            nd += 1
            k.dma("pool", w.v, wd[fb * 11:(fb + 1) * 11, :, n * 512:(n + 1) * 512].rearrange("f p n -> p f n"))
            for j in range(NJ):
                pd = PD[j % 2]
                for fc in range(11):
                    k.mm(pd.v, HT[:, fc, j * 128:(j + 1) * 128], w[:, fc, :], start=(fc == 0), stop=(fc == 10))
                xs = X[j][:, n * 512:(n + 1) * 512]
                k.stt(xs, pd.v, 0.5, xs, ALU.mult, ALU.add)


def alloc_common(k):
    R = {}
    R["XT"] = k.sb([128, 16, TOK], BF16, "XT")
    R["xb"] = k.sb([128, 2048], BF16, "xb")
    R["junk"] = k.sb([128, 2048], BF16, "junk")
    R["ident"] = k.sb([128, 128], BF16, "ident")
    R["sm"] = k.sb([128, 8], F32, "sm")
    R["lng"] = k.sb([128, 2048], F32, "lng")
    R["lnb"] = k.sb([128, 2048], F32, "lnb")
    R["ptr"] = [k.ps([128, 1024], BF16, "ptr%d" % i) for i in range(2)]
    R["P"] = [k.ps([128, 512], F32, "P%d" % i) for i in range(6)]
    return R


def alloc_ffn(k, R):
    R["HT"] = k.sb([128, 11, TOK], BF16, "HT")
    R["WG"] = [k.sb([128, 16, 128], BF16, "WG%d" % i) for i in range(3)]
    R["WU"] = [k.sb([128, 16, 128], BF16, "WU%d" % i) for i in range(3)]
    R["WD"] = [k.sb([128, 11, 512], BF16, "WD%d" % i) for i in range(2)]
    R["SG"] = [k.sb([128, 512], F32, "SG%d" % i) for i in range(2)]
    P = R["P"]
    R["PG"], R["PU"], R["PD"] = P[0:2], P[2:4], P[4:6]


def build_A(UPTO=9):
    nc = bass.Bass("TRN2", target_bir_lowering=False)
    E, nw = fm_entries()
    dt = lambda name, shape, dtype, kind="ExternalInput": nc.dram_tensor(name, list(shape), dtype, kind=kind).ap()
    x_in = dt("x_in", [TOK, 2048], F32)
    pos_in = dt("pos", [128, TOK], I32)
    rc_in = dt("ropec", [128, 8], F32)
    id_in = dt("ident", [128, 128], BF16)
    wgu = dt("wgu0", [88, 128, 16, 128], F32)
    wd = dt("wd0", [44, 128, 2048], F32)
    wfm = dt("wfm", [nw, 128, 16, 128], F32)
    wtm1 = dt("wtm1", [128, 16, 1024], F32)
    wtm2 = dt("wtm2", [128, 16, 284], F32)
    wuq = dt("wuq", [8, 128, 4, 128], F32)
    wuk = dt("wuk", [4, 128, 4, 128], F32)
    wuv = dt("wuv", [128, 4, 512], F32)
    gq_in = dt("gq", [128, 4], F32)
    gkv_in = dt("gkv", [128, 4], F32)
    lng_in = dt("lng", [128, 4, 2048], F32)
    lnb_in = dt("lnb", [128, 4, 2048], F32)
    x1_out = dt("x1", [TOK, 2048], F32, "ExternalOutput")
    fm_out = dt("fm", [NFM, 128, TOK], BF16, "ExternalOutput")
    tm_out = dt("tm", [TOK, NTM], BF16, "ExternalOutput")
    tmf_out = dt("tmf", [TOK, 32], F32, "ExternalOutput")

    with ExitStack() as st:
        k = K(nc, st)
        R = alloc_common(k)
        XT, P = R["XT"], R["P"]
        k.dma("sp", R["ident"].v, id_in)
        k.dma("sp", R["lng"].v, lng_in[:, 0, :])
        k.dma("sp", R["lnb"].v, lnb_in[:, 0, :])
        with k.scope():
            X = R["X"] = [k.sb([128, 2048], F32, "X%d" % j) for j in range(NJ)]
            for j in range(NJ):
                k.dma("sp", X[j].v, x_in[j * 128:(j + 1) * 128, :])
            emit_transposeX(k, X, XT, R["xb"], R["ptr"], R["ident"])
            for j in range(NJ):
                k.ts(X[j].v, X[j].v, ALPHA, ALU.mult, eng="pool")
            if UPTO >= 2:
              with k.scope():
                alloc_ffn(k, R)
                emit_ffn(k, X, XT, wgu, wd, R)
            for j in range(NJ):
                if UPTO >= 2:
                    emit_layernorm(k, X[j], R["lng"].v, R["lnb"].v, R["sm"], R["junk"])
                k.dma("sp", x1_out[j * 128:(j + 1) * 128, :], X[j].v, is_output=True)
            emit_transposeX(k, X, XT, R["xb"], R["ptr"], R["ident"])

        if UPTO < 3:
            k.emit()
            return nc
        rc = k.sb([128, 8], F32, "rc")
        k.dma("sp", rc.v, rc_in)
        CT, ST = {}, {}
        for kind in (128, 64):
            CT[kind] = k.sb([128, TOK], F32, "C%d" % kind)
            ST[kind] = k.sb([128, TOK], F32, "S%d" % kind)
        ropescope = k.scope()
        ropescope.__enter__()
        posi = k.sb([128, TOK], I32, "posi")
        k.dma("sp", posi.v, pos_in)
        posf = k.sb([128, TOK], F32, "posf")
        k.copy(posf.v, posi.v)
        ang = k.sb([128, TOK], F32, "ang")
        kf = k.sb([128, TOK], F32, "kf")
        r = k.sb([128, TOK], F32, "r")
        for kind, ci in ((128, 0), (64, 1)):
            for which in ("sin", "cos"):
                k.ts(ang.v, posf.v, rc[:, ci:ci + 1], ALU.mult)
                if which == "cos":
                    k.ts(ang.v, ang.v, math.pi / 2, ALU.add)
                k.ts(kf.v, ang.v, 1.0 / TWO_PI, ALU.mult)
                k.copy(posi.v, kf.v)
                k.copy(kf.v, posi.v)
                k.stt(r.v, kf.v, -TWO_PI, ang.v, ALU.mult, ALU.add)
                k.ts(kf.v, r.v, math.pi, ALU.is_gt, -TWO_PI, ALU.mult)
                k.tt(r.v, r.v, kf.v, ALU.add)
                k.ts(kf.v, r.v, -math.pi, ALU.is_lt, TWO_PI, ALU.mult)
                k.tt(r.v, r.v, kf.v, ALU.add)
                if which == "sin":
                    k.act(ST[kind].v, r.v, AF.Sin)
                    k.ts(ST[kind].v, ST[kind].v, rc[:, 2 + ci:3 + ci], ALU.mult, -1.0, ALU.mult)
                else:
                    k.act(CT[kind].v, r.v, AF.Sin)

        ropescope.__exit__(None, None, None)
        if UPTO < 3.5:
            dbg = k.sb([128, TOK], BF16, "dbg")
            for i_, t_ in enumerate((CT[128], ST[128], CT[64], ST[64])):
                k.copy(dbg.v, t_.v)
                k.dma("sp", fm_out[i_], dbg.v, is_output=True)
            k.emit()
            return nc
        fmscope = k.scope()
        fmscope.__enter__()
        W1 = [k.sb([128, 16, 128], BF16, "W1_%d" % i) for i in range(2)]
        W2 = [k.sb([128, 16, 128], BF16, "W2_%d" % i) for i in range(2)]
        DST = [k.sb([128, TOK], BF16, "DST%d" % i) for i in range(4)]
        T1 = [k.sb([128, 512], F32, "T1_%d" % i) for i in range(2)]
        T2 = [k.sb([128, 512], F32, "T2_%d" % i) for i in range(2)]
        cqg = k.sb([128, 4, TOK], BF16, "cqg")
        ckvg = k.sb([128, 4, TOK], BF16, "ckvg")
        sqq = k.sb([128, 4, TOK], BF16, "sqq")
        sqkv = k.sb([128, 4, TOK], BF16, "sqkv")
        gq = k.sb([128, 4], F32, "gq")
        gkv = k.sb([128, 4], F32, "gkv")
        k.dma("sp", gq.v, gq_in)
        k.dma("sp", gkv.v, gkv_in)
        nd = 0
        PY, PS = P[0:2], P[2:4]
        import os
        if os.environ.get("ENT"):
            E = [E[int(i_)] for i_ in os.environ["ENT"].split(",")]
        for ei, e in enumerate(E):
            w1 = W1[ei % 2]
            k.dma("pool", w1.v, wfm[e["w1"]])
            if e["rope"]:
                w2 = W2[ei % 2]
                k.dma("pool", w2.v, wfm[e["w2"]])
            dst = None
            raw = None
            if isinstance(e["dst"], int):
                dst = DST[nd % 4]
                nd += 1
                if "raw" in e:
                    raw = DST[nd % 4]
                    nd += 1
            for half in range(2):
                ts_ = slice(half * 512, (half + 1) * 512)
                for kc in range(16):
                    k.mm(PY[half].v, w1[:, kc, :], XT[:, kc, ts_], start=(kc == 0), stop=(kc == 15))
                if e["rope"]:
                    for kc in range(16):
                        k.mm(PS[half].v, w2[:, kc, :], XT[:, kc, ts_], start=(kc == 0), stop=(kc == 15))
                    kind = e["rope"]
                    k.tt(T1[half].v, PY[half].v, CT[kind][:, ts_], ALU.mult)
                    k.tt(T2[half].v, PS[half].v, ST[kind][:, ts_], ALU.mult)
                    k.tt(dst[:, ts_], T1[half].v, T2[half].v, ALU.add)
                    if raw is not None:
                        k.copy(raw[:, ts_], PY[half].v, eng="act")
                elif dst is not None:
                    k.copy(dst[:, ts_], PY[half].v, eng="act")
                else:
                    which, fc = e["dst"]
                    cg, sq, g = (cqg, sqq, gq) if which == "cq" else (ckvg, sqkv, gkv)
                    k.ts(cg[:, fc, ts_], PY[half].v, g[:, fc:fc + 1], ALU.mult)
                    k.act(sq[:, fc, ts_], PY[half].v, AF.Square)
            if dst is not None:
                k.dma("sp", fm_out[e["dst"]], dst.v, is_output=True)
            if raw is not None:
                k.dma("sp", fm_out[e["raw"]], raw.v, is_output=True)

        if UPTO < 4:
            fmscope.__exit__(None, None, None)
            return nc
        ones = k.sb([128, 128], BF16, "ones")
        k.memset(ones.v, 1.0)
        rstdB = {}
        rstdC = {}
        for which, sq in (("q", sqq), ("kv", sqkv)):
            rb = k.sb([128, TOK], F32, "rstdB" + which)
            rcl = k.sb([128, NJ], F32, "rstdC" + which)
            for half in range(2):
                ts_ = slice(half * 512, (half + 1) * 512)
                for fc in range(4):
                    k.mm(PY[half].v, ones.v, sq[:, fc, ts_], start=(fc == 0), stop=(fc == 3))
                k.ts(T1[half].v, PY[half].v, 1.0 / 512, ALU.mult, 1e-6, ALU.add)
                k.act(T1[half].v, T1[half].v, AF.Sqrt)
                k.recip(rb[:, ts_], T1[half].v)
            for j in range(NJ):
                for fc in range(4):
                    k.mm(PS[0][:, j:j + 1], sq[:, fc, j * 128:(j + 1) * 128], ones[:, 0:1], start=(fc == 0), stop=(fc == 3))
            k.ts(T2[0][:, 0:NJ], PS[0][:, 0:NJ], 1.0 / 512, ALU.mult, 1e-6, ALU.add)
            k.act(T2[0][:, 0:NJ], T2[0][:, 0:NJ], AF.Sqrt)
            k.recip(rcl.v, T2[0][:, 0:NJ])
            rstdB[which], rstdC[which] = rb, rcl
        WQ = [k.sb([128, 4, 128], BF16, "WQ%d" % i) for i in range(4)]
        jobs = [("n", c, None, FM_QBN + c) for c in range(4)] + [("r", 4, 5, FM_QBR), ("r", 6, 7, FM_QBR + 1)]
        wi = 0
        for kind_, c1, c2, di in jobs:
            w1 = WQ[wi % 4]; wi += 1
            k.dma("pool", w1.v, wuq[c1])
            if c2 is not None:
                w2 = WQ[wi % 4]; wi += 1
                k.dma("pool", w2.v, wuq[c2])
            dst = DST[nd % 4]; nd += 1
            for half in range(2):
                ts_ = slice(half * 512, (half + 1) * 512)
                for fc in range(4):
                    k.mm(PY[half].v, w1[:, fc, :], cqg[:, fc, ts_], start=(fc == 0), stop=(fc == 3))
                if c2 is None:
                    k.tt(dst[:, ts_], PY[half].v, rstdB["q"][:, ts_], ALU.mult)
                else:
                    for fc in range(4):
                        k.mm(PS[half].v, w2[:, fc, :], cqg[:, fc, ts_], start=(fc == 0), stop=(fc == 3))
                    k.tt(T1[half].v, PY[half].v, CT[64][:, ts_], ALU.mult)
                    k.tt(T2[half].v, PS[half].v, ST[64][:, ts_], ALU.mult)
                    k.tt(T1[half].v, T1[half].v, T2[half].v, ALU.add)
                    k.tt(dst[:, ts_], T1[half].v, rstdB["q"][:, ts_], ALU.mult)
            k.dma("sp", fm_out[di], dst.v, is_output=True)
        for c in range(4):
            w1 = WQ[wi % 4]; wi += 1
            k.dma("pool", w1.v, wuk[c])
            dst = DST[nd % 4]; nd += 1
            for half in range(2):
                ts_ = slice(half * 512, (half + 1) * 512)
                for fc in range(4):
                    k.mm(PY[half].v, w1[:, fc, :], ckvg[:, fc, ts_], start=(fc == 0), stop=(fc == 3))
                k.tt(dst[:, ts_], PY[half].v, rstdB["kv"][:, ts_], ALU.mult)
            k.dma("sp", fm_out[FM_KBN + c], dst.v, is_output=True)
        WV = k.sb([128, 4, 512], BF16, "WV")
        k.dma("pool", WV.v, wuv)
        TMO = [k.sb([128, 512], BF16, "TMO%d" % i) for i in range(2)]
        nt = 0
        for j in range(NJ):
            p = P[4 + j % 2]
            for fc in range(4):
                k.mm(p.v, ckvg[:, fc, j * 128:(j + 1) * 128], WV[:, fc, :], start=(fc == 0), stop=(fc == 3))
            o = TMO[nt % 2]; nt += 1
            k.ts(o.v, p.v, rstdC["kv"][:, j:j + 1], ALU.mult)
            k.dma("sp", tm_out[j * 128:(j + 1) * 128, TM_VB:TM_VB + 512], o.v, is_output=True)

        fmscope.__exit__(None, None, None)
        TMO = [k.sb([128, 512], BF16, "TMOb%d" % i) for i in range(2)]
        WT = k.sb([128, 16, 1024], BF16, "WT")
        k.dma("pool", WT.v, wtm1)
        for j in range(NJ):
            for gi, c0 in ((0, TM_VA), (1, TM_VD)):
                p = P[4 + gi]
                for kc in range(16):
                    k.mm(p.v, XT[:, kc, j * 128:(j + 1) * 128], WT[:, kc, gi * 512:(gi + 1) * 512], start=(kc == 0), stop=(kc == 15))
                o = TMO[nt % 2]; nt += 1
                k.copy(o.v, p.v, eng="act")
                k.dma("sp", tm_out[j * 128:(j + 1) * 128, c0:c0 + 512], o.v, is_output=True)
        WT2 = k.sb([128, 16, 284], BF16, "WT2")
        k.dma("pool", WT2.v, wtm2)
        TF = [k.sb([128, 32], F32, "TF%d" % i) for i in range(2)]
        for j in range(NJ):
            p = P[4 + j % 2]
            for kc in range(16):
                k.mm(p[:, 0:284], XT[:, kc, j * 128:(j + 1) * 128], WT2[:, kc, :], start=(kc == 0), stop=(kc == 15))
            o = TMO[nt % 2]; nt += 1
            k.copy(o[:, 0:256], p[:, 0:256], eng="act")
            k.dma("sp", tm_out[j * 128:(j + 1) * 128, TM_VSLC:TM_VSLC + 256], o[:, 0:256], is_output=True)
            tf = TF[j % 2]
            k.memset(tf.v, 0.0)
            k.act(tf[:, 0:12], p[:, 256:268], AF.Sigmoid)
            k.copy(tf[:, 12:28], p[:, 268:284])
            k.dma("sp", tmf_out[j * 128:(j + 1) * 128, :], tf.v, is_output=True)
        k.emit()
    return nc

import ml_dtypes

NEG = -32768.0
IMM = -2.0e30
BIGN = -1.0e30
KG_KA, KG_KBN, KG_KBR, KG_KCMP, KG_VCMP, KG_KSLC, KG_KWIN, KG_KD, KG_IK, NKG = 0, 4, 8, 9, 10, 11, 12, 13, 17, 18
QF_QA, QF_QBN, QF_QBR, QF_QC, QF_QCR, QF_QD, QF_IQ, NQF = 0, 4, 8, 10, 14, 18, 22, 30
BF = ml_dtypes.bfloat16


def core_masks(c):
    d = {}
    kk = np.arange(128)[:, None]
    qq = np.arange(128)[None, :]
    tri = np.where(kk <= qq, 0.0, NEG).astype(np.float32)
    anti = np.where(kk > qq, 0.0, NEG).astype(np.float32)
    full = np.zeros((128, 128), np.float32)
    none = np.full((128, 128), NEG, np.float32)
    cT = np.stack([full if m < c else (tri if m == c else none) for m in range(8)], 1)
    d["causalT4"] = np.ascontiguousarray(np.tile(cT[:, :, None, :], (1, 1, 4, 1)).reshape(128, 8, 512)).astype(BF)
    w = []
    for mp in range(12):
        rel = mp - 4 - c
        w.append(none if (rel < -4 or rel > 0) else (anti if rel == -4 else (tri if rel == 0 else full)))
    wT = np.stack(w, 1)
    d["winT4"] = np.ascontiguousarray(np.tile(wT[:, :, None, :], (1, 1, 4, 1)).reshape(128, 12, 512)).astype(BF)
    triq = np.where(kk.T >= qq.T * 0 + np.arange(128)[None, :], 0.0, BIGN)
    qi = np.arange(128)[:, None]
    ki = np.arange(128)[None, :]
    triq = np.where(ki <= qi, 0.0, BIGN).astype(np.float32)
    cq = np.stack([np.zeros((128, 128), np.float32) if m < c else (triq if m == c else np.full((128, 128), BIGN, np.float32)) for m in range(8)], 1)
    d["causalQ"] = np.ascontiguousarray(cq).astype(np.float32)
    cm = np.zeros((128, 8, 4, 128), np.float32)
    for j in range(8):
        gc = 8 * j + c
        for nch in range(4):
            ng = nch * 128 + np.arange(128)[:, None]
            t = 128 * gc + np.arange(128)[None, :]
            ok = (16 * ng + 31 <= t) & (ng <= 510)
            cm[:, j, nch, :] = np.where(ok, 0.0, NEG)
    d["cmpmask4"] = np.ascontiguousarray(np.tile(cm[:, :, :, None, :], (1, 1, 1, 4, 1)).reshape(128, 8, 4, 512)).astype(BF)
    gm = np.zeros((128, 8, 4, 32), np.float32)
    own = np.zeros((128, 8, 4, 32), np.float32)
    F = np.zeros((128, 8, 128), np.float32)
    for j in range(8):
        gc = 8 * j + c
        cur = gc // 2
        n = np.arange(32)
        gm[:, j, :, :] = np.where(n < cur, 0.0, BIGN)[None, None, :]
        own[:, j, :, :] = np.where(n >= cur, 1.0, 0.0)[None, None, :]
        curq = 2 * gc + (np.arange(128) >= 64).astype(np.int64)
        b = np.arange(128)[None, :]
        cq_ = curq[:, None]
        forced = ((b == 0) | (b == cq_) | (b == cq_ - 1)) & (b <= cq_)
        F[:, j, :] = np.where(b > cq_, -1e4, np.where(forced, 1e4, 0.0))
    d["gm4"], d["own4"], d["nsaF"] = gm, own, F
    return d


def shared_consts():
    d = {}
    E = np.zeros((32, 32, 128), np.float32)
    for n in range(32):
        E[n, n, :] = 1.0
    d["Emoba"] = E.astype(BF)
    A = np.zeros((128, 4, 128), np.float32)
    for nch in range(4):
        for n in range(128):
            ng = nch * 128 + n
            if ng > 510:
                continue
            for b in range(128):
                if 4 * b - 1 <= ng <= 4 * b + 3:
                    A[n, nch, b] = 1.0
    d["Aimp"] = A.astype(BF)
    d["ident"] = ident_np()
    d["ident4"] = np.ascontiguousarray(np.tile(np.eye(128, dtype=np.float32), (1, 4))).astype(BF)
    return d


def host_prep_B(l, inp):
    d = {}
    d["wout"] = np.ascontiguousarray(inp["w_out"][l].reshape(16, 128, 2048).transpose(1, 0, 2))
    wq = inp["mem_wq"][l]
    d["wqm"] = np.ascontiguousarray(wq.reshape(16, 128, 4, 128).transpose(2, 1, 0, 3))
    wkv = inp["mem_wkv"][l]
    d["wkm"] = np.ascontiguousarray(wkv[:, 0:512].reshape(16, 128, 4, 128).transpose(2, 1, 0, 3))
    d["wvm"] = np.ascontiguousarray(wkv[:, 512:1024].reshape(16, 128, 512).transpose(1, 0, 2))
    d["wom"] = np.ascontiguousarray(inp["mem_wo"][l].reshape(4, 128, 2048).transpose(1, 0, 2))
    gu = inp["ffn_w_gu"][l, 1]
    d["wgu1"] = np.ascontiguousarray(gu.reshape(16, 128, 88, 128).transpose(2, 1, 0, 3))
    d["wd1"] = np.ascontiguousarray(inp["ffn_w_down"][l, 1].reshape(44, 128, 2048))
    d["lng"] = np.ascontiguousarray(np.broadcast_to(inp["ln_g"][l][None], (128, 4, 2048)))
    d["lnb"] = np.ascontiguousarray(np.broadcast_to(inp["ln_b"][l][None], (128, 4, 2048)))
    d["cpe"] = np.ascontiguousarray(inp["nsa_cmp_pe"][l].transpose(0, 2, 1))
    d["cw1"] = np.ascontiguousarray(inp["nsa_cmp_w1"][l].reshape(2, 32, 128, 128).transpose(0, 2, 1, 3))
    d["cw2"] = np.ascontiguousarray(inp["nsa_cmp_w2"][l])
    d["mem"] = np.ascontiguousarray(inp["mem"][0])
    return d


def assemble_global(resA):
    fm = np.stack([r["fm"] for r in resA], 0)
    tm = np.stack([r["tm"] for r in resA], 0)
    kidx = list(range(FM_KA, FM_KA + 4)) + list(range(FM_KBN, FM_KBN + 4)) + [FM_KBR, FM_KCMP, FM_VCMP, FM_KSLC, FM_KWIN] + list(range(FM_KD, FM_KD + 4)) + [FM_IK]
    kg = fm[:, kidx].reshape(8, NKG, 128, 8, 128).transpose(1, 2, 3, 0, 4).reshape(NKG, 128, 8192)
    kg = np.ascontiguousarray(kg)
    tg = tm.reshape(8, 8, 128, NTM).transpose(1, 0, 2, 3).reshape(8192, NTM)
    one = np.ones((8192, 1), BF)

    def aug4(c0):
        v = tg[:, c0:c0 + 512].reshape(8192, 4, 128)
        return np.ascontiguousarray(np.concatenate([v, np.ones((8192, 4, 1), BF)], 2).reshape(8192, 516))

    def aug1(c0):
        return np.ascontiguousarray(np.concatenate([tg[:, c0:c0 + 128], one], 1))
    vs = dict(vA=aug4(TM_VA), vB=aug4(TM_VB), vD=aug4(TM_VD), vslc=aug1(TM_VSLC), vwin=aug1(TM_VWIN))
    qidx = list(range(FM_QA, FM_QA + 4)) + list(range(FM_QBN, FM_QBN + 4)) + [FM_QBR, FM_QBR + 1] + list(range(FM_QC, FM_QC + 4)) + \
        list(range(FM_QCR, FM_QCR + 4)) + list(range(FM_QD, FM_QD + 4)) + list(range(FM_IQ, FM_IQ + 8))
    qfs = [np.ascontiguousarray(fm[c][qidx]) for c in range(8)]
    return qfs, kg, vs


def emit_attn(k, R, nch, G, parts_fn, bias_fn, v_fn, W, scale):
    S, O, PT = R["S"], R["O"], R["PT"]
    for kc_ in range(nch):
        s = S[kc_ % 2]
        biases = bias_fn(kc_)
        first = True
        for (l, r) in biases:
            k.mm(s[:, 0:G * 128], l, r, start=first, stop=False)
            first = False
        for h in range(G):
            parts = parts_fn(kc_, h)
            for pi, (l, r) in enumerate(parts):
                k.mm(s[:, h * 128:(h + 1) * 128], l, r, start=(pi == 0 and not biases), stop=(pi == len(parts) - 1))
        pt = PT[kc_ % 3]
        k.act(pt[:, 0:G * 128], s[:, 0:G * 128], AF.Exp, scale=scale)
        for h in range(G):
            k.mm(O[h][:, 0:W], pt[:, h * 128:(h + 1) * 128], v_fn(kc_, h), start=(kc_ == 0), stop=(kc_ == nch - 1))


def emit_norm(k, R, G, out_fn, gate_fn=None, accumulate=False, first=False):
    O, rs = R["O"], R["rs"]
    for h in range(G):
        k.ts(rs[:, h:h + 1], O[h][:, 128:129], 1e-30, ALU.max)
        k.recip(rs[:, h:h + 1], rs[:, h:h + 1])


def load_v(k, tile, src, W):
    sv = src.rearrange("(ch p) w -> p ch w", p=128)
    for q in range(4):
        k.dma("sp", tile[:, q * 16:(q + 1) * 16, :], sv[:, q * 16:(q + 1) * 16, :])


def build_B(UPTO=9):
    nc = bass.Bass("TRN2", target_bir_lowering=False)
    dt = lambda name, shape, dtype, kind="ExternalInput": nc.dram_tensor(name, list(shape), dtype, kind=kind).ap()
    x1_in = dt("x1", [TOK, 2048], F32)
    qf = dt("qf", [NQF, 128, TOK], BF16)
    kg = dt("kg", [NKG, 128, 8192], BF16)
    vA_in, vB_in, vD_in = dt("vA", [8192, 516], BF16), dt("vB", [8192, 516], BF16), dt("vD", [8192, 516], BF16)
    vslc_in, vwin_in = dt("vslc", [8192, 129], BF16), dt("vwin", [8192, 129], BF16)
    tmf_in = dt("tmf", [TOK, 32], F32)
    mem_in = dt("mem", [256, 2048], F32)
    wout = dt("wout", [128, 16, 2048], F32)
    wqm, wkm = dt("wqm", [4, 128, 16, 128], F32), dt("wkm", [4, 128, 16, 128], F32)
    wvm, wom = dt("wvm", [128, 16, 512], F32), dt("wom", [128, 4, 2048], F32)
    wgu, wd = dt("wgu1", [88, 128, 16, 128], F32), dt("wd1", [44, 128, 2048], F32)
    lng_in, lnb_in = dt("lng", [128, 4, 2048], F32), dt("lnb", [128, 4, 2048], F32)
    cpe, cw1, cw2 = dt("cpe", [2, 128, 32], F32), dt("cw1", [2, 128, 32, 128], F32), dt("cw2", [2, 128, 128], F32)
    causalT4_in, winT4_in = dt("causalT4", [128, 8, 512], BF16), dt("winT4", [128, 12, 512], BF16)
    causalQ_in = dt("causalQ", [128, 8, 128], F32)
    cmpmask4_in = dt("cmpmask4", [128, 8, 4, 512], BF16)
    gm4_in, own4_in = dt("gm4", [128, 8, 4, 32], F32), dt("own4", [128, 8, 4, 32], F32)
    nsaF_in = dt("nsaF", [128, 8, 128], F32)
    Emoba_in, Aimp_in = dt("Emoba", [32, 32, 128], BF16), dt("Aimp", [128, 4, 128], BF16)
    id_in, id4_in = dt("ident", [128, 128], BF16), dt("ident4", [128, 512], BF16)
    x4_out = dt("x4", [TOK, 2048], F32, "ExternalOutput")
    ocat_out = dt("ocat", [TOK, 2048], BF16, "ExternalOutput")
    mqs = nc.dram_tensor("mqs", [8, 128, 8192], BF16, kind="Internal").ap()

    with ExitStack() as st:
        k = K(nc, st)
        R = {}
        R["ident"] = ident = k.sb([128, 128], BF16, "ident")
        P = R["P"] = [k.ps([128, 512], F32, "P%d" % i) for i in range(7)]
        ptr = k.ps([128, 1024], BF16, "ptr")
        attn_scope = k.scope()
        attn_scope.__enter__()
        ident4 = k.sb([128, 512], BF16, "ident4")
        causalT4 = k.sb([128, 8, 512], BF16, "causalT4")
        G_t = k.sb([128, 8, 32], F32, "gates")
        Ocat = k.sb([128, 8, 2048], BF16, "Ocat")
        R["rs"] = rs = k.sb([128, 16], F32, "rs")
        R["PT"] = [k.sb([128, 512], BF16, "PT%d" % i) for i in range(3)]
        R["ptr"] = [ptr, ptr]
        R["S"], R["O"] = P[0:2], P[2:6]
        PX = P[6]
        O = R["O"]
        Ocat_attn, rs_attn, G_attn = Ocat, rs, G_t
        k.dma("sp", ident.v, id_in)
        k.dma("sp", ident4.v, id4_in)
        k.dma("sp", causalT4.v, causalT4_in)
        k.dma("sp", G_t.v, tmf_in.rearrange("(j p) c -> p j c", p=128))
        js = lambda j: slice(j * 128, (j + 1) * 128)
        cs = lambda c: slice(c * 128, (c + 1) * 128)

        def causal_bias(j, kc_):
            return [(ident.v, causalT4[:, kc_ - 8 * j, :])] if kc_ >= 8 * j else []

        def finish(G, j, col0, gate_col=None, mode="set", dst=None, Ocat=None, rs=None, G_t=None):
            Ocat = Ocat if Ocat is not None else Ocat_attn
            rs = rs if rs is not None else rs_attn
            G_t = G_t if G_t is not None else G_attn
            for h in range(G):
                k.ts(rs[:, h:h + 1], O[h][:, 128:129], 1e-30, ALU.max)
                k.recip(rs[:, h:h + 1], rs[:, h:h + 1])
                if gate_col is not None:
                    k.tt(rs[:, h:h + 1], rs[:, h:h + 1], G_t[:, j, 3 * h + gate_col:3 * h + gate_col + 1], ALU.mult)
                if dst is None:
                    k.ts(Ocat[:, j, col0 + h * 128:col0 + (h + 1) * 128], O[h][:, 0:128], rs[:, h:h + 1], ALU.mult)
                elif mode == "set":
                    k.ts(dst[:, h * 128:(h + 1) * 128], O[h][:, 0:128], rs[:, h:h + 1], ALU.mult)
                else:
                    k.stt(dst[:, h * 128:(h + 1) * 128], O[h][:, 0:128], rs[:, h:h + 1], dst[:, h * 128:(h + 1) * 128], ALU.mult, ALU.add)

        if UPTO >= 1:
          with k.scope():
            KT = [k.sb([128, 8192], BF16, "aKT%d" % h) for h in range(4)]
            VA = k.sb([128, 64, 516], BF16, "aV")
            Em = k.sb([32, 32, 128], BF16, "Em")
            Q = [k.sb([128, TOK], BF16, "aQ%d" % h) for h in range(4)]
            gm4 = k.sb([128, 8, 4, 32], F32, "gm4")
            own4 = k.sb([128, 8, 4, 32], F32, "own4")
            kms = k.sb([128, 4, 32], F32, "kms")
            kmb = k.sb([128, 4, 32], BF16, "kmb")
            gate = k.sb([128, 4, 32], F32, "gate")
            sel = k.sb([128, 4, 32], F32, "sel")
            selb = k.sb([128, 4, 32], BF16, "selb")
            m8 = k.sb([128, 32], F32, "m8")
            BT = k.sb([32, 512], BF16, "BT")
            for h in range(4):
                k.dma("sp", KT[h].v, kg[KG_KA + h])
                k.dma("sp", Q[h].v, qf[QF_QA + h])
            load_v(k, VA, vA_in, 516)
            k.dma("sp", Em.v, Emoba_in)
            k.dma("sp", gm4.v, gm4_in)
            k.dma("sp", own4.v, own4_in)
            for h in range(4):
                k.reduce(kms[:, h, :], KT[h].v.re("p (n b) -> p n b", b=256), ALU.add)
            k.copy(kmb.v, kms.v)
            for j in range(8):
                for h in range(4):
                    k.mm(PX[:, h * 32:(h + 1) * 32], Q[h][:, js(j)], kmb[:, h, :])
                k.tt(gate.v, PX[:, 0:128].re("p (h n) -> p h n", h=4), gm4[:, j, :, :], ALU.add)
                for h in range(4):
                    k.max8(m8[:, h * 8:(h + 1) * 8], gate[:, h, :])
                for h in range(4):
                    k.ts(sel[:, h, :], gate[:, h, :], m8[:, h * 8 + 2:h * 8 + 3], ALU.is_ge)
                k.tt(sel.v, sel.v, own4[:, j, :, :], ALU.max)
                k.ts(selb.v, sel.v, 1.0, ALU.subtract, 32768.0, ALU.mult)
                for h in range(4):
                    k.tr(ptr[0:32, h * 128:(h + 1) * 128], selb[:, h, :], ident.v)
                k.copy(BT.v, ptr[0:32, 0:512])
                emit_attn(k, R, 8 * j + 8, 4,
                          lambda kc_, h: [(KT[h][:, cs(kc_)], Q[h][:, js(j)])],
                          lambda kc_: [(Em[:, kc_ // 2, :], BT.v)] + causal_bias(j, kc_),
                          lambda kc_, h: VA[:, kc_, h * 129:(h + 1) * 129], 129, 128 ** -0.5)
                finish(4, j, 0)
        if UPTO >= 2:
          with k.scope():
            KT = [k.sb([128, 8192], BF16, "bKT%d" % h) for h in range(4)]
            KR = k.sb([128, 8192], BF16, "bKR")
            VB = k.sb([128, 64, 516], BF16, "bV")
            Qn = [k.sb([128, TOK], BF16, "bQn%d" % h) for h in range(4)]
            Qr = [k.sb([128, TOK], BF16, "bQr%d" % h) for h in range(2)]
            for h in range(4):
                k.dma("sp", KT[h].v, kg[KG_KBN + h])
                k.dma("sp", Qn[h].v, qf[QF_QBN + h])
            for h in range(2):
                k.dma("sp", Qr[h].v, qf[QF_QBR + h])
            k.dma("sp", KR.v, kg[KG_KBR])
            load_v(k, VB, vB_in, 516)
            for j in range(8):
                def parts(kc_, h, j=j):
                    hs = slice((h % 2) * 64, (h % 2) * 64 + 64)
                    return [(KT[h][:, cs(kc_)], Qn[h][:, js(j)]), (KR[hs, cs(kc_)], Qr[h // 2][hs, js(j)])]
                emit_attn(k, R, 8 * j + 8, 4, parts, lambda kc_: causal_bias(j, kc_),
                          lambda kc_, h: VB[:, kc_, h * 129:(h + 1) * 129], 129, 192 ** -0.5)
                finish(4, j, 512)
        if UPTO >= 3:
          with k.scope():
            KC = k.sb([128, 512], BF16, "KC")
            VC = k.sb([128, 4, 257], BF16, "VC")
            with k.scope():
                kcT = k.sb([128, 8192], BF16, "kcT")
                vcT = k.sb([128, 8192], BF16, "vcT")
                k.dma("sp", kcT.v, kg[KG_KCMP])
                k.dma("sp", vcT.v, kg[KG_VCMP])
                W1 = [k.sb([128, 32, 128], BF16, "cW1_%d" % i) for i in range(2)]
                W2 = [k.sb([128, 128], BF16, "cW2_%d" % i) for i in range(2)]
                PE_ = [k.sb([128, 32], BF16, "cpe%d" % i) for i in range(2)]
                Aimp = k.sb([128, 4, 128], BF16, "Aimp")
                k.dma("sp", Aimp.v, Aimp_in)
                pb = k.sb([128, 1], F32, "pb")
                xh = k.sb([128, 512], F32, "xh")
                uh = k.sb([128, 512], F32, "uh")
                hg = k.sb([128, 512], BF16, "hg")
                for i in range(2):
                    k.dma("pool", W1[i].v, cw1[i])
                    k.dma("pool", W2[i].v, cw2[i])
                    k.dma("pool", PE_[i].v, cpe[i])
                for i, src in ((0, kcT), (1, vcT)):
                    sv = src.v.re("p (n s) -> p n s", s=16)
                    H = P[0]
                    for jj in range(32):
                        k.mm(H[:, 0:511], W1[i][:, jj, :], sv[:, (jj // 16):(jj // 16) + 511, jj % 16], start=(jj == 0), stop=(jj == 31))
                    for jj in range(32):
                        k.mm(PX[:, 0:1], W1[i][:, jj, :], PE_[i][:, jj:jj + 1], start=(jj == 0), stop=(jj == 31))
                    k.copy(pb.v, PX[:, 0:1])
                    k.memset(xh.v, 0.0)
                    k.ts(xh[:, 0:511], H[:, 0:511], pb[:, 0:1], ALU.add)
                    k.tt(uh.v, xh.v, xh.v, ALU.mult)
                    k.ts(uh.v, uh.v, 0.044715, ALU.mult, 1.0, ALU.add)
                    k.tt(uh.v, uh.v, xh.v, ALU.mult)
                    k.act(uh.v, uh.v, AF.Sigmoid, scale=1.5957691216057308)
                    k.tt(hg.v, xh.v, uh.v, ALU.mult)
                    if i == 0:
                        k.mm(P[1].v, W2[0].v, hg.v)
                        k.copy(KC.v, P[1].v, eng="act")
                    else:
                        for nch_ in range(4):
                            k.mm(P[1][:, nch_ * 128:(nch_ + 1) * 128], hg[:, cs(nch_)], W2[1].v)
                        k.copy(VC[:, :, 0:128], P[1].v.re("p (c d) -> p c d", c=4), eng="act")
                        k.memset(VC[:, :, 128:129], 1.0)
                        k.copy(VC[:, :, 129:257], Aimp.v)
            KS = k.sb([128, 8192], BF16, "KS")
            KW = k.sb([128, 8192], BF16, "KW")
            VS = k.sb([128, 64, 129], BF16, "VS")
            VW = k.sb([128, 64, 129], BF16, "VW")
            QC = [k.sb([128, TOK], BF16, "QC%d" % h) for h in range(4)]
            QR = [k.sb([128, TOK], BF16, "QR%d" % h) for h in range(4)]
            cmpmask4 = k.sb([128, 8, 4, 512], BF16, "cmpmask4")
            winT4 = k.sb([128, 12, 512], BF16, "winT4")
            nsaF = k.sb([128, 8, 128], F32, "nsaF")
            Mq = k.sb([128, 8192], BF16, "Mq")
            imp = k.sb([128, 128], F32, "imp")
            wk = k.sb([128, 128], F32, "wk")
            selq = k.sb([128, 128], F32, "selq")
            selqb = k.sb([128, 128], BF16, "selqb")
            m8 = k.sb([128, 16], F32, "m8n")
            acc = k.sb([128, 512], F32, "acc")
            k.dma("sp", KS.v, kg[KG_KSLC])
            k.dma("sp", KW.v, kg[KG_KWIN])
            load_v(k, VS, vslc_in, 129)
            load_v(k, VW, vwin_in, 129)
            for h in range(4):
                k.dma("sp", QC[h].v, qf[QF_QC + h])
                k.dma("sp", QR[h].v, qf[QF_QCR + h])
            k.dma("sp", cmpmask4.v, cmpmask4_in)
            k.dma("sp", winT4.v, winT4_in)
            k.dma("sp", nsaF.v, nsaF_in)
            for j in range(8):
                emit_attn(k, R, 4, 4, lambda kc_, h: [(KC[:, cs(kc_)], QC[h][:, js(j)])],
                          lambda kc_: [(ident.v, cmpmask4[:, j, kc_, :])],
                          lambda kc_, h: VC[:, kc_, :], 257, 128 ** -0.5)
                finish(4, j, 0, gate_col=None, mode="set", dst=None) if False else None
                for h in range(4):
                    k.ts(rs[:, h:h + 1], O[h][:, 128:129], 1e-30, ALU.max)
                    k.recip(rs[:, h:h + 1], rs[:, h:h + 1])
                    if h == 0:
                        k.ts(imp.v, O[h][:, 129:257], rs[:, h:h + 1], ALU.mult)
                    else:
                        k.stt(imp.v, O[h][:, 129:257], rs[:, h:h + 1], imp.v, ALU.mult, ALU.add)
                    k.tt(rs[:, 8 + h:9 + h], rs[:, h:h + 1], G_t[:, j, 3 * h:3 * h + 1], ALU.mult)
                    k.ts(acc[:, cs(h)], O[h][:, 0:128], rs[:, 8 + h:9 + h], ALU.mult)
                k.tt(imp.v, imp.v, nsaF[:, j, :], ALU.add)
                k.max8(m8[:, 0:8], imp.v)
                k.mrep(wk.v, m8[:, 0:8], imp.v, -3.0e4)
                k.max8(m8[:, 8:16], wk.v)
                k.ts(selq.v, imp.v, m8[:, 15:16], ALU.is_ge)
                k.ts(selqb.v, selq.v, 1.0, ALU.subtract, 32768.0, ALU.mult)
                nb = 2 * (8 * j + 8)
                k.copy(Mq[:, 0:nb * 64].re("p (b s) -> p b s", s=64), selqb[:, 0:nb].re("p (b o) -> p b o", o=1).bc([128, nb, 64]))
                emit_attn(k, R, 8 * j + 8, 4, lambda kc_, h: [(KS[:, cs(kc_)], QR[h][:, js(j)])],
                          lambda kc_: [(Mq[:, cs(kc_)], ident4.v)] + causal_bias(j, kc_),
                          lambda kc_, h: VS[:, kc_, :], 129, 128 ** -0.5)
                finish(4, j, 0, gate_col=1, mode="add", dst=acc)
                mps = [mp for mp in range(12) if 8 * j - 4 + mp >= 0]
                emit_attn(k, R, len(mps), 4, lambda ii, h: [(KW[:, cs(8 * j - 4 + mps[ii])], QR[h][:, js(j)])],
                          lambda ii: [(ident.v, winT4[:, mps[ii], :])],
                          lambda ii, h: VW[:, 8 * j - 4 + mps[ii], :], 129, 128 ** -0.5)
                finish(4, j, 0, gate_col=2, mode="add", dst=acc)
                k.copy(Ocat[:, j, 1024:1536], acc.v, eng="act")
        if UPTO >= 4:
          MQS = [Tl(mqs[j], "mqs%d" % j) for j in range(8)]
          with k.scope():
            IK = k.sb([128, 8192], BF16, "IK")
            IQ = [k.sb([128, TOK], BF16, "IQ%d" % i) for i in range(8)]
            causalQ = k.sb([128, 8, 128], F32, "causalQ")
            SC = [k.sb([128, 8192], F32, "SC%d" % i) for i in range(2)]
            MqD = [k.sb([128, 8192], BF16, "MqD%d" % i) for i in range(2)]
            Rr = [k.sb([128, 512], F32, "Rr%d" % i) for i in range(3)]
            bs_ = k.sb([128, 8], F32, "bsd")
            cjunk = k.sb([128, 8192], BF16, "cjunk")
            k.dma("sp", IK.v, kg[KG_IK])
            for i in range(8):
                k.dma("sp", IQ[i].v, qf[QF_IQ + i])
            k.dma("sp", causalQ.v, causalQ_in)
            nr = 0
            for j in range(8):
                sc = SC[j % 2]
                Kc = (8 * j + 8) * 128
                for blk in range(Kc // 512):
                    bs = slice(blk * 512, (blk + 1) * 512)
                    for h in range(16):
                        hs = slice((h % 2) * 64, (h % 2) * 64 + 64)
                        s = R["S"][h % 2]
                        k.mm(s.v, IQ[h // 2][hs, js(j)], IK[hs, bs])
                        rr = Rr[nr % 3]
                        nr += 1
                        k.act(rr.v, s.v, AF.Relu)
                        wv = G_t[:, j, 12 + h:13 + h]
                        if h == 0:
                            k.ts(sc[:, bs], rr.v, wv, ALU.mult)
                        else:
                            k.stt(sc[:, bs], rr.v, wv, sc[:, bs], ALU.mult, ALU.add)
                k.tt(sc[:, (8 * j) * 128:Kc].re("p (m k) -> p m k", m=8), sc[:, (8 * j) * 128:Kc].re("p (m k) -> p m k", m=8), causalQ.v, ALU.add)
                NIT = 30
                W0 = 6.0e4
                k.memset(bs_[:, 0:1], -3.0e4)
                k.ts(bs_[:, 1:2], bs_[:, 0:1], W0 / 2, ALU.add)
                for it in range(NIT):
                    ci = W0 / (2 ** (it + 1))
                    k.ts(cjunk[:, 0:Kc], sc[:, 0:Kc], bs_[:, 1:2], ALU.is_ge, None, ALU.add, accum_out=bs_[:, 2:3])
                    k.ts(bs_[:, 3:4], bs_[:, 2:3], 256.0, ALU.is_ge, ci, ALU.mult)
                    k.tt(bs_[:, 0:1], bs_[:, 0:1], bs_[:, 3:4], ALU.add)
                    if it < NIT - 1:
                        k.ts(bs_[:, 1:2], bs_[:, 0:1], ci / 2, ALU.add)
                mq = MqD[j % 2]
                k.ts(mq[:, 0:Kc], sc[:, 0:Kc], bs_[:, 0:1], ALU.is_lt, NEG, ALU.mult)
                k.dma("sp", V(MQS[j], mqs[j][:, 0:Kc]), mq[:, 0:Kc])
          with k.scope():
            KT = [k.sb([128, 8192], BF16, "dKT%d" % h) for h in range(4)]
            VD = k.sb([128, 64, 516], BF16, "dV")
            Q = [k.sb([128, TOK], BF16, "dQ%d" % h) for h in range(4)]
            MqL = [k.sb([128, 8192], BF16, "MqL%d" % i) for i in range(1)]
            for h in range(4):
                k.dma("sp", KT[h].v, kg[KG_KD + h])
                k.dma("sp", Q[h].v, qf[QF_QD + h])
            load_v(k, VD, vD_in, 516)
            for j in range(8):
                Kc = (8 * j + 8) * 128
                mq = MqL[0]
                k.dma("sp", mq[:, 0:Kc], V(MQS[j], mqs[j][:, 0:Kc]))
                emit_attn(k, R, 8 * j + 8, 4, lambda kc_, h: [(KT[h][:, cs(kc_)], Q[h][:, js(j)])],
                          lambda kc_: [(mq[:, cs(kc_)], ident4.v)] + causal_bias(j, kc_),
                          lambda kc_, h: VD[:, kc_, h * 129:(h + 1) * 129], 129, 128 ** -0.5)
                finish(4, j, 1536)
        OCD = [Tl(ocat_out[js(j), :], "ocd%d" % j) for j in range(8)]
        for j in range(8):
            k.dma("sp", V(OCD[j], ocat_out[js(j), :]), Ocat[:, j, :], is_output=True)
        attn_scope.__exit__(None, None, None)
        if UPTO < 5:
            return nc
        R["XT"] = XT = k.sb([128, 16, TOK], BF16, "XT")
        R["xb"] = k.sb([128, 2048], BF16, "xb")
        R["junk"] = R["xb"]
        R["sm"] = k.sb([128, 8], F32, "sm")
        lng = k.sb([128, 2048], F32, "lng")
        lnb = k.sb([128, 2048], F32, "lnb")
        X = R["X"] = [k.sb([128, 2048], F32, "X%d" % j) for j in range(NJ)]
        PD = P[4:6]
        for j in range(NJ):
            k.dma("sp", X[j].v, x1_in[js(j), :])
            k.ts(X[j].v, X[j].v, ALPHA, ALU.mult, eng="pool")
        with k.scope():
            OCL = [k.sb([128, 2048], BF16, "OCL%d" % i) for i in range(2)]
            for j in range(NJ):
                ocl = OCL[j % 2]
                k.dma("sp", ocl.v, V(OCD[j], ocat_out[js(j), :]))
                for g in range(4):
                    for q in range(4):
                        kc = g * 4 + q
                        k.tr(ptr[:, q * 128:(q + 1) * 128], ocl[:, cs(kc)], ident.v)
                    k.copy(XT[:, g * 4:(g + 1) * 4, js(j)], ptr[:, 0:512].re("p (q n) -> p q n", q=4), eng="dve" if g % 2 == 0 else "act")
            WO = [k.sb([128, 16, 512], BF16, "WO%d" % i) for i in range(2)]
            for n in range(4):
                w = WO[n % 2]
                k.dma("pool", w.v, wout[:, :, n * 512:(n + 1) * 512])
                for j in range(NJ):
                    pd = PD[j % 2]
                    for kc in range(16):
                        k.mm(pd.v, XT[:, kc, js(j)], w[:, kc, :], start=(kc == 0), stop=(kc == 15))
                    xs = X[j][:, n * 512:(n + 1) * 512]
                    k.tt(xs, pd.v, xs, ALU.add)
            k.dma("sp", lng.v, lng_in[:, 1, :])
            k.dma("sp", lnb.v, lnb_in[:, 1, :])
            for j in range(NJ):
                emit_layernorm(k, X[j], lng.v, lnb.v, R["sm"], R["junk"])
        with k.scope():
            R["PT"] = [k.sb([128, 512], BF16, "PTm%d" % i) for i in range(3)]
            rs_m = k.sb([128, 16], F32, "rs_m")
            OM = k.sb([128, 1, 512], BF16, "OM")
            QM = [k.sb([128, TOK], BF16, "QM%d" % h) for h in range(4)]
            KM = [k.sb([128, 256], BF16, "KM%d" % h) for h in range(4)]
            VM = k.sb([128, 2, 516], BF16, "VM")
            projscope = k.scope()
            projscope.__enter__()
            MT = k.sb([128, 16, 256], BF16, "MT")
            mf = k.sb([128, 2048], F32, "mf")
            for mc in range(2):
                k.dma("sp", mf.v, mem_in[mc * 128:(mc + 1) * 128, :])
                k.copy(R["xb"].v, mf.v, eng="act")
                for g in range(4):
                    for q in range(4):
                        kc = g * 4 + q
                        k.tr(ptr[:, q * 128:(q + 1) * 128], R["xb"][:, cs(kc)], ident.v)
                    k.copy(MT[:, g * 4:(g + 1) * 4, mc * 128:(mc + 1) * 128], ptr[:, 0:512].re("p (q n) -> p q n", q=4))
            emit_transposeX(k, X, XT, R["xb"], R["ptr"], ident)
            for j in range(NJ):
                k.ts(X[j].v, X[j].v, ALPHA, ALU.mult, eng="pool")
            Wm = [k.sb([128, 16, 128], BF16, "Wm%d" % i) for i in range(2)]
            Wv = k.sb([128, 16, 512], BF16, "Wvm")
            k.dma("pool", Wv.v, wvm)
            nw = 0
            for h in range(4):
                w = Wm[nw % 2]; nw += 1
                k.dma("pool", w.v, wqm[h])
                for half in range(2):
                    ts_ = slice(half * 512, (half + 1) * 512)
                    for kc in range(16):
                        k.mm(P[half].v, w[:, kc, :], XT[:, kc, ts_], start=(kc == 0), stop=(kc == 15))
                    k.copy(QM[h][:, ts_], P[half].v, eng="act")
                w = Wm[nw % 2]; nw += 1
                k.dma("pool", w.v, wkm[h])
                for kc in range(16):
                    k.mm(PX[:, 0:256], w[:, kc, :], MT[:, kc, :], start=(kc == 0), stop=(kc == 15))
                k.copy(KM[h].v, PX[:, 0:256])
            k.memset(VM.v, 1.0)
            for mc in range(2):
                for kc in range(16):
                    k.mm(PX.v, MT[:, kc, mc * 128:(mc + 1) * 128], Wv[:, kc, :], start=(kc == 0), stop=(kc == 15))
                k.copy(VM[:, mc, :].re("p (h d) -> p h d", h=4)[:, :, 0:128], PX.v.re("p (h d) -> p h d", h=4))
            projscope.__exit__(None, None, None)
            Wo = k.sb([128, 4, 2048], BF16, "Wom")
            k.dma("pool", Wo.v, wom)
            OMT = k.sb([128, 4, TOK], BF16, "OMT")
            for j in range(NJ):
                emit_attn(k, R, 2, 4, lambda kc_, h: [(KM[h][:, cs(kc_)], QM[h][:, js(j)])], lambda kc_: [],
                          lambda kc_, h: VM[:, kc_, h * 129:(h + 1) * 129], 129, 128 ** -0.5)
                finish(4, 0, 0, Ocat=OM, rs=rs_m)
                for q in range(4):
                    k.tr(ptr[:, q * 128:(q + 1) * 128], OM[:, 0, cs(q)], ident.v)
                k.copy(OMT[:, :, js(j)], ptr[:, 0:512].re("p (q n) -> p q n", q=4))
            for n in range(4):
                for j in range(NJ):
                    pd = PD[j % 2]
                    for kc in range(4):
                        k.mm(pd.v, OMT[:, kc, js(j)], Wo[:, kc, n * 512:(n + 1) * 512], start=(kc == 0), stop=(kc == 3))
                    xs = X[j][:, n * 512:(n + 1) * 512]
                    k.tt(xs, pd.v, xs, ALU.add)
            k.dma("sp", lng.v, lng_in[:, 2, :])
            k.dma("sp", lnb.v, lnb_in[:, 2, :])
            for j in range(NJ):
                emit_layernorm(k, X[j], lng.v, lnb.v, R["sm"], R["junk"])
        emit_transposeX(k, X, XT, R["xb"], R["ptr"], ident)
        for j in range(NJ):
            k.ts(X[j].v, X[j].v, ALPHA, ALU.mult, eng="pool")
        with k.scope():
            alloc_ffn(k, R)
            emit_ffn(k, X, XT, wgu, wd, R)
        k.dma("sp", lng.v, lng_in[:, 3, :])
        k.dma("sp", lnb.v, lnb_in[:, 3, :])
        for j in range(NJ):
            emit_layernorm(k, X[j], lng.v, lnb.v, R["sm"], R["junk"])
            k.dma("sp", x4_out[js(j), :], X[j].v, is_output=True)
        k.emit()
    return nc


_PROGS = {}


def _prog(name):
    if name not in _PROGS:
        _PROGS[name] = build_A() if name == "A" else build_B()
    return _PROGS[name]


def kernel(**inputs):
    inp = {k_: np.asarray(v_) for k_, v_ in inputs.items()}
    x = inp["x"][0]
    pos = inp["positions"][0].astype(np.int32)
    rows = [np.concatenate([np.arange((8 * j + c) * 128, (8 * j + c + 1) * 128) for j in range(8)]) for c in range(8)]
    xs = [np.ascontiguousarray(x[rows[c]]) for c in range(8)]
    posb = [np.ascontiguousarray(np.broadcast_to(pos[rows[c]][None], (128, 1024))).astype(np.int32) for c in range(8)]
    rc, idn = rope_consts(), ident_np()
    sc = shared_consts()
    cms = [core_masks(c) for c in range(8)]
    for l in range(4):
        dA = host_prep_A(l, inp)
        wA = {k_: dA[k_] for k_ in ("wgu0", "wd0", "wfm", "wtm1", "wtm2", "wuq", "wuk", "wuv", "gq", "gkv", "lng", "lnb")}
        mapsA = [dict(x_in=xs[c], pos=posb[c], ropec=rc, ident=idn, **wA) for c in range(8)]
        resA = _bu.run_bass_kernel_spmd(_prog("A"), mapsA, core_ids=list(range(8))).results
        del mapsA, wA
        qfs, kg, vs = assemble_global(resA)
        dB = host_prep_B(l, inp)
        del dA
        mapsB = [dict(x1=resA[c]["x1"], qf=qfs[c], kg=kg, tmf=resA[c]["tmf"], **vs, **dB, **sc, **cms[c]) for c in range(8)]
        resB = _bu.run_bass_kernel_spmd(_prog("B"), mapsB, core_ids=list(range(8))).results
        del mapsB, dB
        xs = [np.ascontiguousarray(resB[c]["x4"]) for c in range(8)]
    out = np.zeros((8192, 2048), np.float32)
    for c in range(8):
        out[rows[c]] = xs[c]
    return out[None]
```

```python
import numpy as np
from contextlib import ExitStack
import concourse.bass as bass
import concourse.mybir as mybir
import concourse.bass_utils as _bu

F32 = mybir.dt.float32
BF16 = mybir.dt.bfloat16
I32 = mybir.dt.int32
ALU = mybir.AluOpType
AF = mybir.ActivationFunctionType
AX = mybir.AxisListType


class Tl:
    __slots__ = ("h", "name", "w", "r", "sem", "dcount", "psum")

    def __init__(self, h, name):
        self.h = h
        self.name = name
        self.w = None
        self.r = {}
        self.sem = None
        self.dcount = 0
        self.psum = False

    def __getitem__(self, idx):
        return V(self, self.h[idx])

    @property
    def v(self):
        return V(self, self.h[:])


class V:
    __slots__ = ("t", "ap")

    def __init__(self, t, ap):
        self.t = t
        self.ap = ap

    def __getitem__(self, idx):
        return V(self.t, self.ap[idx])

    def re(self, pat, **kw):
        return V(self.t, self.ap.rearrange(pat, **kw))

    def bc(self, shape):
        return V(self.t, self.ap.to_broadcast(shape))


def _ap(x):
    return x.ap if isinstance(x, V) else x


class K:
    ENG = ("pe", "act", "dve", "pool", "sp")

    def __init__(self, nc, stack):
        self.nc = nc
        self.stack = stack
        self.q = {e: [] for e in self.ENG}
        self.cnt = {e: 0 for e in self.ENG}
        self.sem = {e: stack.enter_context(nc.semaphore("s_" + e)) for e in self.ENG}
        self.known = {e: {} for e in self.ENG}
        self.out_waits = []
        self.ntile = 0
        self.outer = stack
        self.sempool = []
        self.scoped = []

    def sb(self, shape, dt, name=None):
        self.ntile += 1
        name = "sb_%s_%d" % (name or "t", self.ntile)
        h = self.stack.enter_context(self.nc.sbuf_tensor(name, list(shape), dt))
        t = Tl(h, name)
        if self.scoped:
            self.scoped[-1].append(t)
        return t

    def ps(self, shape, dt=F32, name=None):
        self.ntile += 1
        name = "ps_%s_%d" % (name or "p", self.ntile)
        h = self.stack.enter_context(self.nc.psum_tensor(name, list(shape), dt))
        t = Tl(h, name)
        t.psum = True
        if self.scoped:
            self.scoped[-1].append(t)
        return t

    def _deps(self, eng, reads, writes):
        deps = []
        for t in reads:
            if t.w is not None:
                deps.append(t.w)
        for t in writes:
            if t.w is not None:
                deps.append(t.w)
            deps.extend(t.r.values())
        waits = []
        kn = self.known[eng]
        for d in deps:
            if d[0] == "eng":
                _, e2, idx = d
                if e2 == eng and eng in ("pe", "sp"):
                    continue
                key = e2
                if kn.get(key, 0) >= idx:
                    continue
                kn[key] = idx
                waits.append((self.sem[e2], idx))
            else:
                _, t, c = d
                key = t.name
                if kn.get(key, 0) >= c:
                    continue
                kn[key] = c
                waits.append((t.sem, 16 * c))
        return waits

    def op(self, eng, fn, reads, writes):
        reads = [x.t if isinstance(x, V) else x for x in reads if x is not None and not isinstance(x, (int, float))]
        writes = [x.t if isinstance(x, V) else x for x in writes]
        reads = [t for t in reads if isinstance(t, Tl)]
        writes = writes + [t for t in reads if t.psum]
        reads = [t for t in reads if not t.psum]
        waits = self._deps(eng, reads, writes)
        self.cnt[eng] += 1
        idx = self.cnt[eng]
        self.q[eng].append((waits, fn, True))
        for t in reads:
            t.r[eng] = ("eng", eng, idx)
        for t in writes:
            t.w = ("eng", eng, idx)
            t.r = {}

    def uid(self):
        self.ntile += 1
        return self.ntile

    def dma(self, qeng, out, in_, is_output=False, chain=False):
        reads = [in_.t] if isinstance(in_, V) else []
        writes = [out.t] if isinstance(out, V) else []
        saved = None
        if chain and writes and writes[0].w is not None and writes[0].w[0] == "dma" and writes[0].w[1] is writes[0]:
            saved = writes[0].w
            writes[0].w = None
        waits = self._deps(qeng, reads, writes)
        if saved is not None:
            writes[0].w = saved
        t = (writes or reads)[0]
        if t.sem is None:
            if self.sempool:
                t.sem, t.dcount = self.sempool.pop()
            else:
                t.sem = self.outer.enter_context(self.nc.semaphore("d_" + t.name))
        t.dcount += 1
        c = t.dcount
        sem = t.sem
        o, i = _ap(out), _ap(in_)
        self.q[qeng].append((waits, lambda e: e.dma_start(out=o, in_=i), ("dma", sem)))
        for x in writes:
            x.w = ("dma", t, c)
            x.r = {}
        for x in reads:
            x.r["dma"] = ("dma", t, c)
        if is_output:
            self.out_waits.append((sem, 16 * c))

    def mm(self, out, lhsT, rhs, start=True, stop=True):
        o, l, r = _ap(out), _ap(lhsT), _ap(rhs)
        self.op("pe", lambda e: e.matmul(o, l, r, start=start, stop=stop), [lhsT, rhs], [out])

    def tr(self, out, in_, ident):
        o, i, d = _ap(out), _ap(in_), _ap(ident)
        self.op("pe", lambda e: e.transpose(o, i, d), [in_, ident], [out])

    def act(self, out, in_, func, bias=None, scale=None, accum_out=None, eng="act"):
        o, i = _ap(out), _ap(in_)
        kw = {}
        if bias is not None:
            kw["bias"] = _ap(bias)
        if scale is not None:
            kw["scale"] = _ap(scale)
        if accum_out is not None:
            kw["accum_out"] = _ap(accum_out)
        w = [out] + ([accum_out] if accum_out is not None else [])
        self.op(eng, lambda e: e.activation(o, i, func, **kw), [in_, bias, scale], w)

    def tt(self, out, in0, in1, op, eng="dve"):
        o, a, b = _ap(out), _ap(in0), _ap(in1)
        self.op(eng, lambda e: e.tensor_tensor(o, a, b, op), [in0, in1], [out])

    def ts(self, out, in0, s1, op0, s2=None, op1=None, accum_out=None, eng="dve"):
        o, a = _ap(out), _ap(in0)
        x1, x2 = _ap(s1), _ap(s2)
        kw = {}
        if op1 is not None:
            kw["op1"] = op1
        if accum_out is not None:
            kw["accum_out"] = _ap(accum_out)
        w = [out] + ([accum_out] if accum_out is not None else [])
        self.op(eng, lambda e: e.tensor_scalar(o, a, x1, x2, op0, **kw), [in0, s1, s2], w)

    def stt(self, out, in0, scalar, in1, op0, op1, eng="dve"):
        o, a, s, b = _ap(out), _ap(in0), _ap(scalar), _ap(in1)
        self.op(eng, lambda e: e.scalar_tensor_tensor(o, a, s, b, op0, op1), [in0, scalar, in1], [out])

    def copy(self, out, in_, eng="dve"):
        o, i = _ap(out), _ap(in_)
        if eng == "act":
            self.op(eng, lambda e: e.copy(o, i), [in_], [out])
        else:
            self.op(eng, lambda e: e.tensor_copy(o, i), [in_], [out])

    def memset(self, out, val, eng="dve"):
        o = _ap(out)
        self.op(eng, lambda e: e.memset(o, val), [], [out])

    def reduce(self, out, in_, op, axis=AX.X, eng="dve"):
        o, i = _ap(out), _ap(in_)
        self.op(eng, lambda e: e.tensor_reduce(o, i, axis, op), [in_], [out])

    def max8(self, out, in_):
        o, i = _ap(out), _ap(in_)
        self.op("dve", lambda e: e.max(o, i), [in_], [out])

    def mrep(self, out, rep, vals, imm):
        o, r, v = _ap(out), _ap(rep), _ap(vals)
        self.op("dve", lambda e: e.match_replace(o, r, v, imm), [rep, vals], [out])

    def recip(self, out, in_):
        o, i = _ap(out), _ap(in_)
        self.op("dve", lambda e: e.reciprocal(o, i), [in_], [out])

    def scope(self):
        k = self

        class _S:
            def __enter__(self_):
                self_.prev = k.stack
                self_.st = ExitStack()
                self_.st.__enter__()
                k.stack = self_.st
                k.scoped.append([])
                return self_

            def __exit__(self_, *a):
                tiles = k.scoped.pop()
                extra = [(t.sem, 16 * t.dcount) for t in tiles if t.sem is not None]
                for t in tiles:
                    d_ = t.r.get("dma")
                    if d_ is not None:
                        extra.append((d_[1].sem, 16 * d_[2]))
                k.emit(extra)
                for t in tiles:
                    if t.sem is not None:
                        k.sempool.append((t.sem, t.dcount))
                        t.sem = None
                k.stack = self_.prev
                self_.st.__exit__(None, None, None)
                return False
        return _S()

    def emit(self, extra=()):
        nc = self.nc
        fin = list(self.out_waits) + list(extra)
        self.out_waits = []
        q = self.q
        self.q = {e: [] for e in self.ENG}
        sems = self.sem
        self.cnt["sp"] += 1
        cnt = dict(self.cnt)
        for e_ in self.ENG:
            for e2 in self.ENG:
                self.known[e_][e2] = max(self.known[e_].get(e2, 0), cnt[e2])

        def replay(name, e):
            for waits, fn, inc in q[name]:
                for s, v in waits:
                    e.wait_ge(s, v)
                ins = fn(e)
                if inc is True:
                    ins.then_inc(sems[name], 1)
                elif inc is not None:
                    ins.then_inc(inc[1], 16)
            if name == "sp":
                for s, v in fin:
                    e.wait_ge(s, v)
                e.sem_inc(sems["sp"], 1)
            for e2 in ("pe", "act", "dve", "pool", "sp"):
                if e2 != name and cnt[e2] > 0:
                    e.wait_ge(sems[e2], cnt[e2])

        with nc.Block() as block:
            @block.tensor
            def _(e):
                replay("pe", e)

            @block.scalar
            def _(e):
                replay("act", e)

            @block.vector
            def _(e):
                replay("dve", e)

            @block.gpsimd
            def _(e):
                replay("pool", e)

            @block.sync
            def _(e):
                replay("sp", e)

import math

ALPHA = 8 ** 0.25
TOK = 1024
NJ = 8
TWO_PI = 2 * math.pi

FM_QA, FM_KA, FM_QBN, FM_QBR, FM_KBN, FM_KBR = 0, 4, 8, 12, 14, 18
FM_QC, FM_QCR, FM_KCMP, FM_VCMP, FM_KSLC, FM_KWIN = 19, 23, 27, 28, 29, 30
FM_QD, FM_KD, FM_IQ, FM_IK = 31, 35, 39, 47
NFM = 48
TM_VA, TM_VB, TM_VSLC, TM_VWIN, TM_VD, NTM = 0, 512, 1024, 1152, 1280, 1792


def _swap(cols, half):
    c = np.asarray(cols).reshape(-1, 2, half)
    return c[:, ::-1, :].reshape(-1)


def fm_entries():
    hd = lambda base, h: np.arange(base + h * 128, base + (h + 1) * 128)
    E = []
    for h in range(4):
        E.append(dict(cols=hd(0, h), rope=128, dst=FM_QA + h))
    for h in range(4):
        E.append(dict(cols=hd(512, h), rope=128, dst=FM_KA + h))
    for fc in range(4):
        E.append(dict(cols=hd(1536, fc), rope=None, dst=("cq", fc)))
    for fc in range(4):
        E.append(dict(cols=hd(2048, fc), rope=None, dst=("ckv", fc)))
    kr = np.arange(2560, 2624)
    E.append(dict(cols=np.concatenate([kr, kr]), rope=64, dst=FM_KBR))
    for h in range(4):
        E.append(dict(cols=hd(2624, h), rope=128, dst=FM_QCR + h, raw=FM_QC + h))
    E.append(dict(cols=hd(3136, 0), rope=None, dst=FM_KCMP))
    E.append(dict(cols=hd(3136, 1), rope=None, dst=FM_VCMP))
    E.append(dict(cols=hd(3136, 2), rope=128, dst=FM_KSLC))
    E.append(dict(cols=hd(3136, 4), rope=128, dst=FM_KWIN))
    for h in range(4):
        E.append(dict(cols=hd(3916, h), rope=128, dst=FM_QD + h))
    for h in range(4):
        E.append(dict(cols=hd(4428, h), rope=128, dst=FM_KD + h))
    for j in range(8):
        E.append(dict(cols=hd(5452, j), rope=64, dst=FM_IQ + j))
    ik = np.arange(6476, 6540)
    E.append(dict(cols=np.concatenate([ik, ik]), rope=64, dst=FM_IK))
    n = 0
    for e in E:
        e["w1"] = n
        n += 1
        if e["rope"]:
            e["w2"] = n
            n += 1
    return E, n


def host_prep_A(l, inp):
    E, nw = fm_entries()
    w_in = inp["w_in"][l]
    cols = []
    for e in E:
        cols.append(e["cols"])
        if e["rope"]:
            cols.append(_swap(e["cols"], e["rope"] // 2))
    cols = np.concatenate(cols)
    wfm = w_in[:, cols].reshape(16, 128, nw, 128).transpose(2, 1, 0, 3)
    d = {}
    d["wfm"] = np.ascontiguousarray(wfm)
    tmc1 = np.concatenate([np.arange(1024, 1536), np.arange(4940, 5452)])
    tmc2 = np.concatenate([np.arange(3136 + 3 * 128, 3136 + 4 * 128), np.arange(3136 + 5 * 128, 3136 + 6 * 128),
                           np.arange(3904, 3916), np.arange(6540, 6556)])
    d["wtm1"] = np.ascontiguousarray(w_in[:, tmc1].reshape(16, 128, 1024).transpose(1, 0, 2))
    d["wtm2"] = np.ascontiguousarray(w_in[:, tmc2].reshape(16, 128, 284).transpose(1, 0, 2))
    for i in range(1):
        gu = inp["ffn_w_gu"][l, i]
        d["wgu%d" % i] = np.ascontiguousarray(gu.reshape(16, 128, 88, 128).transpose(2, 1, 0, 3))
        d["wd%d" % i] = np.ascontiguousarray(inp["ffn_w_down"][l, i].reshape(44, 128, 2048))
    uq = inp["mla_w_uq"][l]
    qc = []
    for h in range(4):
        qc.append(np.arange(h * 192, h * 192 + 128))
    for pr in range(2):
        rc = np.concatenate([np.arange((2 * pr) * 192 + 128, (2 * pr) * 192 + 192), np.arange((2 * pr + 1) * 192 + 128, (2 * pr + 1) * 192 + 192)])
        qc.append(rc)
        qc.append(_swap(rc, 32))
    qc = np.concatenate(qc)
    d["wuq"] = np.ascontiguousarray(uq[:, qc].reshape(4, 128, 8, 128).transpose(2, 1, 0, 3))
    ukv = inp["mla_w_ukv"][l]
    kc = np.concatenate([np.arange(h * 256, h * 256 + 128) for h in range(4)])
    vc = np.concatenate([np.arange(h * 256 + 128, h * 256 + 256) for h in range(4)])
    d["wuk"] = np.ascontiguousarray(ukv[:, kc].reshape(4, 128, 4, 128).transpose(2, 1, 0, 3))
    d["wuv"] = np.ascontiguousarray(ukv[:, vc].reshape(4, 128, 512).transpose(1, 0, 2))
    d["gq"] = np.ascontiguousarray(inp["mla_g_cq"][l].reshape(4, 128).T)
    d["gkv"] = np.ascontiguousarray(inp["mla_g_ckv"][l].reshape(4, 128).T)
    d["lng"] = np.ascontiguousarray(np.broadcast_to(inp["ln_g"][l][None], (128, 4, 2048)))
    d["lnb"] = np.ascontiguousarray(np.broadcast_to(inp["ln_b"][l][None], (128, 4, 2048)))
    return d


def rope_consts():
    i = np.arange(128)
    c = np.zeros((128, 8), np.float32)
    c[:, 0] = 10000.0 ** (-(2.0 * (i % 64)) / 128.0)
    c[:, 1] = 10000.0 ** (-(2.0 * (i % 32)) / 64.0)
    c[:, 2] = np.where(i % 128 < 64, 1.0, -1.0)
    c[:, 3] = np.where(i % 64 < 32, 1.0, -1.0)
    return c


def ident_np():
    import ml_dtypes
    return np.eye(128, dtype=np.float32).astype(ml_dtypes.bfloat16)


def emit_layernorm(k, Xj, g_v, b_v, sm, junk):
    k.reduce(sm[:, 0:1], Xj.v, ALU.add)
    k.act(junk.v, Xj.v, AF.Square, accum_out=sm[:, 1:2])
    k.ts(sm[:, 2:3], sm[:, 0:1], 1.0 / 2048, ALU.mult)
    k.tt(sm[:, 3:4], sm[:, 2:3], sm[:, 2:3], ALU.mult)
    k.stt(sm[:, 4:5], sm[:, 1:2], 1.0 / 2048, sm[:, 3:4], ALU.mult, ALU.subtract)
    k.ts(sm[:, 4:5], sm[:, 4:5], 1e-5, ALU.add)
    k.act(sm[:, 5:6], sm[:, 4:5], AF.Sqrt)
    k.recip(sm[:, 6:7], sm[:, 5:6])
    k.ts(Xj.v, Xj.v, sm[:, 2:3], ALU.subtract, sm[:, 6:7], ALU.mult)
    k.tt(Xj.v, Xj.v, g_v, ALU.mult)
    k.tt(Xj.v, Xj.v, b_v, ALU.add, eng="pool")


def emit_transposeX(k, X, XT, xb, ptr, ident):
    for j in range(NJ):
        k.copy(xb.v, X[j].v, eng="act")
        for g in range(4):
            p = ptr[g % 2]
            for q in range(4):
                kc = g * 4 + q
                k.tr(p[:, q * 128:(q + 1) * 128], xb[:, kc * 128:(kc + 1) * 128], ident.v)
            k.copy(XT[:, g * 4:(g + 1) * 4, j * 128:(j + 1) * 128], p[:, 0:512].re("p (q n) -> p q n", q=4),
                   eng="dve" if g % 2 == 0 else "act")


def emit_ffn(k, X, XT, wgu, wd, R):
    HT, WG, WU, WD, PG, PU, PD, SG = R["HT"], R["WG"], R["WU"], R["WD"], R["PG"], R["PU"], R["PD"], R["SG"]
    nd = 0
    for fb in range(4):
        for fc in range(11):
            f = fb * 11 + fc
            wg, wu = WG[f % 2], WU[f % 2]
            for mi, (mat, dstT) in enumerate(((f, wg), (44 + f, wu))):
                for hk in range(2):
                    st_ = R["STG"][2 * mi + hk]
                    k.dma("sp", st_.v, wgu[mat][:, hk * 8:(hk + 1) * 8, :])
                    k.copy(dstT[:, hk * 8:(hk + 1) * 8, :], st_.v, eng="act")
            for half in range(2):
                ts_ = slice(half * 512, (half + 1) * 512)
                for kc in range(16):
                    k.mm(PG[half].v, wg[:, kc, :], XT[:, kc, ts_], start=(kc == 0), stop=(kc == 15))
                for kc in range(16):
                    k.mm(PU[half].v, wu[:, kc, :], XT[:, kc, ts_], start=(kc == 0), stop=(kc == 15))
                k.act(SG[half].v, PG[half].v, AF.Silu)
                k.tt(HT[:, fc, ts_], SG[half].v, PU[half].v, ALU.mult)
        for n in range(4):
            w = WD[nd % 2]
            nd += 1
            k.dma("pool", w.v, wd[fb * 11:(fb + 1) * 11, :, n * 512:(n + 1) * 512].rearrange("f p n -> p f n"))
            for j in range(NJ):
                pd = PD[j % 2]
                for fc in range(11):
                    k.mm(pd.v, HT[:, fc, j * 128:(j + 1) * 128], w[:, fc, :], start=(fc == 0), stop=(fc == 10))
                xs = X[j][:, n * 512:(n + 1) * 512]
                k.stt(xs, pd.v, 0.5, xs, ALU.mult, ALU.add)


def alloc_common(k):
    R = {}
    R["XT"] = k.sb([128, 16, TOK], BF16, "XT")
    R["xb"] = k.sb([128, 2048], BF16, "xb")
    R["junk"] = R["xb"]
    R["ident"] = k.sb([128, 128], BF16, "ident")
    R["sm"] = k.sb([128, 8], F32, "sm")
    R["ptr"] = [k.ps([128, 1024], BF16, "ptr%d" % i) for i in range(2)]
    R["P"] = [k.ps([128, 512], F32, "P%d" % i) for i in range(6)]
    return R


def alloc_ffn(k, R):
    R["HT"] = k.sb([128, 11, TOK], BF16, "HT")
    R["WG"] = [k.sb([128, 16, 128], BF16, "WG%d" % i) for i in range(2)]
    R["WU"] = [k.sb([128, 16, 128], BF16, "WU%d" % i) for i in range(2)]
    R["STG"] = [k.sb([128, 8, 128], F32, "STG%d" % i) for i in range(4)]
    R["WD"] = [k.sb([128, 11, 512], BF16, "WD%d" % i) for i in range(2)]
    _sg = k.sb([128, 512], BF16, "SG")
    R["SG"] = [_sg, _sg]
    P = R["P"]
    R["PG"], R["PU"], R["PD"] = P[0:2], P[2:4], P[4:6]


def build_A(UPTO=9):
    nc = bass.Bass("TRN2", target_bir_lowering=False)
    E, nw = fm_entries()
    dt = lambda name, shape, dtype, kind="ExternalInput": nc.dram_tensor(name, list(shape), dtype, kind=kind).ap()
    x_in = dt("x_in", [TOK, 2048], F32)
    pos_in = dt("pos", [128, TOK], I32)
    rc_in = dt("ropec", [128, 8], F32)
    id_in = dt("ident", [128, 128], BF16)
    wgu = dt("wgu0", [88, 128, 16, 128], F32)
    wd = dt("wd0", [44, 128, 2048], F32)
    wfm = dt("wfm", [nw, 128, 16, 128], F32)
    wtm1 = dt("wtm1", [128, 16, 1024], F32)
    wtm2 = dt("wtm2", [128, 16, 284], F32)
    wuq = dt("wuq", [8, 128, 4, 128], F32)
    wuk = dt("wuk", [4, 128, 4, 128], F32)
    wuv = dt("wuv", [128, 4, 512], F32)
    gq_in = dt("gq", [128, 4], F32)
    gkv_in = dt("gkv", [128, 4], F32)
    lng_in = dt("lng", [128, 4, 2048], F32)
    lnb_in = dt("lnb", [128, 4, 2048], F32)
    x1_out = dt("x1", [TOK, 2048], F32, "ExternalOutput")
    fm_out = dt("fm", [NFM, 128, TOK], BF16, "ExternalOutput")
    tm_out = dt("tm", [TOK, NTM], BF16, "ExternalOutput")
    tmf_out = dt("tmf", [TOK, 32], F32, "ExternalOutput")

    with ExitStack() as st:
        k = K(nc, st)
        R = alloc_common(k)
        XT, P = R["XT"], R["P"]
        k.dma("sp", R["ident"].v, id_in)
        with k.scope():
            X = R["X"] = [k.sb([128, 2048], F32, "X%d" % j) for j in range(NJ)]
            for j in range(NJ):
                k.dma("sp", X[j].v, x_in[j * 128:(j + 1) * 128, :])
            emit_transposeX(k, X, XT, R["xb"], R["ptr"], R["ident"])
            for j in range(NJ):
                k.ts(X[j].v, X[j].v, ALPHA, ALU.mult, eng="pool")
            if UPTO >= 2:
              with k.scope():
                alloc_ffn(k, R)
                emit_ffn(k, X, XT, wgu, wd, R)
            R["lng"] = k.sb([128, 2048], F32, "lng")
            R["lnb"] = k.sb([128, 2048], F32, "lnb")
            k.dma("sp", R["lng"].v, lng_in[:, 0, :])
            k.dma("sp", R["lnb"].v, lnb_in[:, 0, :])
            for j in range(NJ):
                if UPTO >= 2:
                    emit_layernorm(k, X[j], R["lng"].v, R["lnb"].v, R["sm"], R["junk"])
                k.dma("sp", x1_out[j * 128:(j + 1) * 128, :], X[j].v, is_output=True)
            emit_transposeX(k, X, XT, R["xb"], R["ptr"], R["ident"])

        if UPTO < 3:
            k.emit()
            return nc
        rc = k.sb([128, 8], F32, "rc")
        k.dma("sp", rc.v, rc_in)
        CT, ST = {}, {}
        for kind in (128, 64):
            CT[kind] = k.sb([128, TOK], F32, "C%d" % kind)
            ST[kind] = k.sb([128, TOK], F32, "S%d" % kind)
        ropescope = k.scope()
        ropescope.__enter__()
        posi = k.sb([128, TOK], I32, "posi")
        k.dma("sp", posi.v, pos_in)
        posf = k.sb([128, TOK], F32, "posf")
        k.copy(posf.v, posi.v)
        ang = k.sb([128, TOK], F32, "ang")
        kf = k.sb([128, TOK], F32, "kf")
        r = k.sb([128, TOK], F32, "r")
        for kind, ci in ((128, 0), (64, 1)):
            for which in ("sin", "cos"):
                k.ts(ang.v, posf.v, rc[:, ci:ci + 1], ALU.mult)
                if which == "cos":
                    k.ts(ang.v, ang.v, math.pi / 2, ALU.add)
                k.ts(kf.v, ang.v, 1.0 / TWO_PI, ALU.mult)
                k.copy(posi.v, kf.v)
                k.copy(kf.v, posi.v)
                k.stt(r.v, kf.v, -TWO_PI, ang.v, ALU.mult, ALU.add)
                k.ts(kf.v, r.v, math.pi, ALU.is_gt, -TWO_PI, ALU.mult)
                k.tt(r.v, r.v, kf.v, ALU.add)
                k.ts(kf.v, r.v, -math.pi, ALU.is_lt, TWO_PI, ALU.mult)
                k.tt(r.v, r.v, kf.v, ALU.add)
                if which == "sin":
                    k.act(ST[kind].v, r.v, AF.Sin)
                    k.ts(ST[kind].v, ST[kind].v, rc[:, 2 + ci:3 + ci], ALU.mult, -1.0, ALU.mult)
                else:
                    k.act(CT[kind].v, r.v, AF.Sin)

        ropescope.__exit__(None, None, None)
        if UPTO < 3.5:
            dbg = k.sb([128, TOK], BF16, "dbg")
            for i_, t_ in enumerate((CT[128], ST[128], CT[64], ST[64])):
                k.copy(dbg.v, t_.v)
                k.dma("sp", fm_out[i_], dbg.v, is_output=True)
            k.emit()
            return nc
        fmscope = k.scope()
        fmscope.__enter__()
        W1 = [k.sb([128, 16, 128], BF16, "W1_%d" % i) for i in range(2)]
        W2 = [k.sb([128, 16, 128], BF16, "W2_%d" % i) for i in range(2)]
        DST = [k.sb([128, TOK], BF16, "DST%d" % i) for i in range(4)]
        T1 = [k.sb([128, 512], F32, "T1_%d" % i) for i in range(2)]
        T2 = [k.sb([128, 512], F32, "T2_%d" % i) for i in range(2)]
        cqg = k.sb([128, 4, TOK], BF16, "cqg")
        ckvg = k.sb([128, 4, TOK], BF16, "ckvg")
        sqq = k.sb([128, 4, TOK], BF16, "sqq")
        sqkv = k.sb([128, 4, TOK], BF16, "sqkv")
        gq = k.sb([128, 4], F32, "gq")
        gkv = k.sb([128, 4], F32, "gkv")
        k.dma("sp", gq.v, gq_in)
        k.dma("sp", gkv.v, gkv_in)
        nd = 0
        PY, PS = P[0:2], P[2:4]
        import os
        if os.environ.get("ENT"):
            E = [E[int(i_)] for i_ in os.environ["ENT"].split(",")]
        for ei, e in enumerate(E):
            w1 = W1[ei % 2]
            k.dma("pool", w1.v, wfm[e["w1"]])
            if e["rope"]:
                w2 = W2[ei % 2]
                k.dma("pool", w2.v, wfm[e["w2"]])
            dst = None
            raw = None
            if isinstance(e["dst"], int):
                dst = DST[nd % 4]
                nd += 1
                if "raw" in e:
                    raw = DST[nd % 4]
                    nd += 1
            for half in range(2):
                ts_ = slice(half * 512, (half + 1) * 512)
                for kc in range(16):
                    k.mm(PY[half].v, w1[:, kc, :], XT[:, kc, ts_], start=(kc == 0), stop=(kc == 15))
                if e["rope"]:
                    for kc in range(16):
                        k.mm(PS[half].v, w2[:, kc, :], XT[:, kc, ts_], start=(kc == 0), stop=(kc == 15))
                    kind = e["rope"]
                    k.tt(T1[half].v, PY[half].v, CT[kind][:, ts_], ALU.mult)
                    k.tt(T2[half].v, PS[half].v, ST[kind][:, ts_], ALU.mult)
                    k.tt(dst[:, ts_], T1[half].v, T2[half].v, ALU.add)
                    if raw is not None:
                        k.copy(raw[:, ts_], PY[half].v, eng="act")
                elif dst is not None:
                    k.copy(dst[:, ts_], PY[half].v, eng="act")
                else:
                    which, fc = e["dst"]
                    cg, sq, g = (cqg, sqq, gq) if which == "cq" else (ckvg, sqkv, gkv)
                    k.ts(cg[:, fc, ts_], PY[half].v, g[:, fc:fc + 1], ALU.mult)
                    k.act(sq[:, fc, ts_], PY[half].v, AF.Square)
            if dst is not None:
                k.dma("sp", fm_out[e["dst"]], dst.v, is_output=True)
            if raw is not None:
                k.dma("sp", fm_out[e["raw"]], raw.v, is_output=True)

        if UPTO < 4:
            fmscope.__exit__(None, None, None)
            return nc
        ones = k.sb([128, 128], BF16, "ones")
        k.memset(ones.v, 1.0)
        rstdB = {}
        rstdC = {}
        for which, sq in (("q", sqq), ("kv", sqkv)):
            rb = k.sb([128, TOK], F32, "rstdB" + which)
            rcl = k.sb([128, NJ], F32, "rstdC" + which)
            for half in range(2):
                ts_ = slice(half * 512, (half + 1) * 512)
                for fc in range(4):
                    k.mm(PY[half].v, ones.v, sq[:, fc, ts_], start=(fc == 0), stop=(fc == 3))
                k.ts(T1[half].v, PY[half].v, 1.0 / 512, ALU.mult, 1e-6, ALU.add)
                k.act(T1[half].v, T1[half].v, AF.Sqrt)
                k.recip(rb[:, ts_], T1[half].v)
            for j in range(NJ):
                for fc in range(4):
                    k.mm(PS[0][:, j:j + 1], sq[:, fc, j * 128:(j + 1) * 128], ones[:, 0:1], start=(fc == 0), stop=(fc == 3))
            k.ts(T2[0][:, 0:NJ], PS[0][:, 0:NJ], 1.0 / 512, ALU.mult, 1e-6, ALU.add)
            k.act(T2[0][:, 0:NJ], T2[0][:, 0:NJ], AF.Sqrt)
            k.recip(rcl.v, T2[0][:, 0:NJ])
            rstdB[which], rstdC[which] = rb, rcl
        WQ = [k.sb([128, 4, 128], BF16, "WQ%d" % i) for i in range(4)]
        jobs = [("n", c, None, FM_QBN + c) for c in range(4)] + [("r", 4, 5, FM_QBR), ("r", 6, 7, FM_QBR + 1)]
        wi = 0
        for kind_, c1, c2, di in jobs:
            w1 = WQ[wi % 4]; wi += 1
            k.dma("pool", w1.v, wuq[c1])
            if c2 is not None:
                w2 = WQ[wi % 4]; wi += 1
                k.dma("pool", w2.v, wuq[c2])
            dst = DST[nd % 4]; nd += 1
            for half in range(2):
                ts_ = slice(half * 512, (half + 1) * 512)
                for fc in range(4):
                    k.mm(PY[half].v, w1[:, fc, :], cqg[:, fc, ts_], start=(fc == 0), stop=(fc == 3))
                if c2 is None:
                    k.tt(dst[:, ts_], PY[half].v, rstdB["q"][:, ts_], ALU.mult)
                else:
                    for fc in range(4):
                        k.mm(PS[half].v, w2[:, fc, :], cqg[:, fc, ts_], start=(fc == 0), stop=(fc == 3))
                    k.tt(T1[half].v, PY[half].v, CT[64][:, ts_], ALU.mult)
                    k.tt(T2[half].v, PS[half].v, ST[64][:, ts_], ALU.mult)
                    k.tt(T1[half].v, T1[half].v, T2[half].v, ALU.add)
                    k.tt(dst[:, ts_], T1[half].v, rstdB["q"][:, ts_], ALU.mult)
            k.dma("sp", fm_out[di], dst.v, is_output=True)
        for c in range(4):
            w1 = WQ[wi % 4]; wi += 1
            k.dma("pool", w1.v, wuk[c])
            dst = DST[nd % 4]; nd += 1
            for half in range(2):
                ts_ = slice(half * 512, (half + 1) * 512)
                for fc in range(4):
                    k.mm(PY[half].v, w1[:, fc, :], ckvg[:, fc, ts_], start=(fc == 0), stop=(fc == 3))
                k.tt(dst[:, ts_], PY[half].v, rstdB["kv"][:, ts_], ALU.mult)
            k.dma("sp", fm_out[FM_KBN + c], dst.v, is_output=True)
        WV = k.sb([128, 4, 512], BF16, "WV")
        k.dma("pool", WV.v, wuv)
        TMO = [k.sb([128, 512], BF16, "TMO%d" % i) for i in range(2)]
        nt = 0
        for j in range(NJ):
            p = P[4 + j % 2]
            for fc in range(4):
                k.mm(p.v, ckvg[:, fc, j * 128:(j + 1) * 128], WV[:, fc, :], start=(fc == 0), stop=(fc == 3))
            o = TMO[nt % 2]; nt += 1
            k.ts(o.v, p.v, rstdC["kv"][:, j:j + 1], ALU.mult)
            k.dma("sp", tm_out[j * 128:(j + 1) * 128, TM_VB:TM_VB + 512], o.v, is_output=True)

        fmscope.__exit__(None, None, None)
        TMO = [k.sb([128, 512], BF16, "TMOb%d" % i) for i in range(2)]
        WT = k.sb([128, 16, 1024], BF16, "WT")
        k.dma("pool", WT.v, wtm1)
        for j in range(NJ):
            for gi, c0 in ((0, TM_VA), (1, TM_VD)):
                p = P[4 + gi]
                for kc in range(16):
                    k.mm(p.v, XT[:, kc, j * 128:(j + 1) * 128], WT[:, kc, gi * 512:(gi + 1) * 512], start=(kc == 0), stop=(kc == 15))
                o = TMO[nt % 2]; nt += 1
                k.copy(o.v, p.v, eng="act")
                k.dma("sp", tm_out[j * 128:(j + 1) * 128, c0:c0 + 512], o.v, is_output=True)
        WT2 = k.sb([128, 16, 284], BF16, "WT2")
        k.dma("pool", WT2.v, wtm2)
        TF = [k.sb([128, 32], F32, "TF%d" % i) for i in range(2)]
        for j in range(NJ):
            p = P[4 + j % 2]
            for kc in range(16):
                k.mm(p[:, 0:284], XT[:, kc, j * 128:(j + 1) * 128], WT2[:, kc, :], start=(kc == 0), stop=(kc == 15))
            o = TMO[nt % 2]; nt += 1
            k.copy(o[:, 0:256], p[:, 0:256], eng="act")
            k.dma("sp", tm_out[j * 128:(j + 1) * 128, TM_VSLC:TM_VSLC + 256], o[:, 0:256], is_output=True)
            tf = TF[j % 2]
            k.memset(tf.v, 0.0)
            k.act(tf[:, 0:12], p[:, 256:268], AF.Sigmoid)
            k.copy(tf[:, 12:28], p[:, 268:284])
            k.dma("sp", tmf_out[j * 128:(j + 1) * 128, :], tf.v, is_output=True)
        k.emit()
    return nc

import ml_dtypes

NEG = -32768.0
IMM = -2.0e30
BIGN = -1.0e30
KG_KA, KG_KBN, KG_KBR, KG_KCMP, KG_VCMP, KG_KSLC, KG_KWIN, KG_KD, KG_IK, NKG = 0, 4, 8, 9, 10, 11, 12, 13, 17, 18
QF_QA, QF_QBN, QF_QBR, QF_QC, QF_QCR, QF_QD, QF_IQ, NQF = 0, 4, 8, 10, 14, 18, 22, 30
BF = ml_dtypes.bfloat16


def core_masks(c):
    d = {}
    kk = np.arange(128)[:, None]
    qq = np.arange(128)[None, :]
    tri = np.where(kk <= qq, 0.0, NEG).astype(np.float32)
    anti = np.where(kk > qq, 0.0, NEG).astype(np.float32)
    full = np.zeros((128, 128), np.float32)
    none = np.full((128, 128), NEG, np.float32)
    cT = np.stack([full if m < c else (tri if m == c else none) for m in range(8)], 1)
    d["causalT4"] = np.ascontiguousarray(np.tile(cT[:, :, None, :], (1, 1, 4, 1)).reshape(128, 8, 512)).astype(BF)
    w = []
    for mp in range(12):
        rel = mp - 4 - c
        w.append(none if (rel < -4 or rel > 0) else (anti if rel == -4 else (tri if rel == 0 else full)))
    wT = np.stack(w, 1)
    d["winT4"] = np.ascontiguousarray(np.tile(wT[:, :, None, :], (1, 1, 4, 1)).reshape(128, 12, 512)).astype(BF)
    triq = np.where(kk.T >= qq.T * 0 + np.arange(128)[None, :], 0.0, BIGN)
    qi = np.arange(128)[:, None]
    ki = np.arange(128)[None, :]
    triq = np.where(ki <= qi, 0.0, BIGN).astype(np.float32)
    cq = np.stack([np.zeros((128, 128), np.float32) if m < c else (triq if m == c else np.full((128, 128), BIGN, np.float32)) for m in range(8)], 1)
    d["causalQ"] = np.ascontiguousarray(cq).astype(np.float32)
    cm = np.zeros((128, 8, 4, 128), np.float32)
    for j in range(8):
        gc = 8 * j + c
        for nch in range(4):
            ng = nch * 128 + np.arange(128)[:, None]
            t = 128 * gc + np.arange(128)[None, :]
            ok = (16 * ng + 31 <= t) & (ng <= 510)
            cm[:, j, nch, :] = np.where(ok, 0.0, NEG)
    d["cmpmask4"] = np.ascontiguousarray(np.tile(cm[:, :, :, None, :], (1, 1, 1, 4, 1)).reshape(128, 8, 4, 512)).astype(BF)
    gm = np.zeros((128, 8, 4, 32), np.float32)
    own = np.zeros((128, 8, 4, 32), np.float32)
    F = np.zeros((128, 8, 128), np.float32)
    for j in range(8):
        gc = 8 * j + c
        cur = gc // 2
        n = np.arange(32)
        gm[:, j, :, :] = np.where(n < cur, 0.0, BIGN)[None, None, :]
        own[:, j, :, :] = np.where(n >= cur, 1.0, 0.0)[None, None, :]
        curq = 2 * gc + (np.arange(128) >= 64).astype(np.int64)
        b = np.arange(128)[None, :]
        cq_ = curq[:, None]
        forced = ((b == 0) | (b == cq_) | (b == cq_ - 1)) & (b <= cq_)
        F[:, j, :] = np.where(b > cq_, -1e4, np.where(forced, 1e4, 0.0))
    d["gm4"], d["own4"], d["nsaF"] = gm, own, F
    return d


def shared_consts():
    d = {}
    E = np.zeros((32, 32, 128), np.float32)
    for n in range(32):
        E[n, n, :] = 1.0
    d["Emoba"] = E.astype(BF)
    A = np.zeros((128, 4, 128), np.float32)
    for nch in range(4):
        for n in range(128):
            ng = nch * 128 + n
            if ng > 510:
                continue
            for b in range(128):
                if 4 * b - 1 <= ng <= 4 * b + 3:
                    A[n, nch, b] = 1.0
    d["Aimp"] = A.astype(BF)
    d["ident"] = ident_np()
    d["ident4"] = np.ascontiguousarray(np.tile(np.eye(128, dtype=np.float32), (1, 4))).astype(BF)
    return d


def host_prep_B(l, inp):
    d = {}
    d["wout"] = np.ascontiguousarray(inp["w_out"][l].reshape(16, 128, 2048).transpose(1, 0, 2))
    wq = inp["mem_wq"][l]
    d["wqm"] = np.ascontiguousarray(wq.reshape(16, 128, 4, 128).transpose(2, 1, 0, 3))
    wkv = inp["mem_wkv"][l]
    d["wkm"] = np.ascontiguousarray(wkv[:, 0:512].reshape(16, 128, 4, 128).transpose(2, 1, 0, 3))
    d["wvm"] = np.ascontiguousarray(wkv[:, 512:1024].reshape(16, 128, 512).transpose(1, 0, 2))
    d["wom"] = np.ascontiguousarray(inp["mem_wo"][l].reshape(4, 128, 2048).transpose(1, 0, 2))
    gu = inp["ffn_w_gu"][l, 1]
    d["wgu1"] = np.ascontiguousarray(gu.reshape(16, 128, 88, 128).transpose(2, 1, 0, 3))
    d["wd1"] = np.ascontiguousarray(inp["ffn_w_down"][l, 1].reshape(44, 128, 2048))
    d["lng"] = np.ascontiguousarray(np.broadcast_to(inp["ln_g"][l][None], (128, 4, 2048)))
    d["lnb"] = np.ascontiguousarray(np.broadcast_to(inp["ln_b"][l][None], (128, 4, 2048)))
    d["cpe"] = np.ascontiguousarray(inp["nsa_cmp_pe"][l].transpose(0, 2, 1))
    d["cw1"] = np.ascontiguousarray(inp["nsa_cmp_w1"][l].reshape(2, 32, 128, 128).transpose(0, 2, 1, 3))
    d["cw2"] = np.ascontiguousarray(inp["nsa_cmp_w2"][l])
    d["mem"] = np.ascontiguousarray(inp["mem"][0])
    return d


def assemble_global(resA):
    fm = np.stack([r["fm"] for r in resA], 0)
    tm = np.stack([r["tm"] for r in resA], 0)
    kidx = list(range(FM_KA, FM_KA + 4)) + list(range(FM_KBN, FM_KBN + 4)) + [FM_KBR, FM_KCMP, FM_VCMP, FM_KSLC, FM_KWIN] + list(range(FM_KD, FM_KD + 4)) + [FM_IK]
    kg = fm[:, kidx].reshape(8, NKG, 128, 8, 128).transpose(1, 2, 3, 0, 4).reshape(NKG, 128, 8192)
    kg = np.ascontiguousarray(kg)
    tg = tm.reshape(8, 8, 128, NTM).transpose(1, 0, 2, 3).reshape(8192, NTM)
    one = np.ones((8192, 1), BF)

    def aug4(c0):
        v = tg[:, c0:c0 + 512].reshape(8192, 4, 128)
        return np.ascontiguousarray(np.concatenate([v, np.ones((8192, 4, 1), BF)], 2).reshape(8192, 516))

    def aug1(c0):
        return np.ascontiguousarray(np.concatenate([tg[:, c0:c0 + 128], one], 1))
    vs = dict(vA=aug4(TM_VA), vB=aug4(TM_VB), vD=aug4(TM_VD), vslc=aug1(TM_VSLC), vwin=aug1(TM_VWIN))
    qidx = list(range(FM_QA, FM_QA + 4)) + list(range(FM_QBN, FM_QBN + 4)) + [FM_QBR, FM_QBR + 1] + list(range(FM_QC, FM_QC + 4)) + \
        list(range(FM_QCR, FM_QCR + 4)) + list(range(FM_QD, FM_QD + 4)) + list(range(FM_IQ, FM_IQ + 8))
    qfs = [np.ascontiguousarray(fm[c][qidx]) for c in range(8)]
    return qfs, kg, vs


def emit_attn(k, R, nch, G, parts_fn, bias_fn, v_fn, W, scale):
    S, O, PT = R["S"], R["O"], R["PT"]
    for kc_ in range(nch):
        s = S[kc_ % 2]
        biases = bias_fn(kc_)
        first = True
        for (l, r) in biases:
            k.mm(s[:, 0:G * 128], l, r, start=first, stop=False)
            first = False
        for h in range(G):
            parts = parts_fn(kc_, h)
            for pi, (l, r) in enumerate(parts):
                k.mm(s[:, h * 128:(h + 1) * 128], l, r, start=(pi == 0 and not biases), stop=(pi == len(parts) - 1))
        pt = PT[kc_ % 3]
        k.act(pt[:, 0:G * 128], s[:, 0:G * 128], AF.Exp, scale=scale)
        for h in range(G):
            k.mm(O[h][:, 0:W], pt[:, h * 128:(h + 1) * 128], v_fn(kc_, h), start=(kc_ == 0), stop=(kc_ == nch - 1))


def emit_norm(k, R, G, out_fn, gate_fn=None, accumulate=False, first=False):
    O, rs = R["O"], R["rs"]
    for h in range(G):
        k.ts(rs[:, h:h + 1], O[h][:, 128:129], 1e-30, ALU.max)
        k.recip(rs[:, h:h + 1], rs[:, h:h + 1])


def load_v(k, tile, src, W):
    sv = src.rearrange("(ch p) w -> p ch w", p=128)
    for q in range(4):
        k.dma("sp", tile[:, q * 16:(q + 1) * 16, :], sv[:, q * 16:(q + 1) * 16, :])


def build_B(UPTO=9):
    nc = bass.Bass("TRN2", target_bir_lowering=False)
    dt = lambda name, shape, dtype, kind="ExternalInput": nc.dram_tensor(name, list(shape), dtype, kind=kind).ap()
    x1_in = dt("x1", [TOK, 2048], F32)
    qf = dt("qf", [NQF, 128, TOK], BF16)
    kg = dt("kg", [NKG, 128, 8192], BF16)
    vA_in, vB_in, vD_in = dt("vA", [8192, 516], BF16), dt("vB", [8192, 516], BF16), dt("vD", [8192, 516], BF16)
    vslc_in, vwin_in = dt("vslc", [8192, 129], BF16), dt("vwin", [8192, 129], BF16)
    tmf_in = dt("tmf", [TOK, 32], F32)
    mem_in = dt("mem", [256, 2048], F32)
    wout = dt("wout", [128, 16, 2048], F32)
    wqm, wkm = dt("wqm", [4, 128, 16, 128], F32), dt("wkm", [4, 128, 16, 128], F32)
    wvm, wom = dt("wvm", [128, 16, 512], F32), dt("wom", [128, 4, 2048], F32)
    wgu, wd = dt("wgu1", [88, 128, 16, 128], F32), dt("wd1", [44, 128, 2048], F32)
    lng_in, lnb_in = dt("lng", [128, 4, 2048], F32), dt("lnb", [128, 4, 2048], F32)
    cpe, cw1, cw2 = dt("cpe", [2, 128, 32], F32), dt("cw1", [2, 128, 32, 128], F32), dt("cw2", [2, 128, 128], F32)
    causalT4_in, winT4_in = dt("causalT4", [128, 8, 512], BF16), dt("winT4", [128, 12, 512], BF16)
    causalQ_in = dt("causalQ", [128, 8, 128], F32)
    cmpmask4_in = dt("cmpmask4", [128, 8, 4, 512], BF16)
    gm4_in, own4_in = dt("gm4", [128, 8, 4, 32], F32), dt("own4", [128, 8, 4, 32], F32)
    nsaF_in = dt("nsaF", [128, 8, 128], F32)
    Emoba_in, Aimp_in = dt("Emoba", [32, 32, 128], BF16), dt("Aimp", [128, 4, 128], BF16)
    id_in, id4_in = dt("ident", [128, 128], BF16), dt("ident4", [128, 512], BF16)
    x4_out = dt("x4", [TOK, 2048], F32, "ExternalOutput")
    ocat_out = dt("ocat", [TOK, 2048], BF16, "ExternalOutput")
    mqs = nc.dram_tensor("mqs", [8, 128, 8192], BF16, kind="Internal").ap()

    with ExitStack() as st:
        k = K(nc, st)
        R = {}
        R["ident"] = ident = k.sb([128, 128], BF16, "ident")
        P = R["P"] = [k.ps([128, 512], F32, "P%d" % i) for i in range(7)]
        ptr = k.ps([128, 1024], BF16, "ptr")
        attn_scope = k.scope()
        attn_scope.__enter__()
        ident4 = k.sb([128, 512], BF16, "ident4")
        causalT4 = k.sb([128, 8, 512], BF16, "causalT4")
        G_t = k.sb([128, 8, 32], F32, "gates")
        Ocat = k.sb([128, 8, 2048], BF16, "Ocat")
        R["rs"] = rs = k.sb([128, 16], F32, "rs")
        R["PT"] = [k.sb([128, 512], BF16, "PT%d" % i) for i in range(3)]
        R["ptr"] = [ptr, ptr]
        R["S"], R["O"] = P[0:2], P[2:6]
        PX = P[6]
        O = R["O"]
        Ocat_attn, rs_attn, G_attn = Ocat, rs, G_t
        k.dma("sp", ident.v, id_in)
        k.dma("sp", ident4.v, id4_in)
        k.dma("sp", causalT4.v, causalT4_in)
        k.dma("sp", G_t.v, tmf_in.rearrange("(j p) c -> p j c", p=128))
        js = lambda j: slice(j * 128, (j + 1) * 128)
        cs = lambda c: slice(c * 128, (c + 1) * 128)

        def causal_bias(j, kc_):
            return [(ident.v, causalT4[:, kc_ - 8 * j, :])] if kc_ >= 8 * j else []

        def finish(G, j, col0, gate_col=None, mode="set", dst=None, Ocat=None, rs=None, G_t=None):
            Ocat = Ocat if Ocat is not None else Ocat_attn
            rs = rs if rs is not None else rs_attn
            G_t = G_t if G_t is not None else G_attn
            for h in range(G):
                k.ts(rs[:, h:h + 1], O[h][:, 128:129], 1e-30, ALU.max)
                k.recip(rs[:, h:h + 1], rs[:, h:h + 1])
                if gate_col is not None:
                    k.tt(rs[:, h:h + 1], rs[:, h:h + 1], G_t[:, j, 3 * h + gate_col:3 * h + gate_col + 1], ALU.mult)
                if dst is None:
                    k.ts(Ocat[:, j, col0 + h * 128:col0 + (h + 1) * 128], O[h][:, 0:128], rs[:, h:h + 1], ALU.mult)
                elif mode == "set":
                    k.ts(dst[:, h * 128:(h + 1) * 128], O[h][:, 0:128], rs[:, h:h + 1], ALU.mult)
                else:
                    k.stt(dst[:, h * 128:(h + 1) * 128], O[h][:, 0:128], rs[:, h:h + 1], dst[:, h * 128:(h + 1) * 128], ALU.mult, ALU.add)

        if UPTO >= 1:
          with k.scope():
            KT = [k.sb([128, 8192], BF16, "aKT%d" % h) for h in range(4)]
            VA = k.sb([128, 64, 516], BF16, "aV")
            Em = k.sb([32, 32, 128], BF16, "Em")
            Q = [k.sb([128, TOK], BF16, "aQ%d" % h) for h in range(4)]
            gm4 = k.sb([128, 8, 4, 32], F32, "gm4")
            own4 = k.sb([128, 8, 4, 32], F32, "own4")
            kms = k.sb([128, 4, 32], F32, "kms")
            kmb = k.sb([128, 4, 32], BF16, "kmb")
            gate = k.sb([128, 4, 32], F32, "gate")
            sel = k.sb([128, 4, 32], F32, "sel")
            selb = k.sb([128, 4, 32], BF16, "selb")
            m8 = k.sb([128, 32], F32, "m8")
            BT = k.sb([32, 512], BF16, "BT")
            for h in range(4):
                k.dma("sp", KT[h].v, kg[KG_KA + h])
                k.dma("sp", Q[h].v, qf[QF_QA + h])
            load_v(k, VA, vA_in, 516)
            k.dma("sp", Em.v, Emoba_in)
            k.dma("sp", gm4.v, gm4_in)
            k.dma("sp", own4.v, own4_in)
            for h in range(4):
                k.reduce(kms[:, h, :], KT[h].v.re("p (n b) -> p n b", b=256), ALU.add)
            k.copy(kmb.v, kms.v)
            for j in range(8):
                for h in range(4):
                    k.mm(PX[:, h * 32:(h + 1) * 32], Q[h][:, js(j)], kmb[:, h, :])
                k.tt(gate.v, PX[:, 0:128].re("p (h n) -> p h n", h=4), gm4[:, j, :, :], ALU.add)
                for h in range(4):
                    k.max8(m8[:, h * 8:(h + 1) * 8], gate[:, h, :])
                for h in range(4):
                    k.ts(sel[:, h, :], gate[:, h, :], m8[:, h * 8 + 2:h * 8 + 3], ALU.is_ge)
                k.tt(sel.v, sel.v, own4[:, j, :, :], ALU.max)
                k.ts(selb.v, sel.v, 1.0, ALU.subtract, 32768.0, ALU.mult)
                for h in range(4):
                    k.tr(ptr[0:32, h * 128:(h + 1) * 128], selb[:, h, :], ident.v)
                k.copy(BT.v, ptr[0:32, 0:512])
                emit_attn(k, R, 8 * j + 8, 4,
                          lambda kc_, h: [(KT[h][:, cs(kc_)], Q[h][:, js(j)])],
                          lambda kc_: [(Em[:, kc_ // 2, :], BT.v)] + causal_bias(j, kc_),
                          lambda kc_, h: VA[:, kc_, h * 129:(h + 1) * 129], 129, 128 ** -0.5)
                finish(4, j, 0)
        if UPTO >= 2:
          with k.scope():
            KT = [k.sb([128, 8192], BF16, "bKT%d" % h) for h in range(4)]
            KR = k.sb([128, 8192], BF16, "bKR")
            VB = k.sb([128, 64, 516], BF16, "bV")
            Qn = [k.sb([128, TOK], BF16, "bQn%d" % h) for h in range(4)]
            Qr = [k.sb([128, TOK], BF16, "bQr%d" % h) for h in range(2)]
            for h in range(4):
                k.dma("sp", KT[h].v, kg[KG_KBN + h])
                k.dma("sp", Qn[h].v, qf[QF_QBN + h])
            for h in range(2):
                k.dma("sp", Qr[h].v, qf[QF_QBR + h])
            k.dma("sp", KR.v, kg[KG_KBR])
            load_v(k, VB, vB_in, 516)
            for j in range(8):
                def parts(kc_, h, j=j):
                    hs = slice((h % 2) * 64, (h % 2) * 64 + 64)
                    return [(KT[h][:, cs(kc_)], Qn[h][:, js(j)]), (KR[hs, cs(kc_)], Qr[h // 2][hs, js(j)])]
                emit_attn(k, R, 8 * j + 8, 4, parts, lambda kc_: causal_bias(j, kc_),
                          lambda kc_, h: VB[:, kc_, h * 129:(h + 1) * 129], 129, 192 ** -0.5)
                finish(4, j, 512)
        if UPTO >= 3:
          with k.scope():
            KC = k.sb([128, 512], BF16, "KC")
            VC = k.sb([128, 4, 257], BF16, "VC")
            with k.scope():
                kcT = k.sb([128, 8192], BF16, "kcT")
                vcT = k.sb([128, 8192], BF16, "vcT")
                k.dma("sp", kcT.v, kg[KG_KCMP])
                k.dma("sp", vcT.v, kg[KG_VCMP])
                W1 = [k.sb([128, 32, 128], BF16, "cW1_%d" % i) for i in range(2)]
                W2 = [k.sb([128, 128], BF16, "cW2_%d" % i) for i in range(2)]
                PE_ = [k.sb([128, 32], BF16, "cpe%d" % i) for i in range(2)]
                Aimp = k.sb([128, 4, 128], BF16, "Aimp")
                k.dma("sp", Aimp.v, Aimp_in)
                pb = k.sb([128, 1], F32, "pb")
                xh = k.sb([128, 512], F32, "xh")
                uh = k.sb([128, 512], F32, "uh")
                hg = k.sb([128, 512], BF16, "hg")
                for i in range(2):
                    k.dma("pool", W1[i].v, cw1[i])
                    k.dma("pool", W2[i].v, cw2[i])
                    k.dma("pool", PE_[i].v, cpe[i])
                for i, src in ((0, kcT), (1, vcT)):
                    sv = src.v.re("p (n s) -> p n s", s=16)
                    H = P[0]
                    for jj in range(32):
                        k.mm(H[:, 0:511], W1[i][:, jj, :], sv[:, (jj // 16):(jj // 16) + 511, jj % 16], start=(jj == 0), stop=(jj == 31))
                    for jj in range(32):
                        k.mm(PX[:, 0:1], W1[i][:, jj, :], PE_[i][:, jj:jj + 1], start=(jj == 0), stop=(jj == 31))
                    k.copy(pb.v, PX[:, 0:1])
                    k.memset(xh.v, 0.0)
                    k.ts(xh[:, 0:511], H[:, 0:511], pb[:, 0:1], ALU.add)
                    k.tt(uh.v, xh.v, xh.v, ALU.mult)
                    k.ts(uh.v, uh.v, 0.044715, ALU.mult, 1.0, ALU.add)
                    k.tt(uh.v, uh.v, xh.v, ALU.mult)
                    k.act(uh.v, uh.v, AF.Sigmoid, scale=1.5957691216057308)
                    k.tt(hg.v, xh.v, uh.v, ALU.mult)
                    if i == 0:
                        k.mm(P[1].v, W2[0].v, hg.v)
                        k.copy(KC.v, P[1].v, eng="act")
                    else:
                        for nch_ in range(4):
                            k.mm(P[1][:, nch_ * 128:(nch_ + 1) * 128], hg[:, cs(nch_)], W2[1].v)
                        k.copy(VC[:, :, 0:128], P[1].v.re("p (c d) -> p c d", c=4), eng="act")
                        k.memset(VC[:, :, 128:129], 1.0)
                        k.copy(VC[:, :, 129:257], Aimp.v)
            KS = k.sb([128, 8192], BF16, "KS")
            KW = k.sb([128, 8192], BF16, "KW")
            VS = k.sb([128, 64, 129], BF16, "VS")
            VW = k.sb([128, 64, 129], BF16, "VW")
            QC = [k.sb([128, TOK], BF16, "QC%d" % h) for h in range(4)]
            QR = [k.sb([128, TOK], BF16, "QR%d" % h) for h in range(4)]
            cmpmask4 = k.sb([128, 8, 4, 512], BF16, "cmpmask4")
            winT4 = k.sb([128, 12, 512], BF16, "winT4")
            nsaF = k.sb([128, 8, 128], F32, "nsaF")
            Mq = k.sb([128, 8192], BF16, "Mq")
            imp = k.sb([128, 128], F32, "imp")
            wk = k.sb([128, 128], F32, "wk")
            selq = k.sb([128, 128], F32, "selq")
            selqb = k.sb([128, 128], BF16, "selqb")
            m8 = k.sb([128, 16], F32, "m8n")
            acc = k.sb([128, 512], F32, "acc")
            k.dma("sp", KS.v, kg[KG_KSLC])
            k.dma("sp", KW.v, kg[KG_KWIN])
            load_v(k, VS, vslc_in, 129)
            load_v(k, VW, vwin_in, 129)
            for h in range(4):
                k.dma("sp", QC[h].v, qf[QF_QC + h])
                k.dma("sp", QR[h].v, qf[QF_QCR + h])
            k.dma("sp", cmpmask4.v, cmpmask4_in)
            k.dma("sp", winT4.v, winT4_in)
            k.dma("sp", nsaF.v, nsaF_in)
            for j in range(8):
                emit_attn(k, R, 4, 4, lambda kc_, h: [(KC[:, cs(kc_)], QC[h][:, js(j)])],
                          lambda kc_: [(ident.v, cmpmask4[:, j, kc_, :])],
                          lambda kc_, h: VC[:, kc_, :], 257, 128 ** -0.5)
                finish(4, j, 0, gate_col=None, mode="set", dst=None) if False else None
                for h in range(4):
                    k.ts(rs[:, h:h + 1], O[h][:, 128:129], 1e-30, ALU.max)
                    k.recip(rs[:, h:h + 1], rs[:, h:h + 1])
                    if h == 0:
                        k.ts(imp.v, O[h][:, 129:257], rs[:, h:h + 1], ALU.mult)
                    else:
                        k.stt(imp.v, O[h][:, 129:257], rs[:, h:h + 1], imp.v, ALU.mult, ALU.add)
                    k.tt(rs[:, 8 + h:9 + h], rs[:, h:h + 1], G_t[:, j, 3 * h:3 * h + 1], ALU.mult)
                    k.ts(acc[:, cs(h)], O[h][:, 0:128], rs[:, 8 + h:9 + h], ALU.mult)
                k.tt(imp.v, imp.v, nsaF[:, j, :], ALU.add)
                k.max8(m8[:, 0:8], imp.v)
                k.mrep(wk.v, m8[:, 0:8], imp.v, -3.0e4)
                k.max8(m8[:, 8:16], wk.v)
                k.ts(selq.v, imp.v, m8[:, 15:16], ALU.is_ge)
                k.ts(selqb.v, selq.v, 1.0, ALU.subtract, 32768.0, ALU.mult)
                nb = 2 * (8 * j + 8)
                k.copy(Mq[:, 0:nb * 64].re("p (b s) -> p b s", s=64), selqb[:, 0:nb].re("p (b o) -> p b o", o=1).bc([128, nb, 64]))
                emit_attn(k, R, 8 * j + 8, 4, lambda kc_, h: [(KS[:, cs(kc_)], QR[h][:, js(j)])],
                          lambda kc_: [(Mq[:, cs(kc_)], ident4.v)] + causal_bias(j, kc_),
                          lambda kc_, h: VS[:, kc_, :], 129, 128 ** -0.5)
                finish(4, j, 0, gate_col=1, mode="add", dst=acc)
                mps = [mp for mp in range(12) if 8 * j - 4 + mp >= 0]
                emit_attn(k, R, len(mps), 4, lambda ii, h: [(KW[:, cs(8 * j - 4 + mps[ii])], QR[h][:, js(j)])],
                          lambda ii: [(ident.v, winT4[:, mps[ii], :])],
                          lambda ii, h: VW[:, 8 * j - 4 + mps[ii], :], 129, 128 ** -0.5)
                finish(4, j, 0, gate_col=2, mode="add", dst=acc)
                k.copy(Ocat[:, j, 1024:1536], acc.v, eng="act")
        if UPTO >= 4:
          MQS = [Tl(mqs[j], "mqs%d" % j) for j in range(8)]
          with k.scope():
            IK = k.sb([128, 8192], BF16, "IK")
            IQ = [k.sb([128, TOK], BF16, "IQ%d" % i) for i in range(8)]
            causalQ = k.sb([128, 8, 128], F32, "causalQ")
            SC = [k.sb([128, 8192], F32, "SC%d" % i) for i in range(2)]
            MqD = [k.sb([128, 8192], BF16, "MqD%d" % i) for i in range(2)]
            Rr = [k.sb([128, 512], F32, "Rr%d" % i) for i in range(3)]
            bs_ = k.sb([128, 8], F32, "bsd")
            cjunk = k.sb([128, 8192], BF16, "cjunk")
            k.dma("sp", IK.v, kg[KG_IK])
            for i in range(8):
                k.dma("sp", IQ[i].v, qf[QF_IQ + i])
            k.dma("sp", causalQ.v, causalQ_in)
            nr = 0
            for j in range(8):
                sc = SC[j % 2]
                Kc = (8 * j + 8) * 128
                for blk in range(Kc // 512):
                    bs = slice(blk * 512, (blk + 1) * 512)
                    for h in range(16):
                        hs = slice((h % 2) * 64, (h % 2) * 64 + 64)
                        s = R["S"][h % 2]
                        k.mm(s.v, IQ[h // 2][hs, js(j)], IK[hs, bs])
                        rr = Rr[nr % 3]
                        nr += 1
                        k.act(rr.v, s.v, AF.Relu)
                        wv = G_t[:, j, 12 + h:13 + h]
                        if h == 0:
                            k.ts(sc[:, bs], rr.v, wv, ALU.mult)
                        else:
                            k.stt(sc[:, bs], rr.v, wv, sc[:, bs], ALU.mult, ALU.add)
                k.tt(sc[:, (8 * j) * 128:Kc].re("p (m k) -> p m k", m=8), sc[:, (8 * j) * 128:Kc].re("p (m k) -> p m k", m=8), causalQ.v, ALU.add)
                NIT = 30
                W0 = 6.0e4
                k.memset(bs_[:, 0:1], -3.0e4)
                k.ts(bs_[:, 1:2], bs_[:, 0:1], W0 / 2, ALU.add)
                for it in range(NIT):
                    ci = W0 / (2 ** (it + 1))
                    k.ts(cjunk[:, 0:Kc], sc[:, 0:Kc], bs_[:, 1:2], ALU.is_ge, None, ALU.add, accum_out=bs_[:, 2:3])
                    k.ts(bs_[:, 3:4], bs_[:, 2:3], 256.0, ALU.is_ge, ci, ALU.mult)
                    k.tt(bs_[:, 0:1], bs_[:, 0:1], bs_[:, 3:4], ALU.add)
                    if it < NIT - 1:
                        k.ts(bs_[:, 1:2], bs_[:, 0:1], ci / 2, ALU.add)
                mq = MqD[j % 2]
                k.ts(mq[:, 0:Kc], sc[:, 0:Kc], bs_[:, 0:1], ALU.is_lt, NEG, ALU.mult)
                k.dma("sp", V(MQS[j], mqs[j][:, 0:Kc]), mq[:, 0:Kc])
          with k.scope():
            KT = [k.sb([128, 8192], BF16, "dKT%d" % h) for h in range(4)]
            VD = k.sb([128, 64, 516], BF16, "dV")
            Q = [k.sb([128, TOK], BF16, "dQ%d" % h) for h in range(4)]
            MqL = [k.sb([128, 8192], BF16, "MqL%d" % i) for i in range(1)]
            for h in range(4):
                k.dma("sp", KT[h].v, kg[KG_KD + h])
                k.dma("sp", Q[h].v, qf[QF_QD + h])
            load_v(k, VD, vD_in, 516)
            for j in range(8):
                Kc = (8 * j + 8) * 128
                mq = MqL[0]
                k.dma("sp", mq[:, 0:Kc], V(MQS[j], mqs[j][:, 0:Kc]))
                emit_attn(k, R, 8 * j + 8, 4, lambda kc_, h: [(KT[h][:, cs(kc_)], Q[h][:, js(j)])],
                          lambda kc_: [(mq[:, cs(kc_)], ident4.v)] + causal_bias(j, kc_),
                          lambda kc_, h: VD[:, kc_, h * 129:(h + 1) * 129], 129, 128 ** -0.5)
                finish(4, j, 1536)
        OCD = [Tl(ocat_out[js(j), :], "ocd%d" % j) for j in range(8)]
        for j in range(8):
            k.dma("sp", V(OCD[j], ocat_out[js(j), :]), Ocat[:, j, :], is_output=True)
        attn_scope.__exit__(None, None, None)
        if UPTO < 5:
            return nc
        R["XT"] = XT = k.sb([128, 16, TOK], BF16, "XT")
        R["xb"] = k.sb([128, 2048], BF16, "xb")
        R["junk"] = R["xb"]
        R["sm"] = k.sb([128, 8], F32, "sm")
        lng = k.sb([128, 2048], F32, "lng")
        lnb = k.sb([128, 2048], F32, "lnb")
        X = R["X"] = [k.sb([128, 2048], F32, "X%d" % j) for j in range(NJ)]
        PD = P[4:6]
        for j in range(NJ):
            k.dma("sp", X[j].v, x1_in[js(j), :])
            k.ts(X[j].v, X[j].v, ALPHA, ALU.mult, eng="pool")
        with k.scope():
            OCL = [k.sb([128, 2048], BF16, "OCL%d" % i) for i in range(2)]
            for j in range(NJ):
                ocl = OCL[j % 2]
                k.dma("sp", ocl.v, V(OCD[j], ocat_out[js(j), :]))
                for g in range(4):
                    for q in range(4):
                        kc = g * 4 + q
                        k.tr(ptr[:, q * 128:(q + 1) * 128], ocl[:, cs(kc)], ident.v)
                    k.copy(XT[:, g * 4:(g + 1) * 4, js(j)], ptr[:, 0:512].re("p (q n) -> p q n", q=4), eng="dve" if g % 2 == 0 else "act")
            WO = [k.sb([128, 16, 512], BF16, "WO%d" % i) for i in range(2)]
            for n in range(4):
                w = WO[n % 2]
                k.dma("pool", w.v, wout[:, :, n * 512:(n + 1) * 512])
                for j in range(NJ):
                    pd = PD[j % 2]
                    for kc in range(16):
                        k.mm(pd.v, XT[:, kc, js(j)], w[:, kc, :], start=(kc == 0), stop=(kc == 15))
                    xs = X[j][:, n * 512:(n + 1) * 512]
                    k.tt(xs, pd.v, xs, ALU.add)
            k.dma("sp", lng.v, lng_in[:, 1, :])
            k.dma("sp", lnb.v, lnb_in[:, 1, :])
            for j in range(NJ):
                emit_layernorm(k, X[j], lng.v, lnb.v, R["sm"], R["junk"])
        with k.scope():
            R["PT"] = [k.sb([128, 512], BF16, "PTm%d" % i) for i in range(3)]
            rs_m = k.sb([128, 16], F32, "rs_m")
            OM = k.sb([128, 1, 512], BF16, "OM")
            QM = [k.sb([128, TOK], BF16, "QM%d" % h) for h in range(4)]
            KM = [k.sb([128, 256], BF16, "KM%d" % h) for h in range(4)]
            VM = k.sb([128, 2, 516], BF16, "VM")
            projscope = k.scope()
            projscope.__enter__()
            MT = k.sb([128, 16, 256], BF16, "MT")
            mf = k.sb([128, 2048], F32, "mf")
            for mc in range(2):
                k.dma("sp", mf.v, mem_in[mc * 128:(mc + 1) * 128, :])
                k.copy(R["xb"].v, mf.v, eng="act")
                for g in range(4):
                    for q in range(4):
                        kc = g * 4 + q
                        k.tr(ptr[:, q * 128:(q + 1) * 128], R["xb"][:, cs(kc)], ident.v)
                    k.copy(MT[:, g * 4:(g + 1) * 4, mc * 128:(mc + 1) * 128], ptr[:, 0:512].re("p (q n) -> p q n", q=4))
            emit_transposeX(k, X, XT, R["xb"], R["ptr"], ident)
            for j in range(NJ):
                k.ts(X[j].v, X[j].v, ALPHA, ALU.mult, eng="pool")
            Wm = [k.sb([128, 16, 128], BF16, "Wm%d" % i) for i in range(2)]
            Wv = k.sb([128, 16, 512], BF16, "Wvm")
            k.dma("pool", Wv.v, wvm)
            nw = 0
            for h in range(4):
                w = Wm[nw % 2]; nw += 1
                k.dma("pool", w.v, wqm[h])
                for half in range(2):
                    ts_ = slice(half * 512, (half + 1) * 512)
                    for kc in range(16):
                        k.mm(P[half].v, w[:, kc, :], XT[:, kc, ts_], start=(kc == 0), stop=(kc == 15))
                    k.copy(QM[h][:, ts_], P[half].v, eng="act")
                w = Wm[nw % 2]; nw += 1
                k.dma("pool", w.v, wkm[h])
                for kc in range(16):
                    k.mm(PX[:, 0:256], w[:, kc, :], MT[:, kc, :], start=(kc == 0), stop=(kc == 15))
                k.copy(KM[h].v, PX[:, 0:256])
            k.memset(VM.v, 1.0)
            for mc in range(2):
                for kc in range(16):
                    k.mm(PX.v, MT[:, kc, mc * 128:(mc + 1) * 128], Wv[:, kc, :], start=(kc == 0), stop=(kc == 15))
                k.copy(VM[:, mc, :].re("p (h d) -> p h d", h=4)[:, :, 0:128], PX.v.re("p (h d) -> p h d", h=4))
            projscope.__exit__(None, None, None)
            Wo = k.sb([128, 4, 2048], BF16, "Wom")
            k.dma("pool", Wo.v, wom)
            OMT = k.sb([128, 4, TOK], BF16, "OMT")
            for j in range(NJ):
                emit_attn(k, R, 2, 4, lambda kc_, h: [(KM[h][:, cs(kc_)], QM[h][:, js(j)])], lambda kc_: [],
                          lambda kc_, h: VM[:, kc_, h * 129:(h + 1) * 129], 129, 128 ** -0.5)
                finish(4, 0, 0, Ocat=OM, rs=rs_m)
                for q in range(4):
                    k.tr(ptr[:, q * 128:(q + 1) * 128], OM[:, 0, cs(q)], ident.v)
                k.copy(OMT[:, :, js(j)], ptr[:, 0:512].re("p (q n) -> p q n", q=4))
            for n in range(4):
                for j in range(NJ):
                    pd = PD[j % 2]
                    for kc in range(4):
                        k.mm(pd.v, OMT[:, kc, js(j)], Wo[:, kc, n * 512:(n + 1) * 512], start=(kc == 0), stop=(kc == 3))
                    xs = X[j][:, n * 512:(n + 1) * 512]
                    k.tt(xs, pd.v, xs, ALU.add)
            k.dma("sp", lng.v, lng_in[:, 2, :])
            k.dma("sp", lnb.v, lnb_in[:, 2, :])
            for j in range(NJ):
                emit_layernorm(k, X[j], lng.v, lnb.v, R["sm"], R["junk"])
        emit_transposeX(k, X, XT, R["xb"], R["ptr"], ident)
        for j in range(NJ):
            k.ts(X[j].v, X[j].v, ALPHA, ALU.mult, eng="pool")
        with k.scope():
            alloc_ffn(k, R)
            emit_ffn(k, X, XT, wgu, wd, R)
        k.dma("sp", lng.v, lng_in[:, 3, :])
        k.dma("sp", lnb.v, lnb_in[:, 3, :])
        for j in range(NJ):
            emit_layernorm(k, X[j], lng.v, lnb.v, R["sm"], R["junk"])
            k.dma("sp", x4_out[js(j), :], X[j].v, is_output=True)
        k.emit()
    return nc


_PROGS = {}


def _prog(name):
    if name not in _PROGS:
        _PROGS[name] = build_A() if name == "A" else build_B()
    return _PROGS[name]


def kernel(**inputs):
    inp = {k_: np.asarray(v_) for k_, v_ in inputs.items()}
    x = inp["x"][0]
    pos = inp["positions"][0].astype(np.int32)
    rows = [np.concatenate([np.arange((8 * j + c) * 128, (8 * j + c + 1) * 128) for j in range(8)]) for c in range(8)]
    xs = [np.ascontiguousarray(x[rows[c]]) for c in range(8)]
    posb = [np.ascontiguousarray(np.broadcast_to(pos[rows[c]][None], (128, 1024))).astype(np.int32) for c in range(8)]
    rc, idn = rope_consts(), ident_np()
    sc = shared_consts()
    cms = [core_masks(c) for c in range(8)]
    for l in range(4):
        dA = host_prep_A(l, inp)
        wA = {k_: dA[k_] for k_ in ("wgu0", "wd0", "wfm", "wtm1", "wtm2", "wuq", "wuk", "wuv", "gq", "gkv", "lng", "lnb")}
        mapsA = [dict(x_in=xs[c], pos=posb[c], ropec=rc, ident=idn, **wA) for c in range(8)]
        resA = _bu.run_bass_kernel_spmd(_prog("A"), mapsA, core_ids=list(range(8))).results
        del mapsA, wA
        qfs, kg, vs = assemble_global(resA)
        dB = host_prep_B(l, inp)
        del dA
        mapsB = [dict(x1=resA[c]["x1"], qf=qfs[c], kg=kg, tmf=resA[c]["tmf"], **vs, **dB, **sc, **cms[c]) for c in range(8)]
        resB = _bu.run_bass_kernel_spmd(_prog("B"), mapsB, core_ids=list(range(8))).results
        del mapsB, dB
        xs = [np.ascontiguousarray(resB[c]["x4"]) for c in range(8)]
    out = np.zeros((8192, 2048), np.float32)
    for c in range(8):
        out[rows[c]] = xs[c]
    return out[None]
```
